# Optimizing a Trainium2 kernel written in Bass

```python
import jax, jax.numpy as jnp
from jax import lax
import numpy as np

D_MODEL = 1024
BATCH = 16
SEQ = 2048
DEPTH = 2

GRID_W = 64
CTX_LEN = 256
N_HEADS = 8
N_KV_HEADS = 2
HEAD_DIM = 64
ATTN_WIDTH = N_HEADS * HEAD_DIM
KV_WIDTH = N_KV_HEADS * HEAD_DIM
Q_BLOCK = 128
ROPE_THETA = 10000.0
POOL_WINDOWS = (2, 4, 8, 16)
POOL_GROUPS = 4
POOL_GROUP_DIM = 64
POOL_WIDTH = POOL_GROUPS * POOL_GROUP_DIM
SGU_GROUPS = 4
SGU_GROUP_DIM = 64
SGU_WIDTH = SGU_GROUPS * SGU_GROUP_DIM
CHUNK = 128
N_BRANCHES = 3
D_FF = 2816
EPS = 1e-6
N_MOD = 9
OFF_Q = 0
OFF_K = OFF_Q + ATTN_WIDTH
OFF_V = OFF_K + KV_WIDTH
OFF_POOL = OFF_V + KV_WIDTH
OFF_SGU = OFF_POOL + POOL_WIDTH
OFF_GATE = OFF_SGU + 2 * SGU_WIDTH
IN_COLS = OFF_GATE + N_BRANCHES * D_MODEL

kernel_name = 'hybrid_pool_gqa_sgu_macaron_dit'


def rms_norm(x, g):
    xf = x.astype(jnp.float32)
    y = xf * lax.rsqrt(jnp.mean(xf * xf, axis=-1, keepdims=True) + EPS)
    return (y * g.astype(jnp.float32)).astype(x.dtype)


def modulate(x, g, shift, scale):
    return rms_norm(x, g) * (1 + scale) + shift


def adaln(cond, w, b):
    mod = jax.nn.silu(cond) @ w + b
    return jnp.split(mod[..., None, :], N_MOD, axis=-1)


def swiglu(x, w13, w2):
    a, b = jnp.split(x @ w13, 2, axis=-1)
    return (jax.nn.silu(a) * b) @ w2


def split_cols(p):
    return (p[..., OFF_Q:OFF_K], p[..., OFF_K:OFF_V], p[..., OFF_V:OFF_POOL],
            p[..., OFF_POOL:OFF_SGU], p[..., OFF_SGU:OFF_GATE], p[..., OFF_GATE:IN_COLS])


def axial_rope_tables(L):
    rows = L // GRID_W
    row = jnp.broadcast_to(jnp.arange(rows, dtype=jnp.int32)[:, None], (rows, GRID_W)).reshape(-1)
    col = jnp.broadcast_to(jnp.arange(GRID_W, dtype=jnp.int32)[None, :], (rows, GRID_W)).reshape(-1)
    half = HEAD_DIM // 2
    inv_freq = ROPE_THETA ** (-jnp.arange(0, half, 2, dtype=jnp.float32) / half)
    ang = jnp.stack([row.astype(jnp.float32)[:, None] * inv_freq[None, :],
                     col.astype(jnp.float32)[:, None] * inv_freq[None, :]])
    return jnp.cos(ang), jnp.sin(ang)


def rope_1d(x, cos, sin):
    cos = cos.astype(x.dtype)[None, :, None, :]
    sin = sin.astype(x.dtype)[None, :, None, :]
    x1, x2 = jnp.split(x, 2, axis=-1)
    return jnp.concatenate([x1 * cos - x2 * sin, x2 * cos + x1 * sin], axis=-1)


def apply_axial_rope(x, cos, sin):
    xr, xc = jnp.split(x, 2, axis=-1)
    return jnp.concatenate([rope_1d(xr, cos[0], sin[0]), rope_1d(xc, cos[1], sin[1])], axis=-1)


def gqa_softmax(q, k, v):
    s = jnp.einsum('bqkgd,bskd->bkgqs', q, k).astype(jnp.float32) * (HEAD_DIM ** -0.5)
    p = jax.nn.softmax(s, axis=-1).astype(v.dtype)
    return jnp.einsum('bkgqs,bskd->bqkgd', p, v)


def latent_attention(q, k_lat, v_lat, k_ctx, v_ctx):
    B, L = q.shape[0], q.shape[1]
    G = N_HEADS // N_KV_HEADS
    k_all = jnp.concatenate([k_lat, k_ctx], axis=1)
    v_all = jnp.concatenate([v_lat, v_ctx], axis=1)
    nblk = L // Q_BLOCK
    qb = q.reshape(B, nblk, Q_BLOCK, N_KV_HEADS, G, HEAD_DIM).transpose(1, 0, 2, 3, 4, 5)
    out = lax.map(lambda qblk: gqa_softmax(qblk, k_all, v_all), qb)
    return out.transpose(1, 0, 2, 3, 4, 5).reshape(B, L, ATTN_WIDTH)


def context_attention(q, k, v):
    B, C = q.shape[0], q.shape[1]
    G = N_HEADS // N_KV_HEADS
    out = gqa_softmax(q.reshape(B, C, N_KV_HEADS, G, HEAD_DIM), k, v)
    return out.reshape(B, C, ATTN_WIDTH)


def pool_mix(x, w, scale):
    B, L = x.shape[0], x.shape[1]
    xg = x.reshape(B, L, POOL_GROUPS, POOL_GROUP_DIM)
    xf = xg.astype(jnp.float32)
    cs = jnp.concatenate([jnp.zeros((B, 1, POOL_GROUPS, POOL_GROUP_DIM), jnp.float32),
                          jnp.cumsum(xf, axis=1)], axis=1)
    t = jnp.arange(L, dtype=jnp.int32)
    means = []
    for gi, win in enumerate(POOL_WINDOWS):
        lo = jnp.clip(t - win // 2, 0, L)
        hi = jnp.clip(t + win // 2, 0, L)
        cnt = (hi - lo).astype(jnp.float32)[None, :, None]
        means.append((cs[:, hi, gi] - cs[:, lo, gi]) / cnt)
    pooled = (jnp.stack(means, axis=2) - xf).astype(x.dtype)
    y = jnp.einsum('blgc,gcd->blgd', pooled, w).reshape(B, L, POOL_WIDTH)
    return y * scale


def sgu_mix(uv, norm_g, w_s, b_s):
    B, L = uv.shape[0], uv.shape[1]
    u, v = jnp.split(jax.nn.gelu(uv), 2, axis=-1)
    v = rms_norm(v, norm_g)
    vc = v.reshape(B, L // CHUNK, CHUNK, SGU_GROUPS, SGU_GROUP_DIM)
    s = jnp.einsum('gts,bnsgc->bntgc', w_s, vc) + b_s.T[None, None, :, :, None]
    return u * s.reshape(B, L, SGU_WIDTH)


def merge_branches(pool_out, attn_out, sgu_out, gate_logits, w_br_pool, w_br_attn, w_br_sgu, w_out):
    g = jax.nn.sigmoid(gate_logits)
    merged = (g[..., :D_MODEL] * (pool_out @ w_br_pool)
              + g[..., D_MODEL:2 * D_MODEL] * (attn_out @ w_br_attn)
              + g[..., 2 * D_MODEL:] * (sgu_out @ w_br_sgu))
    return merged @ w_out


def setup_inputs(seed: int = 0) -> dict:
    key = jax.random.key(seed)
    ks = jax.random.split(key, 26)
    f32 = jnp.float32

    def nrm(k, shape, scale):
        return jax.random.normal(k, shape, f32) * scale

    def gain(k, shape):
        return 1.0 + 0.05 * jax.random.normal(k, shape, f32)

    D = D_MODEL
    return {
        'x': nrm(ks[0], (BATCH, SEQ, D), 1.0),
        'c': nrm(ks[1], (BATCH, D), 1.0),
        'ctx': nrm(ks[2], (BATCH, CTX_LEN, D), 1.0),
        'c_ctx': nrm(ks[3], (D,), 1.0),
        'w_ada': nrm(ks[4], (DEPTH, D, N_MOD * D), 0.5 * D ** -0.5),
        'b_ada': nrm(ks[5], (DEPTH, N_MOD * D), 0.02),
        'norm_ffn1': gain(ks[6], (DEPTH, D)),
        'ffn1_w13': nrm(ks[7], (DEPTH, D, 2 * D_FF), D ** -0.5),
        'ffn1_w2': nrm(ks[8], (DEPTH, D_FF, D), D_FF ** -0.5),
        'norm_mix': gain(ks[9], (DEPTH, D)),
        'w_in': nrm(ks[10], (DEPTH, D, IN_COLS), D ** -0.5),
        'q_norm': gain(ks[11], (DEPTH, HEAD_DIM)),
        'k_norm': gain(ks[12], (DEPTH, HEAD_DIM)),
        'pool_w': nrm(ks[13], (DEPTH, POOL_GROUPS, POOL_GROUP_DIM, POOL_GROUP_DIM), POOL_GROUP_DIM ** -0.5),
        'pool_scale': gain(ks[14], (DEPTH, POOL_WIDTH)),
        'sgu_norm': gain(ks[15], (DEPTH, SGU_WIDTH)),
        'sgu_w': nrm(ks[16], (DEPTH, SGU_GROUPS, CHUNK, CHUNK), CHUNK ** -0.5),
        'sgu_b': 1.0 + nrm(ks[17], (DEPTH, SGU_GROUPS, CHUNK), 0.02),
        'w_br_pool': nrm(ks[18], (DEPTH, POOL_WIDTH, D), POOL_WIDTH ** -0.5),
        'w_br_attn': nrm(ks[19], (DEPTH, ATTN_WIDTH, D), ATTN_WIDTH ** -0.5),
        'w_br_sgu': nrm(ks[20], (DEPTH, SGU_WIDTH, D), SGU_WIDTH ** -0.5),
        'w_out': nrm(ks[21], (DEPTH, D, D), D ** -0.5),
        'norm_ffn2': gain(ks[22], (DEPTH, D)),
        'ffn2_w13': nrm(ks[23], (DEPTH, D, 2 * D_FF), D ** -0.5),
        'ffn2_w2': nrm(ks[24], (DEPTH, D_FF, D), D_FF ** -0.5),
        'final_norm': gain(ks[25], (D,)),
    }


def reference(x, c, ctx, c_ctx, w_ada, b_ada, norm_ffn1, ffn1_w13, ffn1_w2, norm_mix, w_in,
              q_norm, k_norm, pool_w, pool_scale, sgu_norm, sgu_w, sgu_b, w_br_pool, w_br_attn,
              w_br_sgu, w_out, norm_ffn2, ffn2_w13, ffn2_w2, final_norm):
    B, L = x.shape[0], x.shape[1]
    C = ctx.shape[1]
    cos, sin = axial_rope_tables(L)
    h, hc = x, ctx
    for l in range(DEPTH):
        last = l == DEPTH - 1
        m = adaln(c, w_ada[l], b_ada[l])
        mc = adaln(c_ctx, w_ada[l], b_ada[l])

        h = h + 0.5 * m[2] * swiglu(modulate(h, norm_ffn1[l], m[0], m[1]), ffn1_w13[l], ffn1_w2[l])
        hc = hc + 0.5 * mc[2] * swiglu(modulate(hc, norm_ffn1[l], mc[0], mc[1]), ffn1_w13[l], ffn1_w2[l])

        hn = modulate(h, norm_mix[l], m[3], m[4])
        hnc = modulate(hc, norm_mix[l], mc[3], mc[4])
        q, k, v, pool_in, sgu_in, gate_logits = split_cols(hn @ w_in[l])
        q = apply_axial_rope(rms_norm(q.reshape(B, L, N_HEADS, HEAD_DIM), q_norm[l]), cos, sin)
        k = apply_axial_rope(rms_norm(k.reshape(B, L, N_KV_HEADS, HEAD_DIM), k_norm[l]), cos, sin)
        v = v.reshape(B, L, N_KV_HEADS, HEAD_DIM)
        if last:
            kv_c = hnc @ w_in[l][:, OFF_K:OFF_POOL]
            k_c, v_c = kv_c[..., :KV_WIDTH], kv_c[..., KV_WIDTH:]
        else:
            q_c, k_c, v_c, pool_c, sgu_c, gate_c = split_cols(hnc @ w_in[l])
        k_c = rms_norm(k_c.reshape(B, C, N_KV_HEADS, HEAD_DIM), k_norm[l])
        v_c = v_c.reshape(B, C, N_KV_HEADS, HEAD_DIM)

        attn = latent_attention(q, k, v, k_c, v_c)
        out = merge_branches(pool_mix(pool_in, pool_w[l], pool_scale[l]), attn,
                             sgu_mix(sgu_in, sgu_norm[l], sgu_w[l], sgu_b[l]), gate_logits,
                             w_br_pool[l], w_br_attn[l], w_br_sgu[l], w_out[l])
        h = h + m[5] * out

        if not last:
            q_c = rms_norm(q_c.reshape(B, C, N_HEADS, HEAD_DIM), q_norm[l])
            attn_c = context_attention(q_c, k_c, v_c)
            out_c = merge_branches(pool_mix(pool_c, pool_w[l], pool_scale[l]), attn_c,
                                   sgu_mix(sgu_c, sgu_norm[l], sgu_w[l], sgu_b[l]), gate_c,
                                   w_br_pool[l], w_br_attn[l], w_br_sgu[l], w_out[l])
            hc = hc + mc[5] * out_c
            hc = hc + 0.5 * mc[8] * swiglu(modulate(hc, norm_ffn2[l], mc[6], mc[7]), ffn2_w13[l], ffn2_w2[l])

        h = h + 0.5 * m[8] * swiglu(modulate(h, norm_ffn2[l], m[6], m[7]), ffn2_w13[l], ffn2_w2[l])
    return rms_norm(h, final_norm)
```

```python
import contextlib
import numpy as np
import concourse.bass as bass
import concourse.mybir as mybir
from concourse.bass_utils import run_bass_kernel_spmd

F32 = mybir.dt.float32
BF16 = mybir.dt.bfloat16
ALU = mybir.AluOpType
AF = mybir.ActivationFunctionType

D = 1024; L = 2048; C = 256; DFF = 2816; NFF = 22; KC = 8
NB = 2
NCORES = 8
EPS = 1e-6
NV = 212
VL = 100
V_N1, V_NM, V_N2, V_BADA, V_PSC, V_QN, V_KN = 0, 8, 16, 24, 96, 98, 99
V_FN = 200
NKS = 18
GELU_K = 1.5957691216057308


DEBUG = {}


class _Buf:
    __slots__ = ("w", "r")

    def __init__(self):
        self.w = None
        self.r = {}


class Sched:
    NDMA = 8

    def __init__(self, nc, es):
        self.nc = nc; self.es = es
        self.E = {"pe": nc.tensor, "act": nc.scalar, "dve": nc.vector, "pool": nc.gpsimd, "sp": nc.sync}
        self.sem = {}; self.cnt = {}; self.nsem = 0
        self.seen = {e: {} for e in self.E}
        self.allsems = {}
        for e in self.E:
            self._newsem(e)
        self.bufs = {}
        self.dq = {}; self.dqi = {}

    def _mk(self, name):
        self.nsem += 1
        s = self.es.enter_context(self.nc.semaphore(f"{name}_{self.nsem}"))
        return s

    def _newsem(self, e):
        self.sem[e] = self._mk("s" + e); self.cnt[e] = 0

    def buf(self, k):
        b = self.bufs.get(k)
        if b is None:
            b = self.bufs[k] = _Buf()
        return b

    def _wait(self, e, tok):
        sem, val = tok[0], tok[1]
        d = self.seen[e]
        if d.get(id(sem), 0) >= val:
            return
        self.E[e].wait_ge(sem, val)
        d[id(sem)] = val

    def _deps(self, e, reads, writes):
        for k in reads:
            b = self.buf(k)
            if b.w is not None and not (b.w[2] == e and e == "pe"):
                self._wait(e, b.w)
        for k in writes:
            b = self.buf(k)
            if b.w is not None and not (b.w[2] == e and e == "pe"):
                self._wait(e, b.w)
            for rk, t in b.r.items():
                if t[2] == e and e == "pe":
                    continue
                self._wait(e, t)

    def _commit(self, tok, reads, writes):
        rk = tok[2] if tok[2] != "dma" else id(tok[0])
        for k in reads:
            self.buf(k).r[rk] = tok
        for k in writes:
            b = self.buf(k); b.w = tok; b.r = {}

    def _tok(self, e, ins):
        if self.cnt[e] >= 4000:
            self._newsem(e)
        self.cnt[e] += 1
        ins.then_inc(self.sem[e], 1)
        tok = (self.sem[e], self.cnt[e], e)
        self.allsems[id(tok[0])] = tok
        return tok

    def op(self, e, fn, reads=(), writes=()):
        self._deps(e, reads, writes)
        ins = fn(self.E[e])
        self._commit(self._tok(e, ins), reads, writes)

    def mm(self, out, pairs, reads, writes, start=True, stop=True):
        self._deps("pe", reads, writes)
        n = len(pairs); ins = None
        for i, (lt, rh) in enumerate(pairs):
            ins = self.nc.tensor.matmul(out, lt, rh, start=(start and i == 0), stop=(stop and i == n - 1))
        self._commit(self._tok("pe", ins), reads, writes)

    def dma(self, q, out, in_, reads=(), writes=()):
        pool = self.dq.setdefault(q, [])
        i = self.dqi.get(q, 0); self.dqi[q] = i + 1
        if len(pool) < (16 if q == "pool" else self.NDMA):
            pool.append([self._mk("d" + q), 0])
        ent = pool[i % (16 if q == "pool" else self.NDMA)]
        if ent[1] > 0:
            self._wait(q, (ent[0], ent[1]))
        self._deps(q, reads, writes)
        ins = self.E[q].dma_start(out=out, in_=in_)
        ent[1] += 16
        ins.then_inc(ent[0], 16)
        tok = (ent[0], ent[1], "dma")
        self.allsems[id(ent[0])] = tok
        self._commit(tok, reads, writes)

    def barrier(self):
        skip = set(id(x[0]) for x in self.dq.get("pool", []))
        for e in self.E:
            for t in list(self.allsems.values()):
                if t[2] == e or id(t[0]) in skip:
                    continue
                self._wait(e, t)

    def drain(self, e):
        for t in list(self.allsems.values()):
            if t[2] == "dma":
                self._wait(e, t)


class Arena:
    def __init__(self, nc, es, nbytes):
        self.t = es.enter_context(nc.sbuf_tensor("arena", [128, nbytes // 2], BF16))
        self.cap = nbytes; self.off = 0

    def alloc(self, nbytes, dt=BF16, at=None, name=None):
        rb = (nbytes + 127) // 128 * 128
        if name:
            DEBUG[name] = (self.off if at is None else at, nbytes)
        if at is None:
            at = self.off; self.off += rb
        assert at + rb <= self.cap, (at, rb, self.cap)
        ap = self.t[:, at // 2:(at + nbytes) // 2]
        if dt == F32:
            ap = ap.bitcast(F32)
        return ap

    def mark(self):
        return self.off

    def reset(self, m):
        self.off = m


def _tiles(include_ctx=True):
    t = [("lat", 0, 512), ("lat", 512, 256), ("lat", 768, 512), ("lat", 1280, 256), ("lat", 1536, 512)]
    if include_ctx:
        t.append(("ctx", 0, 256))
    return t


def build(nb=NB, layers=(0, 1), stages=("ffn1", "mix", "ffn2"), final=True, p2_tiles=None):
    nc = bass.Bass("TRN2", target_bir_lowering=False)

    def din(name, shape, dt=F32):
        return nc.dram_tensor(name, list(shape), dt, kind="ExternalInput").ap()

    def dscr(name, shape, dt=BF16):
        return nc.dram_tensor(name, list(shape), dt, kind="Internal").ap()

    xT = din("xT", [nb, 128, KC, L]); ctxT = din("ctxT", [nb, 128, KC, C])
    cT = din("cT", [128, KC, 3]); vecs_d = din("vecs", [128, NV])
    w_ada = din("w_ada", [2, D, 9 * D])
    w13_d = [din("ffn1_w13", [2, D, 2 * DFF]), din("ffn2_w13", [2, D, 2 * DFF])]
    w2_d = [din("ffn1_w2", [2, DFF, D]), din("ffn2_w2", [2, DFF, D])]
    w_in = din("w_in", [2, D, 4608])
    w_brp = din("w_br_pool", [2, 256, D]); w_bra = din("w_br_attn", [2, 512, D]); w_brs = din("w_br_sgu", [2, 256, D])
    w_out = din("w_out", [2, D, D])
    poolbd_d = din("poolbd", [2, 128, 2, 128]); sguwT_d = din("sguwT", [2, 128, 4, 128])
    sgun_d = din("sgun", [2, 128, 256]); sgub_d = din("sgub", [2, 128, 2, 128])
    cos_d = din("ropecos", [128, L]); sin_d = din("ropesin", [128, L]); perm_d = din("ropeperm", [128, 128])
    band_d = din("bands", [128, 20, 128])
    outT = nc.dram_tensor("outT", [nb, 128, KC, L], F32, kind="ExternalOutput").ap()
    hcout = nc.dram_tensor("hcout", [nb, 128, KC, C], F32, kind="ExternalOutput").ap()

    w13s = [[dscr(f"w13s_{l}_{f}", [11, 128, KC, 512]) for f in range(2)] for l in range(2)]
    w2s = [[dscr(f"w2s_{l}_{f}", [4, 128, NFF, 256]) for f in range(2)] for l in range(2)]
    wins = [dscr(f"wins_{l}", [12, 128, KC, 512]) for l in range(2)]
    wouts = [dscr(f"wouts_{l}", [2, 128, KC, 512]) for l in range(2)]

    with contextlib.ExitStack() as es:
        S = Sched(nc, es)
        A = Arena(nc, es, 207 * 1024)
        PS = [es.enter_context(nc.psum_tensor(f"ps{i}", [128, 512], F32)) for i in range(8)]
        psi = [0]

        def ps_rot():
            i = psi[0] % 6; psi[0] += 1
            return PS[i], ("ps", i)

        hT = A.alloc(KC * L * 4, F32).rearrange("p (k n) -> p k n", k=KC)
        hcT = A.alloc(KC * C * 4, F32).rearrange("p (k n) -> p k n", k=KC)
        cosb = A.alloc(L * 2); sinb = A.alloc(L * 2)
        perm = A.alloc(128 * 4, F32)
        vecs = A.alloc(NV * 4, F32)
        ones = A.alloc(128 * 2); bdones = A.alloc(128 * 2)
        bands = A.alloc(20 * 128 * 2).rearrange("p (k n) -> p k n", k=20)
        poolbd = A.alloc(2 * 2 * 128 * 2).rearrange("p (l c n) -> p l c n", l=2, c=2)
        sguwT = A.alloc(2 * 4 * 128 * 2).rearrange("p (l g n) -> p l g n", l=2, g=4)
        sgun = A.alloc(2 * 256 * 4, F32).rearrange("p (l n) -> p l n", l=2)
        sgub = A.alloc(2 * 2 * 128 * 4, F32).rearrange("p (l c n) -> p l c n", l=2, c=2)
        cTs = A.alloc(KC * 3 * 4, F32).rearrange("p (k n) -> p k n", k=KC)
        scT = A.alloc(KC * 3 * 2).rearrange("p (k n) -> p k n", k=KC)
        modT = A.alloc(2 * 72 * 3 * 4, F32).rearrange("p (l j n) -> p l j n", l=2, j=72)
        Amod = A.alloc(2 * 3 * KC * 3 * 4, F32).rearrange("p (l s k n) -> p l s k n", l=2, s=3, k=KC)
        Gmod = A.alloc(2 * 3 * KC * 3 * 4, F32).rearrange("p (l s k n) -> p l s k n", l=2, s=3, k=KC)
        NWA = 2
        wA = [A.alloc(KC * 512 * 2).rearrange("p (k n) -> p k n", k=KC) for _ in range(NWA)]
        wai = [0]

        def wa_next():
            i = wai[0] % NWA; wai[0] += 1
            return wA[i], ("wA", i)

        sqb = [A.alloc(512 * 2) for _ in range(2)]
        t32 = [A.alloc(512 * 4, F32) for _ in range(4)]
        rrb = [A.alloc(512 * 4, F32) for _ in range(2)]
        tci = {"sq": 0, "t32": 0, "rr": 0}

        def tmp(kind):
            lst = {"sq": sqb, "t32": t32, "rr": rrb}[kind]
            i = tci[kind] % len(lst); tci[kind] += 1
            return lst[i], (kind, i)

        xn = A.alloc(KC * 768 * 2, name="xn").rearrange("p (k n) -> p k n", k=KC)
        xnb = [xn, None]
        phase0 = A.mark()
        xn2 = A.alloc(KC * 768 * 2).rearrange("p (k n) -> p k n", k=KC)
        xnb[1] = xn2
        u_ff = A.alloc(NFF * 768 * 2).rearrange("p (k n) -> p k n", k=NFF)
        wB = [A.alloc(NFF * 256 * 2).rearrange("p (k n) -> p k n", k=NFF) for _ in range(2)]
        ffn_end = A.mark()
        A.reset(phase0)
        kT = A.alloc(2 * NKS * 128 * 2, name="kT").rearrange("p (k n) -> p k n", k=2)
        Vaug = A.alloc(NKS * 2 * 128 * 2, name="Vaug").rearrange("p (s k n) -> p s k n", s=NKS, k=2)
        xpool = A.alloc(NKS * 256 * 2, name="xpool").rearrange("p (s n) -> p s n", s=NKS)
        uT = A.alloc(2 * 512 * 2, name="uT").rearrange("p (k n) -> p k n", k=2)
        qT = A.alloc(4 * 512 * 2, name="qT").rearrange("p (k n) -> p k n", k=4)
        attnT = A.alloc(4 * 512 * 2, name="attnT").rearrange("p (k n) -> p k n", k=4)
        pTb = [A.alloc(512 * 2) for _ in range(4)]
        pooled = A.alloc(512 * 2)
        poolout = A.alloc(2 * 512 * 2, name="poolout").rearrange("p (k n) -> p k n", k=2)
        gates = [A.alloc(512 * 2) for _ in range(6)]
        merged = A.alloc(KC * 512 * 2, name="merged").rearrange("p (k n) -> p k n", k=KC)
        kg32 = [A.alloc(512 * 4, F32) for _ in range(1)]
        vn = [A.alloc(256 * 2) for _ in range(2)]
        gv = [A.alloc(256 * 4, F32) for _ in range(2)]
        small = A.alloc(64 * 4, F32)
        mix_end = A.mark()
        A.reset(max(ffn_end, mix_end))
        print("SBUF bytes/partition: ffn_end", ffn_end, "mix_end", mix_end)

        cnt = {"pt": 0, "gate": 0, "kg": 0, "vn": 0, "sm": 0}

        S.dma("sp", vecs, vecs_d, writes=[("vecs",)])
        S.dma("sp", cTs, cT, writes=[("cTs",)])
        S.dma("sp", perm, perm_d, writes=[("perm",)])
        S.dma("sp", sgun, sgun_d.rearrange("l p n -> p l n"), writes=[("sgun",)])
        S.dma("sp", sgub, sgub_d.rearrange("l p c n -> p l c n"), writes=[("sgub",)])
        S.dma("pool", cosb, cos_d, writes=[("cos",)])
        S.dma("pool", sinb, sin_d, writes=[("sin",)])
        S.dma("pool", bands, band_d, writes=[("bands",)])
        S.dma("pool", poolbd, poolbd_d.rearrange("l p c n -> p l c n"), writes=[("poolbd",)])
        S.dma("pool", sguwT, sguwT_d.rearrange("l p g n -> p l g n"), writes=[("sguwT",)])
        S.op("dve", lambda e: e.memset(ones, 1.0), writes=[("ones",)])
        S.op("dve", lambda e: e.memset(bdones, 0.0), writes=[("bdones",)])
        S.op("dve", lambda e: e.memset(bdones[0:64, 0:64], 1.0), writes=[("bdones",)])
        S.op("dve", lambda e: e.memset(bdones[64:128, 64:128], 1.0), writes=[("bdones",)])

        def precast_ffn(l, f):
            src = w13_d[f][l].rearrange("(k p) n -> p k n", p=128)
            for g in range(11):
                S.dma("pool", w13s[l][f][g][:, :, 0:256], src[:, :, g * 256:(g + 1) * 256], writes=[("w13s", l, f, g)])
                S.dma("pool", w13s[l][f][g][:, :, 256:512], src[:, :, DFF + g * 256:DFF + (g + 1) * 256],
                      writes=[("w13s", l, f, g)])
            src2 = w2_d[f][l].rearrange("(k p) n -> p k n", p=128)
            for g in range(4):
                S.dma("pool", w2s[l][f][g], src2[:, :, g * 256:(g + 1) * 256], writes=[("w2s", l, f, g)])

        def precast_mix(l):
            src = w_in[l].rearrange("(k p) n -> p k n", p=128)
            W = wins[l]
            k0 = ("wins", l, 0)
            for kv in range(2):
                for dup in range(2):
                    for r in range(2):
                        d0 = kv * 128 + dup * 64 + r * 32
                        s0 = 512 + kv * 64 + r * 16
                        for a in range(2):
                            S.dma("pool", W[0][:, :, d0 + a * 16:d0 + a * 16 + 16], src[:, :, s0 + a * 32:s0 + a * 32 + 16], writes=[k0])
            S.dma("pool", W[1][:, :, 0:128], src[:, :, 640:768], writes=[("wins", l, 1)])
            S.dma("pool", W[1][:, :, 128:384], src[:, :, 768:1024], writes=[("wins", l, 1)])
            S.dma("pool", W[2], src[:, :, 1024:1536], writes=[("wins", l, 2)])
            for h in range(8):
                for r in range(2):
                    d0 = h * 64 + r * 32
                    s0 = h * 64 + r * 16
                    for a in range(2):
                        S.dma("pool", W[3][:, :, d0 + a * 16:d0 + a * 16 + 16], src[:, :, s0 + a * 32:s0 + a * 32 + 16], writes=[("wins", l, 3)])
            brp = w_brp[l].rearrange("(k p) n -> p k n", p=128)
            bra = w_bra[l].rearrange("(k p) n -> p k n", p=128)
            brs = w_brs[l].rearrange("(k p) n -> p k n", p=128)
            for m in range(8):
                key = ("wins", l, 4 + m)
                for gi in range(3):
                    S.dma("pool", W[4 + m][:, :, gi * 128:(gi + 1) * 128],
                          src[:, :, 1536 + gi * 1024 + m * 128: 1536 + gi * 1024 + (m + 1) * 128], writes=[key])
                S.dma("pool", W[4 + m][:, 0:2, 384:512], brp[:, :, m * 128:(m + 1) * 128], writes=[key])
                S.dma("pool", W[4 + m][:, 2:6, 384:512], bra[:, :, m * 128:(m + 1) * 128], writes=[key])
                S.dma("pool", W[4 + m][:, 6:8, 384:512], brs[:, :, m * 128:(m + 1) * 128], writes=[key])
            so = w_out[l].rearrange("(k p) n -> p k n", p=128)
            for og in range(2):
                S.dma("pool", wouts[l][og], so[:, :, og * 512:(og + 1) * 512], writes=[("wouts", l, og)])

        def adaln_piece(l, g):
            src = w_ada[l].rearrange("(k p) n -> p k n", p=128)
            wa, wk = wa_next()
            S.dma("pool", wa, src[:, :, g * 512:(g + 1) * 512], writes=[wk])
            pm, pmk = ps_rot()
            for j in range(4):
                S.mm(pm[:, j * 3:(j + 1) * 3], [(wa[:, kc, j * 128:(j + 1) * 128], scT[:, kc, :]) for kc in range(KC)],
                     reads=[wk, ("scT",)], writes=[pmk])
            bo = l * VL + V_BADA + g * 4
            S.op("dve", lambda e: e.tensor_tensor(out=modT[:, l, g * 4:(g + 1) * 4, :], in0=pm[:, 0:12].rearrange("p (j n) -> p j n", n=3),
                                                  in1=vecs[:, bo:bo + 4].unsqueeze(2).to_broadcast([128, 4, 3]), op=ALU.add),
                 reads=[pmk, ("vecs",)], writes=[("modT", l)])

        def adaln_finish(l):
            for s in range(3):
                no = l * VL + (V_N1, V_NM, V_N2)[s]
                S.op("dve", lambda e: e.scalar_tensor_tensor(
                    out=Amod[:, l, s], in0=modT[:, l, (3 * s + 1) * 8:(3 * s + 2) * 8, :], scalar=1.0,
                    in1=vecs[:, no:no + 8].unsqueeze(2).to_broadcast([128, 8, 3]), op0=ALU.add, op1=ALU.mult),
                    reads=[("modT", l), ("vecs",)], writes=[("Amod", l)])
                gs = 1.0 if s == 1 else 0.5
                S.op("dve", lambda e: e.tensor_scalar(out=Gmod[:, l, s], in0=modT[:, l, (3 * s + 2) * 8:(3 * s + 3) * 8, :],
                                                      scalar1=gs, scalar2=None, op0=ALU.mult),
                     reads=[("modT", l)], writes=[("Gmod", l)])

        bg = []
        aux = ["dve"]

        def run_bg(n=1):
            for _ in range(n):
                if bg:
                    bg.pop(0)()

        def hap(t, kc):
            st, t0, w = t
            return (hT if st == "lat" else hcT)[:, kc, t0:t0 + w]

        def hkey(t, kc):
            return ("h", t[0], t[1], kc)

        def rstd_from_ps(ps_ap, pskey, w, scale, npart=128):
            sr, srk = tmp("rr")
            S.op("act", lambda e: e.activation(out=sr[:, 0:w], in_=ps_ap, func=AF.Sqrt, bias=epsc[:, 0:1], scale=scale),
                 reads=[pskey, ("epsc",)], writes=[srk])
            S.op("dve", lambda e: e.reciprocal(out=sr[:, 0:w], in_=sr[:, 0:w]), reads=[srk], writes=[srk])
            return sr, srk

        def modulate(t, l, s, bcol, xoff, xb=0):
            st, t0, w = t
            xn = xnb[xb]
            col = 2 if st == "ctx" else bcol
            ps, pk = ps_rot()
            for kc in range(KC):
                sq, sk = tmp("sq")
                S.op("act", lambda e: e.activation(out=sq[:, 0:w], in_=hap(t, kc), func=AF.Square),
                     reads=[hkey(t, kc)], writes=[sk])
                S.mm(ps[:, 0:w], [(ones, sq[:, 0:w])], reads=[sk, ("ones",)], writes=[pk], start=(kc == 0), stop=(kc == KC - 1))
            rr, rk = rstd_from_ps(ps[:, 0:w], pk, w, 1.0 / D)
            for kc in range(KC):
                tt, tk = tmp("t32")
                S.op("dve", lambda e: e.scalar_tensor_tensor(out=tt[:, 0:w], in0=hap(t, kc), scalar=Amod[:, l, s, kc, col:col + 1],
                                                             in1=rr[:, 0:w], op0=ALU.mult, op1=ALU.mult),
                     reads=[hkey(t, kc), rk, ("Amod", l)], writes=[tk])
                S.op("act", lambda e: e.activation(out=xn[:, kc, xoff:xoff + w], in_=tt[:, 0:w], func=AF.Identity,
                                                   bias=modT[:, l, 3 * s * 8 + kc, col:col + 1], scale=1.0),
                     reads=[tk, ("modT", l)], writes=[("xn", xb, kc, xoff)])

        def xn_keys(xoff, xb=0):
            return [("xn", xb, kc, xoff) for kc in range(KC)]

        def ffn(l, f, bcol, tiles):
            s = 0 if f == 0 else 2
            sbs = [tiles[i:i + 2] for i in range(0, len(tiles), 2)]

            def offs_of(sb):
                offs = []; o = 0
                for t in sb:
                    offs.append(o); o += t[2]
                return offs

            def mod_sb(i):
                for t, xo in zip(sbs[i], offs_of(sbs[i])):
                    modulate(t, l, s, bcol, xo, xb=i % 2)

            mod_sb(0)
            for i, sb in enumerate(sbs):
                offs = offs_of(sb)
                xb = i % 2
                xn = xnb[xb]
                for g in range(11):
                    wa, wk = wa_next()
                    S.dma("sp", wa, w13s[l][f][g], reads=[("w13s", l, f, g)], writes=[wk])
                    for j in range(2):
                        n = 2 * g + j
                        for t, xo in zip(sb, offs):
                            w = t[2]
                            pa, pak = ps_rot(); pb, pbk = ps_rot()
                            S.mm(pa[:, 0:w], [(wa[:, kc, j * 128:(j + 1) * 128], xn[:, kc, xo:xo + w]) for kc in range(KC)],
                                 reads=[wk] + xn_keys(xo, xb), writes=[pak])
                            S.mm(pb[:, 0:w], [(wa[:, kc, 256 + j * 128:256 + (j + 1) * 128], xn[:, kc, xo:xo + w]) for kc in range(KC)],
                                 reads=[wk] + xn_keys(xo, xb), writes=[pbk])
                            sa, sak = tmp("t32")
                            S.op("act", lambda e: e.activation(out=sa[:, 0:w], in_=pa[:, 0:w], func=AF.Silu), reads=[pak], writes=[sak])
                            S.op("dve", lambda e: e.tensor_tensor(out=u_ff[:, n, xo:xo + w], in0=sa[:, 0:w], in1=pb[:, 0:w], op=ALU.mult),
                                 reads=[sak, pbk], writes=[("u", n, xo)])
                if i + 1 < len(sbs):
                    mod_sb(i + 1)
                for g in range(4):
                    wb = wB[g % 2]; wbk = ("wB", g % 2)
                    S.dma("sp", wb, w2s[l][f][g], reads=[("w2s", l, f, g)], writes=[wbk])
                    for j in range(2):
                        m = 2 * g + j
                        for t, xo in zip(sb, offs):
                            w = t[2]; col = 2 if t[0] == "ctx" else bcol
                            py, pyk = ps_rot()
                            S.mm(py[:, 0:w], [(wb[:, n, j * 128:(j + 1) * 128], u_ff[:, n, xo:xo + w]) for n in range(NFF)],
                                 reads=[wbk] + [("u", n, xo) for n in range(NFF)], writes=[pyk])
                            S.op("dve", lambda e: e.scalar_tensor_tensor(out=hap(t, m), in0=py[:, 0:w], scalar=Gmod[:, l, s, m, col:col + 1],
                                                                         in1=hap(t, m), op0=ALU.mult, op1=ALU.add),
                                 reads=[pyk, ("Gmod", l), hkey(t, m)], writes=[hkey(t, m)])

        def gelu_to(out_ap, ps_ap, pskey, outkey, npart, w):
            S.op("act", lambda e: e.activation(out=out_ap, in_=ps_ap, func=AF.Gelu_apprx_tanh), reads=[pskey], writes=[outkey])

        def qk_norm_rope(ps, pk, w, gain_col, out_ap, outkey, rope, tok0):
            sq, sk = tmp("sq")
            S.op("act", lambda e: e.activation(out=sq[:, 0:w], in_=ps[:, 0:w], func=AF.Square), reads=[pk], writes=[sk])
            p2, p2k = ps_rot()
            S.mm(p2[:, 0:w], [(bdones, sq[:, 0:w])], reads=[sk, ("bdones",)], writes=[p2k])
            rr, rk = rstd_from_ps(p2[:, 0:w], p2k, w, 1.0 / 64)
            if not rope:
                S.op("dve", lambda e: e.scalar_tensor_tensor(out=out_ap, in0=ps[:, 0:w], scalar=vecs[:, gain_col:gain_col + 1],
                                                             in1=rr[:, 0:w], op0=ALU.mult, op1=ALU.mult),
                     reads=[pk, rk, ("vecs",)], writes=[outkey])
                return
            i = 0
            kg = kg32[i]; kk = ("kg", i)
            S.op("dve", lambda e: e.scalar_tensor_tensor(out=kg[:, 0:w], in0=ps[:, 0:w], scalar=vecs[:, gain_col:gain_col + 1],
                                                         in1=rr[:, 0:w], op0=ALU.mult, op1=ALU.mult),
                 reads=[pk, rk, ("vecs",)], writes=[kk])
            w1, w1k = tmp("t32")
            S.op(aux[0], lambda e: e.tensor_tensor(out=w1[:, 0:w], in0=rr[:, 0:w], in1=sinb[:, tok0:tok0 + w], op=ALU.mult),
                 reads=[rk, ("sin",)], writes=[w1k])
            t1, t1k = tmp("t32")
            gsw = 208 + 2 * (gain_col // VL) + (1 if gain_col % VL == V_KN else 0)
            for qd in range(4):
                o = qd * 32; so = (qd ^ 1) * 32
                S.op("dve", lambda e: e.scalar_tensor_tensor(out=t1[o:o + 32, 0:w], in0=ps[so:so + 32, 0:w], scalar=vecs[o:o + 32, gsw:gsw + 1],
                                                             in1=w1[o:o + 32, 0:w], op0=ALU.mult, op1=ALU.mult),
                     reads=[pk, w1k, ("vecs",)], writes=[t1k])
            S.op(aux[0], lambda e: e.tensor_tensor(out=kg[:, 0:w], in0=kg[:, 0:w], in1=cosb[:, tok0:tok0 + w], op=ALU.mult),
                 reads=[kk, ("cos",)], writes=[kk])
            S.op(aux[0], lambda e: e.tensor_tensor(out=out_ap, in0=kg[:, 0:w], in1=t1[:, 0:w], op=ALU.add),
                 reads=[kk, t1k], writes=[outkey])

        def koff(t):
            return t[1] if t[0] == "lat" else L + t[1]

        def pass1(l, bcol, t):
            st, t0, w = t
            ko = koff(t)
            isctx = st == "ctx"
            modulate(t, l, 1, bcol, 0)
            wa, wk = wa_next()
            S.dma("sp", wa[:, :, 0:256], wins[l][0][:, :, 0:256], reads=[("wins", l, 0)], writes=[wk])
            kps = []
            for j in range(2):
                ps, pk = PS[6 + j], ("ps", 6 + j)
                S.mm(ps[:, 0:w], [(wa[:, kc, j * 128:(j + 1) * 128], xn[:, kc, 0:w]) for kc in range(KC)],
                     reads=[wk] + xn_keys(0), writes=[pk])
                kps.append((ps, pk))
            wa, wk = wa_next()
            S.dma("sp", wa[:, :, 0:384], wins[l][1][:, :, 0:384], reads=[("wins", l, 1)], writes=[wk])
            for si in range(w // 128):
                gs = (ko + si * 128) // 128
                ps, pk = ps_rot()
                S.mm(ps[:, 0:384], [(xn[:, kc, si * 128:(si + 1) * 128], wa[:, kc, 0:384]) for kc in range(KC)],
                     reads=[wk] + xn_keys(0), writes=[pk])
                S.op("act", lambda e: e.activation(out=Vaug[:, gs, :, 0:64], in_=ps[:, 0:128].rearrange("p (k n) -> p k n", k=2),
                                                   func=AF.Copy), reads=[pk], writes=[("V", gs)])
                S.op("act", lambda e: e.activation(out=xpool[:, gs, :], in_=ps[:, 128:384], func=AF.Copy), reads=[pk], writes=[("xpool", gs)])
            for j in range(2):
                qk_norm_rope(kps[j][0], kps[j][1], w, l * VL + V_KN, kT[:, j, ko:ko + w], ("kT", j, ko), not isctx, t0)

        def sgu_items(l, t):
            st, t0, w = t
            hold = {}

            def item_u():
                wa, wk = wa_next()
                hold["wa"] = (wa, wk)
                S.dma("sp", wa, wins[l][2], reads=[("wins", l, 2)], writes=[wk])
                for j in range(2):
                    ps, pk = ps_rot()
                    S.mm(ps[:, 0:w], [(wa[:, kc, j * 128:(j + 1) * 128], xn[:, kc, 0:w]) for kc in range(KC)],
                         reads=[wk] + xn_keys(0), writes=[pk])
                    gelu_to(uT[:, j, 0:w], ps[:, 0:w], pk, ("uT", j), 128, w)

            def item_v(si):
                wa, wk = hold["wa"]
                ps, pk = ps_rot()
                S.mm(ps[:, 0:256], [(xn[:, kc, si * 128:(si + 1) * 128], wa[:, kc, 256:512]) for kc in range(KC)],
                     reads=[wk] + xn_keys(0), writes=[pk])
                i = cnt["vn"] % 2; cnt["vn"] += 1
                g_ = gv[i]; gk = ("gv", i); v_ = vn[i]; vk = ("vn", i)
                gelu_to(g_, ps[:, 0:256], pk, gk, 128, 256)
                sq, sk = tmp("sq")
                sm = small[:, (cnt["sm"] % 8) * 2:(cnt["sm"] % 8) * 2 + 1]; smk = ("sm", cnt["sm"] % 8); cnt["sm"] += 1
                S.op("act", lambda e: e.activation(out=sq[:, 0:256], in_=g_, func=AF.Square, accum_out=sm),
                     reads=[gk], writes=[sk, smk])
                S.op("act", lambda e: e.activation(out=sm, in_=sm, func=AF.Sqrt, bias=epsc[:, 0:1], scale=1.0 / 256),
                     reads=[smk, ("epsc",)], writes=[smk])
                S.op("dve", lambda e: e.reciprocal(out=sm, in_=sm), reads=[smk], writes=[smk])
                S.op("dve", lambda e: e.scalar_tensor_tensor(out=v_, in0=g_, scalar=sm, in1=sgun[:, l, :], op0=ALU.mult, op1=ALU.mult),
                     reads=[gk, smk, ("sgun",)], writes=[vk])
                p2, p2k = ps_rot()
                for gi in range(4):
                    S.mm(p2[(gi % 2) * 64:(gi % 2) * 64 + 64, (gi // 2) * 128:(gi // 2) * 128 + 128],
                         [(v_[:, gi * 64:(gi + 1) * 64], sguwT[:, l, gi, :])], reads=[vk, ("sguwT",)], writes=[p2k])
                for c in range(2):
                    tt, tk = tmp("t32")
                    S.op("dve", lambda e: e.tensor_tensor(out=tt[:, 0:128], in0=p2[:, c * 128:(c + 1) * 128], in1=sgub[:, l, c, :], op=ALU.add),
                         reads=[p2k, ("sgub",)], writes=[tk])
                    us = uT[:, c, si * 128:(si + 1) * 128]
                    S.op("dve", lambda e: e.tensor_tensor(out=us, in0=tt[:, 0:128], in1=us, op=ALU.mult),
                         reads=[tk, ("uT", c)], writes=[("uT", c)])

            return [item_u] + [(lambda si=si: item_v(si)) for si in range(w // 128)]

        def pass2(l, bcol, t):
            st, t0, w = t
            ko = koff(t)
            isctx = st == "ctx"
            col = 2 if isctx else bcol
            modulate(t, l, 1, bcol, 0)
            wa, wk = wa_next()
            S.dma("sp", wa, wins[l][3], reads=[("wins", l, 3)], writes=[wk])
            for j in range(4):
                ps, pk = ps_rot()
                S.mm(ps[:, 0:w], [(wa[:, kc, j * 128:(j + 1) * 128], xn[:, kc, 0:w]) for kc in range(KC)],
                     reads=[wk] + xn_keys(0), writes=[pk])
                qk_norm_rope(ps, pk, w, l * VL + V_QN, qT[:, j, 0:w], ("qT", j), not isctx, t0)
            side = sgu_items(l, t)
            nblk = 2 if isctx else 16
            sbase = 16 if isctx else 0

            def pool_item(c):
                pp, ppk = ps_rot()
                for tb in range(w // 128):
                    i = t0 // 128 + tb
                    for half in range(2):
                        g = 2 * c + half
                        srcs = []
                        if i > 0:
                            srcs.append((i - 1, 3))
                        srcs.append((i, 0 if i == 0 else (2 if i == nblk - 1 else 1)))
                        if i < nblk - 1:
                            srcs.append((i + 1, 4))
                        S.mm(pp[half * 64:half * 64 + 64, tb * 128:(tb + 1) * 128],
                             [(xpool[:, sbase + si, c * 128 + half * 64:c * 128 + half * 64 + 64], bands[:, g * 5 + kind, :]) for si, kind in srcs],
                             reads=[("xpool", sbase + si) for si, _ in srcs] + [("bands",)], writes=[ppk])
                S.op("act", lambda e: e.activation(out=pooled[:, 0:w], in_=pp[:, 0:w], func=AF.Copy), reads=[ppk], writes=[("pooled",)])
                po_, pok = ps_rot()
                S.mm(po_[:, 0:w], [(poolbd[:, l, c, :], pooled[:, 0:w])], reads=[("pooled",), ("poolbd",)], writes=[pok])
                pc = l * VL + V_PSC + c
                S.op("act", lambda e: e.activation(out=poolout[:, c, 0:w], in_=po_[:, 0:w], func=AF.Copy, scale=vecs[:, pc:pc + 1]),
                     reads=[pok, ("vecs",)], writes=[("poolout", c)])

            kslices = list(range(16, 18)) if isctx else list(range(NKS))
            steps = [(h, gs) for h in range(8) for gs in kslices]
            pend = []

            def kkeys(kv, gs):
                o = gs * 128
                for tt in _tiles():
                    k0 = koff(tt)
                    if k0 <= o < k0 + tt[2]:
                        return ("kT", kv, k0)
                raise AssertionError

            def emit_S(h, gs):
                ps, pk = ps_rot()
                pr = slice((h % 2) * 64, (h % 2) * 64 + 64)
                S.mm(ps[:, 0:w], [(kT[pr, h // 4, gs * 128:(gs + 1) * 128], qT[pr, h // 2, 0:w])],
                     reads=[kkeys(h // 4, gs), ("qT", h // 2)], writes=[pk])
                i = cnt["pt"] % 4; cnt["pt"] += 1
                pt = pTb[i]; ptk = ("pT", i)
                S.op("act", lambda e: e.activation(out=pt[:, 0:w], in_=ps[:, 0:w], func=AF.Exp, scale=0.125), reads=[pk], writes=[ptk])
                return pt, ptk

            def emit_PV(h, gs, pt, ptk):
                pv = PS[6 + (h % 2)]; pvk = ("ps", 6 + (h % 2))
                S.mm(pv[:, 0:w], [(Vaug[:, gs, h // 4, :], pt[:, 0:w])], reads=[("V", gs), ("Vones",), ptk], writes=[pvk],
                     start=(gs == kslices[0]), stop=(gs == kslices[-1]))
                if gs == kslices[-1]:
                    rd, rdk = tmp("rr")
                    S.op("dve", lambda e: e.reciprocal(out=rd[64:128, 0:w], in_=pv[64:128, 0:w]), reads=[pvk], writes=[rdk])
                    po = (h % 2) * 64
                    S.op("dve", lambda e: e.tensor_tensor(out=attnT[po:po + 64, h // 2, 0:w], in0=pv[0:64, 0:w], in1=rd[64:128, 0:w], op=ALU.mult),
                         reads=[pvk, rdk], writes=[("attnT", h // 2, h % 2)])

            nblk = 2 if isctx else 16
            sbase = 16 if isctx else 0
            for c in range(2):
                side.append(lambda c=c: pool_item(c))
            LA = 2
            for i, (h, gs) in enumerate(steps):
                pend.append((h, gs) + emit_S(h, gs))
                if len(pend) > LA:
                    emit_PV(*pend.pop(0))
                if gs == kslices[-1] and side and h >= 1:
                    side.pop(0)()
            while pend:
                emit_PV(*pend.pop(0))
            while side:
                side.pop(0)()
            utk = [("uT", 0), ("uT", 1)]
            for m in range(8):
                if t[1] >= 1280 or isctx:
                    run_bg()
                wa, wk = wa_next()
                S.dma("sp", wa, wins[l][4 + m], reads=[("wins", l, 4 + m)], writes=[wk])
                gts = []
                for gi in range(3):
                    ps, pk = ps_rot()
                    S.mm(ps[:, 0:w], [(wa[:, kc, gi * 128:(gi + 1) * 128], xn[:, kc, 0:w]) for kc in range(KC)],
                         reads=[wk] + xn_keys(0), writes=[pk])
                    i = cnt["gate"] % 6; cnt["gate"] += 1
                    S.op("act", lambda e: e.activation(out=gates[i][:, 0:w], in_=ps[:, 0:w], func=AF.Sigmoid), reads=[pk], writes=[("gate", i)])
                    gts.append((gates[i], ("gate", i)))
                brs = [([(wa[:, kc, 384:512], poolout[:, kc, 0:w]) for kc in range(2)], [("poolout", 0), ("poolout", 1)]),
                       ([(wa[:, 2 + kc, 384:512], attnT[:, kc, 0:w]) for kc in range(4)], [("attnT", kc, hh) for kc in range(4) for hh in range(2)]),
                       ([(wa[:, 6 + kc, 384:512], uT[:, kc, 0:w]) for kc in range(2)], utk)]
                prods = []
                for gi in range(3):
                    ps, pk = ps_rot()
                    S.mm(ps[:, 0:w], brs[gi][0], reads=[wk] + brs[gi][1], writes=[pk])
                    g_, gk = gts[gi]
                    pr_, prk = tmp("t32")
                    S.op("dve", lambda e: e.tensor_tensor(out=pr_[:, 0:w], in0=g_[:, 0:w], in1=ps[:, 0:w], op=ALU.mult),
                         reads=[gk, pk], writes=[prk])
                    prods.append((pr_, prk))
                    if gi == 1:
                        S.op(aux[0], lambda e: e.tensor_tensor(out=prods[0][0][:, 0:w], in0=prods[0][0][:, 0:w], in1=prods[1][0][:, 0:w], op=ALU.add),
                             reads=[prods[0][1], prods[1][1]], writes=[prods[0][1]])
                S.op(aux[0], lambda e: e.tensor_tensor(out=merged[:, m, 0:w], in0=prods[0][0][:, 0:w], in1=prods[2][0][:, 0:w], op=ALU.add),
                     reads=[prods[0][1], prods[2][1]], writes=[("merged", m)])
            for og in range(2):
                wa, wk = wa_next()
                S.dma("sp", wa, wouts[l][og], reads=[("wouts", l, og)], writes=[wk])
                for j in range(4):
                    m = og * 4 + j
                    ps, pk = ps_rot()
                    S.mm(ps[:, 0:w], [(wa[:, kc, j * 128:(j + 1) * 128], merged[:, kc, 0:w]) for kc in range(KC)],
                         reads=[wk] + [("merged", kc) for kc in range(KC)], writes=[pk])
                    S.op("dve", lambda e: e.scalar_tensor_tensor(out=hap(t, m), in0=ps[:, 0:w], scalar=Gmod[:, l, 1, m, col:col + 1],
                                                                 in1=hap(t, m), op0=ALU.mult, op1=ALU.add),
                         reads=[pk, ("Gmod", l), hkey(t, m)], writes=[hkey(t, m)])

        epsc = A.alloc(4 * 4, F32)
        S.op("dve", lambda e: e.memset(epsc, EPS), writes=[("epsc",)])
        S.op("act", lambda e: e.activation(out=scT, in_=cTs, func=AF.Silu), reads=[("cTs",)], writes=[("scT",)])
        l0 = layers[0]
        precast_ffn(l0, 0)
        for g in range(18):
            adaln_piece(l0, g)
        adaln_finish(l0)
        precast_mix(l0)
        precast_ffn(l0, 1)
        for l in layers[1:]:
            precast_ffn(l, 0)
            precast_mix(l)
            precast_ffn(l, 1)
            for g in range(18):
                bg.append(lambda l=l, g=g: adaln_piece(l, g))
            bg.append(lambda l=l: adaln_finish(l))

        for b in range(nb):
            for kc in range(KC):
                ks = [hkey(t, kc) for t in _tiles(False)]
                S.dma("sp", hT[:, kc, :], xT[b][:, kc, :], writes=ks)
            S.dma("sp", hcT, ctxT[b], writes=[hkey(_tiles()[-1], kc) for kc in range(KC)])
            for l in layers:
                last = (l == 1)
                aux[0] = "dve" if (b == 0 and l == layers[0]) else "pool"
                if l != layers[0]:
                    run_bg(len(bg))
                if "ffn1" in stages:
                    ffn(l, 0, b, _tiles(True))
                if "mix" in stages:
                    S.barrier()
                    S.op("dve", lambda e: e.memset(Vaug[:, :, :, 64:128], 1.0), writes=[("Vones",)])
                    for t in _tiles(True):
                        pass1(l, b, t)
                    for t in (_tiles(not last) if p2_tiles is None else [_tiles()[i] for i in p2_tiles]):
                        pass2(l, b, t)
                    S.barrier()
                if "ffn2" in stages:
                    ffn(l, 1, b, _tiles(not last))
            if final:
                for t in _tiles(False):
                    st, t0, w = t
                    ps, pk = ps_rot()
                    for kc in range(KC):
                        sq, sk = tmp("sq")
                        S.op("act", lambda e: e.activation(out=sq[:, 0:w], in_=hap(t, kc), func=AF.Square), reads=[hkey(t, kc)], writes=[sk])
                        S.mm(ps[:, 0:w], [(ones, sq[:, 0:w])], reads=[sk, ("ones",)], writes=[pk], start=(kc == 0), stop=(kc == KC - 1))
                    rr, rk = rstd_from_ps(ps[:, 0:w], pk, w, 1.0 / D)
                    for kc in range(KC):
                        S.op("dve", lambda e: e.scalar_tensor_tensor(out=hap(t, kc), in0=hap(t, kc), scalar=vecs[:, V_FN + kc:V_FN + kc + 1],
                                                                     in1=rr[:, 0:w], op0=ALU.mult, op1=ALU.mult),
                             reads=[hkey(t, kc), rk, ("vecs",)], writes=[hkey(t, kc)])
            for t in _tiles(False):
                st, t0, w = t
                S.dma("sp", outT[b][:, :, t0:t0 + w], hT[:, :, t0:t0 + w], reads=[hkey(t, kc) for kc in range(KC)])
            S.dma("sp", hcout[b], hcT, reads=[hkey(_tiles()[-1], kc) for kc in range(KC)])
        S.drain("sp")
    return nc


def _fm(v):
    return np.ascontiguousarray(v.reshape(-1, 128).T)


def _qk_perm():
    old = np.zeros(64, np.int64)
    for r in range(2):
        for a in range(2):
            for i in range(16):
                old[r * 32 + a * 16 + i] = a * 32 + r * 16 + i
    return old


def _rope_tables():
    half = 32
    inv = (10000.0 ** (-np.arange(0, half, 2, dtype=np.float32) / half)).astype(np.float32)
    t = np.arange(L)
    row = (t // 64).astype(np.float32); col = (t % 64).astype(np.float32)
    cos = np.zeros((128, L), np.float32); sin = np.zeros((128, L), np.float32)
    for p in range(128):
        pp = p % 64
        r = pp // 32; a = (pp % 32) // 16; i = pp % 16
        ang = ((row if a == 0 else col) * inv[i]).astype(np.float32)
        cos[p] = np.cos(ang)
        sin[p] = -np.sin(ang) if r == 0 else np.sin(ang)
    return cos, sin, np.zeros((128, 128), np.float32)


def _bands():
    out = np.zeros((128, 20, 128), np.float32)
    n = 384
    for g, w in enumerate((2, 4, 8, 16)):
        hw = w // 2
        M = np.zeros((n, n), np.float64)
        for t in range(n):
            lo = max(t - hw, 0); hi = min(t + hw, n)
            M[lo:hi, t] = 1.0 / (hi - lo)
            M[t, t] -= 1.0
        out[:, g * 5 + 0] = M[0:128, 0:128]
        out[:, g * 5 + 1] = M[128:256, 128:256]
        out[:, g * 5 + 2] = M[256:384, 256:384]
        out[:, g * 5 + 3] = M[0:128, 128:256]
        out[:, g * 5 + 4] = M[128:256, 0:128]
    return out


def _host_prep(inp, core, nb=NB):
    f = lambda a: np.ascontiguousarray(np.asarray(a, dtype=np.float32))
    bs = [core * nb + i for i in range(nb)]
    m = {}
    m["xT"] = np.stack([f(inp["x"][b].T.reshape(KC, 128, L).transpose(1, 0, 2)) for b in bs])
    m["ctxT"] = np.stack([f(inp["ctx"][b].T.reshape(KC, 128, C).transpose(1, 0, 2)) for b in bs])
    cols = [inp["c"][b] for b in bs] + [inp["c_ctx"]]
    while len(cols) < 3:
        cols.insert(-1, cols[0])
    m["cT"] = f(np.stack([_fm(np.asarray(v)) for v in cols], axis=2))
    return m


def _shared_prep(inp):
    f = lambda a: np.ascontiguousarray(np.asarray(a, dtype=np.float32))
    vecs = np.zeros((128, NV), np.float32)
    for l in range(2):
        o = l * VL
        vecs[:, o + V_N1:o + V_N1 + 8] = _fm(inp["norm_ffn1"][l])
        vecs[:, o + V_NM:o + V_NM + 8] = _fm(inp["norm_mix"][l])
        vecs[:, o + V_N2:o + V_N2 + 8] = _fm(inp["norm_ffn2"][l])
        vecs[:, o + V_BADA:o + V_BADA + 72] = _fm(inp["b_ada"][l])
        vecs[:, o + V_PSC:o + V_PSC + 2] = _fm(inp["pool_scale"][l])
        vecs[:, o + V_QN] = np.tile(inp["q_norm"][l][_qk_perm()], 2)
        vecs[:, o + V_KN] = np.tile(inp["k_norm"][l][_qk_perm()], 2)
        sw = np.arange(128) ^ 32
        vecs[:, 208 + 2 * l] = vecs[sw, o + V_QN]
        vecs[:, 209 + 2 * l] = vecs[sw, o + V_KN]
    vecs[:, V_FN:V_FN + 8] = _fm(inp["final_norm"])
    m = {"vecs": vecs}
    for k in ("w_ada", "ffn1_w13", "ffn2_w13", "ffn1_w2", "ffn2_w2", "w_in", "w_br_pool", "w_br_attn", "w_br_sgu", "w_out"):
        m[k] = f(inp[k])
    pbd = np.zeros((2, 128, 2, 128), np.float32)
    for l in range(2):
        for g in range(4):
            c, hh = g // 2, g % 2
            pbd[l, hh * 64:(hh + 1) * 64, c, hh * 64:(hh + 1) * 64] = inp["pool_w"][l, g]
    m["poolbd"] = pbd
    m["sguwT"] = f(np.transpose(inp["sgu_w"], (0, 3, 1, 2)))
    m["sgun"] = f(np.broadcast_to(inp["sgu_norm"][:, None, :], (2, 128, 256)))
    sb = np.zeros((2, 128, 2, 128), np.float32)
    for l in range(2):
        for g in range(4):
            c, hh = g // 2, g % 2
            sb[l, hh * 64:(hh + 1) * 64, c, :] = inp["sgu_b"][l, g][None, :]
    m["sgub"] = sb
    cos, sin, perm = _rope_tables()
    m["ropecos"] = cos; m["ropesin"] = sin; m["ropeperm"] = perm
    m["bands"] = _bands()
    return m


_NC_CACHE = {}


def kernel(**inputs):
    inp = {k: np.asarray(v) for k, v in inputs.items()}
    if "full" not in _NC_CACHE:
        _NC_CACHE["full"] = build()
    nc = _NC_CACHE["full"]
    shared = _shared_prep(inp)
    in_maps = []
    for core in range(NCORES):
        m = dict(shared)
        m.update(_host_prep(inp, core))
        in_maps.append(m)
    res = run_bass_kernel_spmd(nc, in_maps, core_ids=list(range(NCORES)))
    out = np.empty((NCORES * NB, L, D), np.float32)
    for core in range(NCORES):
        o = res.results[core]["outT"]
        for i in range(NB):
            out[core * NB + i] = o[i].transpose(1, 0, 2).reshape(D, L).T
    return out
```

```python
import contextlib
import numpy as np
import concourse.bass as bass
import concourse.mybir as mybir
from concourse.bass_utils import run_bass_kernel_spmd

F32 = mybir.dt.float32
BF16 = mybir.dt.bfloat16
ALU = mybir.AluOpType
AF = mybir.ActivationFunctionType

D = 1024; L = 2048; C = 256; DFF = 2816; NFF = 22; KC = 8
NB = 2
NCORES = 8
EPS = 1e-6
NV = 212
VL = 100
V_N1, V_NM, V_N2, V_BADA, V_PSC, V_QN, V_KN = 0, 8, 16, 24, 96, 98, 99
V_FN = 200
NKS = 18
GELU_K = 1.5957691216057308


DEBUG = {}


class _Buf:
    __slots__ = ("w", "r")

    def __init__(self):
        self.w = None
        self.r = {}


class Sched:
    NDMA = 8

    def __init__(self, nc, es):
        self.nc = nc; self.es = es
        self.E = {"pe": nc.tensor, "act": nc.scalar, "dve": nc.vector, "pool": nc.gpsimd, "sp": nc.sync}
        self.sem = {}; self.cnt = {}; self.nsem = 0
        self.seen = {e: {} for e in self.E}
        self.allsems = {}
        for e in self.E:
            self._newsem(e)
        self.bufs = {}
        self.dq = {}; self.dqi = {}

    def _mk(self, name):
        self.nsem += 1
        s = self.es.enter_context(self.nc.semaphore(f"{name}_{self.nsem}"))
        return s

    def _newsem(self, e):
        self.sem[e] = self._mk("s" + e); self.cnt[e] = 0

    def buf(self, k):
        b = self.bufs.get(k)
        if b is None:
            b = self.bufs[k] = _Buf()
        return b

    def _wait(self, e, tok):
        sem, val = tok[0], tok[1]
        d = self.seen[e]
        if d.get(id(sem), 0) >= val:
            return
        self.E[e].wait_ge(sem, val)
        d[id(sem)] = val

    def _deps(self, e, reads, writes):
        for k in reads:
            b = self.buf(k)
            if b.w is not None and not (b.w[2] == e and e == "pe"):
                self._wait(e, b.w)
        for k in writes:
            b = self.buf(k)
            if b.w is not None and not (b.w[2] == e and e == "pe"):
                self._wait(e, b.w)
            for rk, t in b.r.items():
                if t[2] == e and e == "pe":
                    continue
                self._wait(e, t)

    def _commit(self, tok, reads, writes):
        rk = tok[2] if tok[2] != "dma" else id(tok[0])
        for k in reads:
            self.buf(k).r[rk] = tok
        for k in writes:
            b = self.buf(k); b.w = tok; b.r = {}

    def _tok(self, e, ins):
        if self.cnt[e] >= 4000:
            self._newsem(e)
        self.cnt[e] += 1
        ins.then_inc(self.sem[e], 1)
        tok = (self.sem[e], self.cnt[e], e)
        self.allsems[id(tok[0])] = tok
        return tok

    def op(self, e, fn, reads=(), writes=()):
        self._deps(e, reads, writes)
        ins = fn(self.E[e])
        self._commit(self._tok(e, ins), reads, writes)

    def mm(self, out, pairs, reads, writes, start=True, stop=True):
        self._deps("pe", reads, writes)
        n = len(pairs); ins = None
        for i, (lt, rh) in enumerate(pairs):
            ins = self.nc.tensor.matmul(out, lt, rh, start=(start and i == 0), stop=(stop and i == n - 1))
        self._commit(self._tok("pe", ins), reads, writes)

    def dma(self, q, out, in_, reads=(), writes=()):
        pool = self.dq.setdefault(q, [])
        i = self.dqi.get(q, 0); self.dqi[q] = i + 1
        if len(pool) < (16 if q == "pool" else self.NDMA):
            pool.append([self._mk("d" + q), 0])
        ent = pool[i % (16 if q == "pool" else self.NDMA)]
        if ent[1] > 0:
            self._wait(q, (ent[0], ent[1]))
        self._deps(q, reads, writes)
        ins = self.E[q].dma_start(out=out, in_=in_)
        ent[1] += 16
        ins.then_inc(ent[0], 16)
        tok = (ent[0], ent[1], "dma")
        self.allsems[id(ent[0])] = tok
        self._commit(tok, reads, writes)

    def barrier(self):
        skip = set(id(x[0]) for x in self.dq.get("pool", []))
        for e in self.E:
            for t in list(self.allsems.values()):
                if t[2] == e or id(t[0]) in skip:
                    continue
                self._wait(e, t)

    def drain(self, e):
        for t in list(self.allsems.values()):
            if t[2] == "dma":
                self._wait(e, t)


class Arena:
    def __init__(self, nc, es, nbytes):
        self.t = es.enter_context(nc.sbuf_tensor("arena", [128, nbytes // 2], BF16))
        self.cap = nbytes; self.off = 0

    def alloc(self, nbytes, dt=BF16, at=None, name=None):
        rb = (nbytes + 127) // 128 * 128
        if name:
            DEBUG[name] = (self.off if at is None else at, nbytes)
        if at is None:
            at = self.off; self.off += rb
        assert at + rb <= self.cap, (at, rb, self.cap)
        ap = self.t[:, at // 2:(at + nbytes) // 2]
        if dt == F32:
            ap = ap.bitcast(F32)
        return ap

    def mark(self):
        return self.off

    def reset(self, m):
        self.off = m


def _tiles(include_ctx=True):
    t = [("lat", 0, 512), ("lat", 512, 256), ("lat", 768, 512), ("lat", 1280, 256), ("lat", 1536, 512)]
    if include_ctx:
        t.append(("ctx", 0, 256))
    return t


def build(nb=NB, layers=(0, 1), stages=("ffn1", "mix", "ffn2"), final=True, p2_tiles=None):
    nc = bass.Bass("TRN2", target_bir_lowering=False)

    def din(name, shape, dt=F32):
        return nc.dram_tensor(name, list(shape), dt, kind="ExternalInput").ap()

    def dscr(name, shape, dt=BF16):
        return nc.dram_tensor(name, list(shape), dt, kind="Internal").ap()

    xT = din("xT", [nb, 128, KC, L]); ctxT = din("ctxT", [nb, 128, KC, C])
    cT = din("cT", [128, KC, 3]); vecs_d = din("vecs", [128, NV])
    w_ada = din("w_ada", [2, D, 9 * D])
    w13_d = [din("ffn1_w13", [2, D, 2 * DFF]), din("ffn2_w13", [2, D, 2 * DFF])]
    w2_d = [din("ffn1_w2", [2, DFF, D]), din("ffn2_w2", [2, DFF, D])]
    w_in = din("w_in", [2, D, 4608])
    w_brp = din("w_br_pool", [2, 256, D]); w_bra = din("w_br_attn", [2, 512, D]); w_brs = din("w_br_sgu", [2, 256, D])
    w_out = din("w_out", [2, D, D])
    poolbd_d = din("poolbd", [2, 128, 2, 128]); sguwT_d = din("sguwT", [2, 128, 4, 128])
    sgun_d = din("sgun", [2, 128, 256]); sgub_d = din("sgub", [2, 128, 2, 128])
    cos_d = din("ropecos", [128, L]); sin_d = din("ropesin", [128, L]); perm_d = din("ropeperm", [128, 128])
    band_d = din("bands", [128, 20, 128])
    outT = nc.dram_tensor("outT", [nb, 128, KC, L], F32, kind="ExternalOutput").ap()
    hcout = nc.dram_tensor("hcout", [nb, 128, KC, C], F32, kind="ExternalOutput").ap()

    w13s = [[dscr(f"w13s_{l}_{f}", [11, 128, KC, 512]) for f in range(2)] for l in range(2)]
    w2s = [[dscr(f"w2s_{l}_{f}", [4, 128, NFF, 256]) for f in range(2)] for l in range(2)]
    wins = [dscr(f"wins_{l}", [12, 128, KC, 512]) for l in range(2)]
    wouts = [dscr(f"wouts_{l}", [2, 128, KC, 512]) for l in range(2)]

    with contextlib.ExitStack() as es:
        S = Sched(nc, es)
        A = Arena(nc, es, 207 * 1024)
        PS = [es.enter_context(nc.psum_tensor(f"ps{i}", [128, 512], F32)) for i in range(8)]
        psi = [0]

        def ps_rot():
            i = psi[0] % 6; psi[0] += 1
            return PS[i], ("ps", i)

        hT = A.alloc(KC * L * 4, F32).rearrange("p (k n) -> p k n", k=KC)
        hcT = A.alloc(KC * C * 4, F32).rearrange("p (k n) -> p k n", k=KC)
        cosb = A.alloc(L * 2); sinb = A.alloc(L * 2)
        perm = A.alloc(128 * 4, F32)
        vecs = A.alloc(NV * 4, F32)
        ones = A.alloc(128 * 2); bdones = A.alloc(128 * 2)
        bands = A.alloc(20 * 128 * 2).rearrange("p (k n) -> p k n", k=20)
        poolbd = A.alloc(2 * 2 * 128 * 2).rearrange("p (l c n) -> p l c n", l=2, c=2)
        sguwT = A.alloc(2 * 4 * 128 * 2).rearrange("p (l g n) -> p l g n", l=2, g=4)
        sgun = A.alloc(2 * 256 * 4, F32).rearrange("p (l n) -> p l n", l=2)
        sgub = A.alloc(2 * 2 * 128 * 4, F32).rearrange("p (l c n) -> p l c n", l=2, c=2)
        cTs = A.alloc(KC * 3 * 4, F32).rearrange("p (k n) -> p k n", k=KC)
        scT = A.alloc(KC * 3 * 2).rearrange("p (k n) -> p k n", k=KC)
        modT = A.alloc(2 * 72 * 3 * 4, F32).rearrange("p (l j n) -> p l j n", l=2, j=72)
        Amod = A.alloc(2 * 3 * KC * 3 * 4, F32).rearrange("p (l s k n) -> p l s k n", l=2, s=3, k=KC)
        Gmod = A.alloc(2 * 3 * KC * 3 * 4, F32).rearrange("p (l s k n) -> p l s k n", l=2, s=3, k=KC)
        NWA = 2
        wA = [A.alloc(KC * 512 * 2).rearrange("p (k n) -> p k n", k=KC) for _ in range(NWA)]
        wai = [0]

        def wa_next():
            i = wai[0] % NWA; wai[0] += 1
            return wA[i], ("wA", i)

        sqb = [A.alloc(512 * 2) for _ in range(2)]
        t32 = [A.alloc(512 * 4, F32) for _ in range(4)]
        rrb = [A.alloc(512 * 4, F32) for _ in range(2)]
        tci = {"sq": 0, "t32": 0, "rr": 0}

        def tmp(kind):
            lst = {"sq": sqb, "t32": t32, "rr": rrb}[kind]
            i = tci[kind] % len(lst); tci[kind] += 1
            return lst[i], (kind, i)

        xn = A.alloc(KC * 768 * 2, name="xn").rearrange("p (k n) -> p k n", k=KC)
        xnb = [xn, None]
        phase0 = A.mark()
        xn2 = A.alloc(KC * 768 * 2).rearrange("p (k n) -> p k n", k=KC)
        xnb[1] = xn2
        u_ff = A.alloc(NFF * 768 * 2).rearrange("p (k n) -> p k n", k=NFF)
        wB = [A.alloc(NFF * 256 * 2).rearrange("p (k n) -> p k n", k=NFF) for _ in range(2)]
        ffn_end = A.mark()
        A.reset(phase0)
        kT = A.alloc(2 * NKS * 128 * 2, name="kT").rearrange("p (k n) -> p k n", k=2)
        Vaug = A.alloc(NKS * 2 * 128 * 2, name="Vaug").rearrange("p (s k n) -> p s k n", s=NKS, k=2)
        xpool = A.alloc(NKS * 256 * 2, name="xpool").rearrange("p (s n) -> p s n", s=NKS)
        uT = A.alloc(2 * 512 * 2, name="uT").rearrange("p (k n) -> p k n", k=2)
        qT = A.alloc(4 * 512 * 2, name="qT").rearrange("p (k n) -> p k n", k=4)
        attnT = A.alloc(4 * 512 * 2, name="attnT").rearrange("p (k n) -> p k n", k=4)
        pTb = [A.alloc(512 * 2) for _ in range(4)]
        pooled = A.alloc(512 * 2)
        poolout = A.alloc(2 * 512 * 2, name="poolout").rearrange("p (k n) -> p k n", k=2)
        gates = [A.alloc(512 * 2) for _ in range(6)]
        merged = A.alloc(KC * 512 * 2, name="merged").rearrange("p (k n) -> p k n", k=KC)
        kg32 = [A.alloc(512 * 4, F32) for _ in range(1)]
        vn = [A.alloc(256 * 2) for _ in range(2)]
        gv = [A.alloc(256 * 4, F32) for _ in range(2)]
        small = A.alloc(64 * 4, F32)
        mix_end = A.mark()
        A.reset(max(ffn_end, mix_end))
        print("SBUF bytes/partition: ffn_end", ffn_end, "mix_end", mix_end)

        cnt = {"pt": 0, "gate": 0, "kg": 0, "vn": 0, "sm": 0}

        S.dma("sp", vecs, vecs_d, writes=[("vecs",)])
        S.dma("sp", cTs, cT, writes=[("cTs",)])
        S.dma("sp", perm, perm_d, writes=[("perm",)])
        S.dma("sp", sgun, sgun_d.rearrange("l p n -> p l n"), writes=[("sgun",)])
        S.dma("sp", sgub, sgub_d.rearrange("l p c n -> p l c n"), writes=[("sgub",)])
        S.dma("pool", cosb, cos_d, writes=[("cos",)])
        S.dma("pool", sinb, sin_d, writes=[("sin",)])
        S.dma("pool", bands, band_d, writes=[("bands",)])
        S.dma("pool", poolbd, poolbd_d.rearrange("l p c n -> p l c n"), writes=[("poolbd",)])
        S.dma("pool", sguwT, sguwT_d.rearrange("l p g n -> p l g n"), writes=[("sguwT",)])
        S.op("dve", lambda e: e.memset(ones, 1.0), writes=[("ones",)])
        S.op("dve", lambda e: e.memset(bdones, 0.0), writes=[("bdones",)])
        S.op("dve", lambda e: e.memset(bdones[0:64, 0:64], 1.0), writes=[("bdones",)])
        S.op("dve", lambda e: e.memset(bdones[64:128, 64:128], 1.0), writes=[("bdones",)])

        def precast_ffn(l, f):
            src = w13_d[f][l].rearrange("(k p) n -> p k n", p=128)
            for g in range(11):
                S.dma("pool", w13s[l][f][g][:, :, 0:256], src[:, :, g * 256:(g + 1) * 256], writes=[("w13s", l, f, g)])
                S.dma("pool", w13s[l][f][g][:, :, 256:512], src[:, :, DFF + g * 256:DFF + (g + 1) * 256],
                      writes=[("w13s", l, f, g)])
            src2 = w2_d[f][l].rearrange("(k p) n -> p k n", p=128)
            for g in range(4):
                S.dma("pool", w2s[l][f][g], src2[:, :, g * 256:(g + 1) * 256], writes=[("w2s", l, f, g)])

        def precast_mix(l):
            src = w_in[l].rearrange("(k p) n -> p k n", p=128)
            W = wins[l]
            k0 = ("wins", l, 0)
            for kv in range(2):
                for dup in range(2):
                    for r in range(2):
                        d0 = kv * 128 + dup * 64 + r * 32
                        s0 = 512 + kv * 64 + r * 16
                        for a in range(2):
                            S.dma("pool", W[0][:, :, d0 + a * 16:d0 + a * 16 + 16], src[:, :, s0 + a * 32:s0 + a * 32 + 16], writes=[k0])
            S.dma("pool", W[1][:, :, 0:128], src[:, :, 640:768], writes=[("wins", l, 1)])
            S.dma("pool", W[1][:, :, 128:384], src[:, :, 768:1024], writes=[("wins", l, 1)])
            S.dma("pool", W[2], src[:, :, 1024:1536], writes=[("wins", l, 2)])
            for h in range(8):
                for r in range(2):
                    d0 = h * 64 + r * 32
                    s0 = h * 64 + r * 16
                    for a in range(2):
                        S.dma("pool", W[3][:, :, d0 + a * 16:d0 + a * 16 + 16], src[:, :, s0 + a * 32:s0 + a * 32 + 16], writes=[("wins", l, 3)])
            brp = w_brp[l].rearrange("(k p) n -> p k n", p=128)
            bra = w_bra[l].rearrange("(k p) n -> p k n", p=128)
            brs = w_brs[l].rearrange("(k p) n -> p k n", p=128)
            for m in range(8):
                key = ("wins", l, 4 + m)
                for gi in range(3):
                    S.dma("pool", W[4 + m][:, :, gi * 128:(gi + 1) * 128],
                          src[:, :, 1536 + gi * 1024 + m * 128: 1536 + gi * 1024 + (m + 1) * 128], writes=[key])
                S.dma("pool", W[4 + m][:, 0:2, 384:512], brp[:, :, m * 128:(m + 1) * 128], writes=[key])
                S.dma("pool", W[4 + m][:, 2:6, 384:512], bra[:, :, m * 128:(m + 1) * 128], writes=[key])
                S.dma("pool", W[4 + m][:, 6:8, 384:512], brs[:, :, m * 128:(m + 1) * 128], writes=[key])
            so = w_out[l].rearrange("(k p) n -> p k n", p=128)
            for og in range(2):
                S.dma("pool", wouts[l][og], so[:, :, og * 512:(og + 1) * 512], writes=[("wouts", l, og)])

        def adaln_piece(l, g):
            src = w_ada[l].rearrange("(k p) n -> p k n", p=128)
            wa, wk = wa_next()
            S.dma("pool", wa, src[:, :, g * 512:(g + 1) * 512], writes=[wk])
            pm, pmk = ps_rot()
            for j in range(4):
                S.mm(pm[:, j * 3:(j + 1) * 3], [(wa[:, kc, j * 128:(j + 1) * 128], scT[:, kc, :]) for kc in range(KC)],
                     reads=[wk, ("scT",)], writes=[pmk])
            bo = l * VL + V_BADA + g * 4
            S.op("dve", lambda e: e.tensor_tensor(out=modT[:, l, g * 4:(g + 1) * 4, :], in0=pm[:, 0:12].rearrange("p (j n) -> p j n", n=3),
                                                  in1=vecs[:, bo:bo + 4].unsqueeze(2).to_broadcast([128, 4, 3]), op=ALU.add),
                 reads=[pmk, ("vecs",)], writes=[("modT", l)])

        def adaln_finish(l):
            for s in range(3):
                no = l * VL + (V_N1, V_NM, V_N2)[s]
                S.op("dve", lambda e: e.scalar_tensor_tensor(
                    out=Amod[:, l, s], in0=modT[:, l, (3 * s + 1) * 8:(3 * s + 2) * 8, :], scalar=1.0,
                    in1=vecs[:, no:no + 8].unsqueeze(2).to_broadcast([128, 8, 3]), op0=ALU.add, op1=ALU.mult),
                    reads=[("modT", l), ("vecs",)], writes=[("Amod", l)])
                gs = 1.0 if s == 1 else 0.5
                S.op("dve", lambda e: e.tensor_scalar(out=Gmod[:, l, s], in0=modT[:, l, (3 * s + 2) * 8:(3 * s + 3) * 8, :],
                                                      scalar1=gs, scalar2=None, op0=ALU.mult),
                     reads=[("modT", l)], writes=[("Gmod", l)])

        bg = []
        aux = ["dve"]

        def run_bg(n=1):
            for _ in range(n):
                if bg:
                    bg.pop(0)()

        def hap(t, kc):
            st, t0, w = t
            return (hT if st == "lat" else hcT)[:, kc, t0:t0 + w]

        def hkey(t, kc):
            return ("h", t[0], t[1], kc)

        def rstd_from_ps(ps_ap, pskey, w, scale, npart=128):
            sr, srk = tmp("rr")
            S.op("act", lambda e: e.activation(out=sr[:, 0:w], in_=ps_ap, func=AF.Sqrt, bias=epsc[:, 0:1], scale=scale),
                 reads=[pskey, ("epsc",)], writes=[srk])
            S.op("dve", lambda e: e.reciprocal(out=sr[:, 0:w], in_=sr[:, 0:w]), reads=[srk], writes=[srk])
            return sr, srk

        def modulate(t, l, s, bcol, xoff, xb=0):
            st, t0, w = t
            xn = xnb[xb]
            col = 2 if st == "ctx" else bcol
            ps, pk = ps_rot()
            for kc in range(KC):
                sq, sk = tmp("sq")
                S.op("act", lambda e: e.activation(out=sq[:, 0:w], in_=hap(t, kc), func=AF.Square),
                     reads=[hkey(t, kc)], writes=[sk])
                S.mm(ps[:, 0:w], [(ones, sq[:, 0:w])], reads=[sk, ("ones",)], writes=[pk], start=(kc == 0), stop=(kc == KC - 1))
            rr, rk = rstd_from_ps(ps[:, 0:w], pk, w, 1.0 / D)
            for kc in range(KC):
                tt, tk = tmp("t32")
                S.op("dve", lambda e: e.scalar_tensor_tensor(out=tt[:, 0:w], in0=hap(t, kc), scalar=Amod[:, l, s, kc, col:col + 1],
                                                             in1=rr[:, 0:w], op0=ALU.mult, op1=ALU.mult),
                     reads=[hkey(t, kc), rk, ("Amod", l)], writes=[tk])
                S.op("act", lambda e: e.activation(out=xn[:, kc, xoff:xoff + w], in_=tt[:, 0:w], func=AF.Identity,
                                                   bias=modT[:, l, 3 * s * 8 + kc, col:col + 1], scale=1.0),
                     reads=[tk, ("modT", l)], writes=[("xn", xb, kc, xoff)])

        def xn_keys(xoff, xb=0):
            return [("xn", xb, kc, xoff) for kc in range(KC)]

        def ffn(l, f, bcol, tiles):
            s = 0 if f == 0 else 2
            sbs = [tiles[i:i + 2] for i in range(0, len(tiles), 2)]

            def offs_of(sb):
                offs = []; o = 0
                for t in sb:
                    offs.append(o); o += t[2]
                return offs

            def mod_sb(i):
                for t, xo in zip(sbs[i], offs_of(sbs[i])):
                    modulate(t, l, s, bcol, xo, xb=i % 2)

            mod_sb(0)
            for i, sb in enumerate(sbs):
                offs = offs_of(sb)
                xb = i % 2
                xn = xnb[xb]
                for g in range(11):
                    wa, wk = wa_next()
                    S.dma("sp", wa, w13s[l][f][g], reads=[("w13s", l, f, g)], writes=[wk])
                    for j in range(2):
                        n = 2 * g + j
                        for t, xo in zip(sb, offs):
                            w = t[2]
                            pa, pak = ps_rot(); pb, pbk = ps_rot()
                            S.mm(pa[:, 0:w], [(wa[:, kc, j * 128:(j + 1) * 128], xn[:, kc, xo:xo + w]) for kc in range(KC)],
                                 reads=[wk] + xn_keys(xo, xb), writes=[pak])
                            S.mm(pb[:, 0:w], [(wa[:, kc, 256 + j * 128:256 + (j + 1) * 128], xn[:, kc, xo:xo + w]) for kc in range(KC)],
                                 reads=[wk] + xn_keys(xo, xb), writes=[pbk])
                            sa, sak = tmp("t32")
                            S.op("act", lambda e: e.activation(out=sa[:, 0:w], in_=pa[:, 0:w], func=AF.Silu), reads=[pak], writes=[sak])
                            S.op("dve", lambda e: e.tensor_tensor(out=u_ff[:, n, xo:xo + w], in0=sa[:, 0:w], in1=pb[:, 0:w], op=ALU.mult),
                                 reads=[sak, pbk], writes=[("u", n, xo)])
                for g in range(4):
                    if g == 2 and i + 1 < len(sbs):
                        mod_sb(i + 1)
                    wb = wB[g % 2]; wbk = ("wB", g % 2)
                    S.dma("sp", wb, w2s[l][f][g], reads=[("w2s", l, f, g)], writes=[wbk])
                    for j in range(2):
                        m = 2 * g + j
                        for t, xo in zip(sb, offs):
                            w = t[2]; col = 2 if t[0] == "ctx" else bcol
                            py, pyk = ps_rot()
                            S.mm(py[:, 0:w], [(wb[:, n, j * 128:(j + 1) * 128], u_ff[:, n, xo:xo + w]) for n in range(NFF)],
                                 reads=[wbk] + [("u", n, xo) for n in range(NFF)], writes=[pyk])
                            S.op("dve", lambda e: e.scalar_tensor_tensor(out=hap(t, m), in0=py[:, 0:w], scalar=Gmod[:, l, s, m, col:col + 1],
                                                                         in1=hap(t, m), op0=ALU.mult, op1=ALU.add),
                                 reads=[pyk, ("Gmod", l), hkey(t, m)], writes=[hkey(t, m)])

        def gelu_to(out_ap, ps_ap, pskey, outkey, npart, w):
            S.op("act", lambda e: e.activation(out=out_ap, in_=ps_ap, func=AF.Gelu_apprx_tanh), reads=[pskey], writes=[outkey])

        def qk_norm_rope(ps, pk, w, gain_col, out_ap, outkey, rope, tok0):
            sq, sk = tmp("sq")
            S.op("act", lambda e: e.activation(out=sq[:, 0:w], in_=ps[:, 0:w], func=AF.Square), reads=[pk], writes=[sk])
            p2, p2k = ps_rot()
            S.mm(p2[:, 0:w], [(bdones, sq[:, 0:w])], reads=[sk, ("bdones",)], writes=[p2k])
            rr, rk = rstd_from_ps(p2[:, 0:w], p2k, w, 1.0 / 64)
            if not rope:
                S.op("dve", lambda e: e.scalar_tensor_tensor(out=out_ap, in0=ps[:, 0:w], scalar=vecs[:, gain_col:gain_col + 1],
                                                             in1=rr[:, 0:w], op0=ALU.mult, op1=ALU.mult),
                     reads=[pk, rk, ("vecs",)], writes=[outkey])
                return
            i = 0
            kg = kg32[i]; kk = ("kg", i)
            S.op("dve", lambda e: e.scalar_tensor_tensor(out=kg[:, 0:w], in0=ps[:, 0:w], scalar=vecs[:, gain_col:gain_col + 1],
                                                         in1=rr[:, 0:w], op0=ALU.mult, op1=ALU.mult),
                 reads=[pk, rk, ("vecs",)], writes=[kk])
            w1, w1k = tmp("t32")
            S.op(aux[0], lambda e: e.tensor_tensor(out=w1[:, 0:w], in0=rr[:, 0:w], in1=sinb[:, tok0:tok0 + w], op=ALU.mult),
                 reads=[rk, ("sin",)], writes=[w1k])
            t1, t1k = tmp("t32")
            gsw = 208 + 2 * (gain_col // VL) + (1 if gain_col % VL == V_KN else 0)
            for qd in range(4):
                o = qd * 32; so = (qd ^ 1) * 32
                S.op("dve", lambda e: e.scalar_tensor_tensor(out=t1[o:o + 32, 0:w], in0=ps[so:so + 32, 0:w], scalar=vecs[o:o + 32, gsw:gsw + 1],
                                                             in1=w1[o:o + 32, 0:w], op0=ALU.mult, op1=ALU.mult),
                     reads=[pk, w1k, ("vecs",)], writes=[t1k])
            S.op(aux[0], lambda e: e.tensor_tensor(out=kg[:, 0:w], in0=kg[:, 0:w], in1=cosb[:, tok0:tok0 + w], op=ALU.mult),
                 reads=[kk, ("cos",)], writes=[kk])
            S.op(aux[0], lambda e: e.tensor_tensor(out=out_ap, in0=kg[:, 0:w], in1=t1[:, 0:w], op=ALU.add),
                 reads=[kk, t1k], writes=[outkey])

        def koff(t):
            return t[1] if t[0] == "lat" else L + t[1]

        def pass1_all(l, bcol, tiles):
            kps_of = {}

            def p1_mod(t):
                modulate(t, l, 1, bcol, 0)

            def p1_proj(t):
                st, t0, w = t
                ko = koff(t)
                wa, wk = wa_next()
                S.dma("sp", wa[:, :, 0:256], wins[l][0][:, :, 0:256], reads=[("wins", l, 0)], writes=[wk])
                kps = []
                for j in range(2):
                    ps, pk = PS[6 + j], ("ps", 6 + j)
                    S.mm(ps[:, 0:w], [(wa[:, kc, j * 128:(j + 1) * 128], xn[:, kc, 0:w]) for kc in range(KC)],
                         reads=[wk] + xn_keys(0), writes=[pk])
                    kps.append((ps, pk))
                kps_of[t] = kps
                wa, wk = wa_next()
                S.dma("sp", wa[:, :, 0:384], wins[l][1][:, :, 0:384], reads=[("wins", l, 1)], writes=[wk])
                for si in range(w // 128):
                    gs = (ko + si * 128) // 128
                    ps, pk = ps_rot()
                    S.mm(ps[:, 0:384], [(xn[:, kc, si * 128:(si + 1) * 128], wa[:, kc, 0:384]) for kc in range(KC)],
                         reads=[wk] + xn_keys(0), writes=[pk])
                    S.op("act", lambda e: e.activation(out=Vaug[:, gs, :, 0:64], in_=ps[:, 0:128].rearrange("p (k n) -> p k n", k=2),
                                                       func=AF.Copy), reads=[pk], writes=[("V", gs)])
                    S.op("act", lambda e: e.activation(out=xpool[:, gs, :], in_=ps[:, 128:384], func=AF.Copy), reads=[pk], writes=[("xpool", gs)])

            def p1_chain(t):
                st, t0, w = t
                ko = koff(t)
                kps = kps_of[t]
                for j in range(2):
                    qk_norm_rope(kps[j][0], kps[j][1], w, l * VL + V_KN, kT[:, j, ko:ko + w], ("kT", j, ko), st != "ctx", t0)

            p1_mod(tiles[0]); p1_proj(tiles[0])
            for k in range(1, len(tiles)):
                p1_mod(tiles[k])
                p1_chain(tiles[k - 1])
                p1_proj(tiles[k])
            p1_chain(tiles[-1])

        def sgu_items(l, t):
            st, t0, w = t
            hold = {}

            def item_u():
                wa, wk = wa_next()
                hold["wa"] = (wa, wk)
                S.dma("sp", wa, wins[l][2], reads=[("wins", l, 2)], writes=[wk])
                for j in range(2):
                    ps, pk = ps_rot()
                    S.mm(ps[:, 0:w], [(wa[:, kc, j * 128:(j + 1) * 128], xn[:, kc, 0:w]) for kc in range(KC)],
                         reads=[wk] + xn_keys(0), writes=[pk])
                    gelu_to(uT[:, j, 0:w], ps[:, 0:w], pk, ("uT", j), 128, w)

            def item_v(si):
                wa, wk = hold["wa"]
                ps, pk = ps_rot()
                S.mm(ps[:, 0:256], [(xn[:, kc, si * 128:(si + 1) * 128], wa[:, kc, 256:512]) for kc in range(KC)],
                     reads=[wk] + xn_keys(0), writes=[pk])
                i = cnt["vn"] % 2; cnt["vn"] += 1
                g_ = gv[i]; gk = ("gv", i); v_ = vn[i]; vk = ("vn", i)
                gelu_to(g_, ps[:, 0:256], pk, gk, 128, 256)
                sq, sk = tmp("sq")
                sm = small[:, (cnt["sm"] % 8) * 2:(cnt["sm"] % 8) * 2 + 1]; smk = ("sm", cnt["sm"] % 8); cnt["sm"] += 1
                S.op("act", lambda e: e.activation(out=sq[:, 0:256], in_=g_, func=AF.Square, accum_out=sm),
                     reads=[gk], writes=[sk, smk])
                S.op("act", lambda e: e.activation(out=sm, in_=sm, func=AF.Sqrt, bias=epsc[:, 0:1], scale=1.0 / 256),
                     reads=[smk, ("epsc",)], writes=[smk])
                S.op("dve", lambda e: e.reciprocal(out=sm, in_=sm), reads=[smk], writes=[smk])
                S.op("dve", lambda e: e.scalar_tensor_tensor(out=v_, in0=g_, scalar=sm, in1=sgun[:, l, :], op0=ALU.mult, op1=ALU.mult),
                     reads=[gk, smk, ("sgun",)], writes=[vk])
                p2, p2k = ps_rot()
                for gi in range(4):
                    S.mm(p2[(gi % 2) * 64:(gi % 2) * 64 + 64, (gi // 2) * 128:(gi // 2) * 128 + 128],
                         [(v_[:, gi * 64:(gi + 1) * 64], sguwT[:, l, gi, :])], reads=[vk, ("sguwT",)], writes=[p2k])
                for c in range(2):
                    tt, tk = tmp("t32")
                    S.op("dve", lambda e: e.tensor_tensor(out=tt[:, 0:128], in0=p2[:, c * 128:(c + 1) * 128], in1=sgub[:, l, c, :], op=ALU.add),
                         reads=[p2k, ("sgub",)], writes=[tk])
                    us = uT[:, c, si * 128:(si + 1) * 128]
                    S.op("dve", lambda e: e.tensor_tensor(out=us, in0=tt[:, 0:128], in1=us, op=ALU.mult),
                         reads=[tk, ("uT", c)], writes=[("uT", c)])

            return [item_u] + [(lambda si=si: item_v(si)) for si in range(w // 128)]

        def pass2(l, bcol, t):
            st, t0, w = t
            ko = koff(t)
            isctx = st == "ctx"
            col = 2 if isctx else bcol
            modulate(t, l, 1, bcol, 0)
            wa, wk = wa_next()
            S.dma("sp", wa, wins[l][3], reads=[("wins", l, 3)], writes=[wk])
            for j in range(4):
                ps, pk = ps_rot()
                S.mm(ps[:, 0:w], [(wa[:, kc, j * 128:(j + 1) * 128], xn[:, kc, 0:w]) for kc in range(KC)],
                     reads=[wk] + xn_keys(0), writes=[pk])
                qk_norm_rope(ps, pk, w, l * VL + V_QN, qT[:, j, 0:w], ("qT", j), not isctx, t0)
            side = sgu_items(l, t)
            nblk = 2 if isctx else 16
            sbase = 16 if isctx else 0

            def pool_item(c):
                pp, ppk = ps_rot()
                for tb in range(w // 128):
                    i = t0 // 128 + tb
                    for half in range(2):
                        g = 2 * c + half
                        srcs = []
                        if i > 0:
                            srcs.append((i - 1, 3))
                        srcs.append((i, 0 if i == 0 else (2 if i == nblk - 1 else 1)))
                        if i < nblk - 1:
                            srcs.append((i + 1, 4))
                        S.mm(pp[half * 64:half * 64 + 64, tb * 128:(tb + 1) * 128],
                             [(xpool[:, sbase + si, c * 128 + half * 64:c * 128 + half * 64 + 64], bands[:, g * 5 + kind, :]) for si, kind in srcs],
                             reads=[("xpool", sbase + si) for si, _ in srcs] + [("bands",)], writes=[ppk])
                S.op("act", lambda e: e.activation(out=pooled[:, 0:w], in_=pp[:, 0:w], func=AF.Copy), reads=[ppk], writes=[("pooled",)])
                po_, pok = ps_rot()
                S.mm(po_[:, 0:w], [(poolbd[:, l, c, :], pooled[:, 0:w])], reads=[("pooled",), ("poolbd",)], writes=[pok])
                pc = l * VL + V_PSC + c
                S.op("act", lambda e: e.activation(out=poolout[:, c, 0:w], in_=po_[:, 0:w], func=AF.Copy, scale=vecs[:, pc:pc + 1]),
                     reads=[pok, ("vecs",)], writes=[("poolout", c)])

            kslices = list(range(16, 18)) if isctx else list(range(NKS))
            steps = [(h, gs) for h in range(8) for gs in kslices]
            pend = []

            def kkeys(kv, gs):
                o = gs * 128
                for tt in _tiles():
                    k0 = koff(tt)
                    if k0 <= o < k0 + tt[2]:
                        return ("kT", kv, k0)
                raise AssertionError

            def emit_S(h, gs):
                ps, pk = ps_rot()
                pr = slice((h % 2) * 64, (h % 2) * 64 + 64)
                S.mm(ps[:, 0:w], [(kT[pr, h // 4, gs * 128:(gs + 1) * 128], qT[pr, h // 2, 0:w])],
                     reads=[kkeys(h // 4, gs), ("qT", h // 2)], writes=[pk])
                i = cnt["pt"] % 4; cnt["pt"] += 1
                pt = pTb[i]; ptk = ("pT", i)
                S.op("act", lambda e: e.activation(out=pt[:, 0:w], in_=ps[:, 0:w], func=AF.Exp, scale=0.125), reads=[pk], writes=[ptk])
                return pt, ptk

            def emit_PV(h, gs, pt, ptk):
                pv = PS[6 + (h % 2)]; pvk = ("ps", 6 + (h % 2))
                S.mm(pv[:, 0:w], [(Vaug[:, gs, h // 4, :], pt[:, 0:w])], reads=[("V", gs), ("Vones",), ptk], writes=[pvk],
                     start=(gs == kslices[0]), stop=(gs == kslices[-1]))
                if gs == kslices[-1]:
                    rd, rdk = tmp("rr")
                    S.op("dve", lambda e: e.reciprocal(out=rd[64:128, 0:w], in_=pv[64:128, 0:w]), reads=[pvk], writes=[rdk])
                    po = (h % 2) * 64
                    S.op("dve", lambda e: e.tensor_tensor(out=attnT[po:po + 64, h // 2, 0:w], in0=pv[0:64, 0:w], in1=rd[64:128, 0:w], op=ALU.mult),
                         reads=[pvk, rdk], writes=[("attnT", h // 2, h % 2)])

            nblk = 2 if isctx else 16
            sbase = 16 if isctx else 0
            for c in range(2):
                side.append(lambda c=c: pool_item(c))
            LA = 2
            for i, (h, gs) in enumerate(steps):
                pend.append((h, gs) + emit_S(h, gs))
                if len(pend) > LA:
                    emit_PV(*pend.pop(0))
                if gs == kslices[-1] and side and h >= 1:
                    side.pop(0)()
            while pend:
                emit_PV(*pend.pop(0))
            while side:
                side.pop(0)()
            utk = [("uT", 0), ("uT", 1)]
            for m in range(8):
                if t[1] >= 1280 or isctx:
                    run_bg()
                wa, wk = wa_next()
                S.dma("sp", wa, wins[l][4 + m], reads=[("wins", l, 4 + m)], writes=[wk])
                gts = []
                for gi in range(3):
                    ps, pk = ps_rot()
                    S.mm(ps[:, 0:w], [(wa[:, kc, gi * 128:(gi + 1) * 128], xn[:, kc, 0:w]) for kc in range(KC)],
                         reads=[wk] + xn_keys(0), writes=[pk])
                    i = cnt["gate"] % 6; cnt["gate"] += 1
                    S.op("act", lambda e: e.activation(out=gates[i][:, 0:w], in_=ps[:, 0:w], func=AF.Sigmoid), reads=[pk], writes=[("gate", i)])
                    gts.append((gates[i], ("gate", i)))
                brs = [([(wa[:, kc, 384:512], poolout[:, kc, 0:w]) for kc in range(2)], [("poolout", 0), ("poolout", 1)]),
                       ([(wa[:, 2 + kc, 384:512], attnT[:, kc, 0:w]) for kc in range(4)], [("attnT", kc, hh) for kc in range(4) for hh in range(2)]),
                       ([(wa[:, 6 + kc, 384:512], uT[:, kc, 0:w]) for kc in range(2)], utk)]
                prods = []
                for gi in range(3):
                    ps, pk = ps_rot()
                    S.mm(ps[:, 0:w], brs[gi][0], reads=[wk] + brs[gi][1], writes=[pk])
                    g_, gk = gts[gi]
                    pr_, prk = tmp("t32")
                    S.op("dve", lambda e: e.tensor_tensor(out=pr_[:, 0:w], in0=g_[:, 0:w], in1=ps[:, 0:w], op=ALU.mult),
                         reads=[gk, pk], writes=[prk])
                    prods.append((pr_, prk))
                    if gi == 1:
                        S.op(aux[0], lambda e: e.tensor_tensor(out=prods[0][0][:, 0:w], in0=prods[0][0][:, 0:w], in1=prods[1][0][:, 0:w], op=ALU.add),
                             reads=[prods[0][1], prods[1][1]], writes=[prods[0][1]])
                S.op(aux[0], lambda e: e.tensor_tensor(out=merged[:, m, 0:w], in0=prods[0][0][:, 0:w], in1=prods[2][0][:, 0:w], op=ALU.add),
                     reads=[prods[0][1], prods[2][1]], writes=[("merged", m)])
            for og in range(2):
                wa, wk = wa_next()
                S.dma("sp", wa, wouts[l][og], reads=[("wouts", l, og)], writes=[wk])
                for j in range(4):
                    m = og * 4 + j
                    ps, pk = ps_rot()
                    S.mm(ps[:, 0:w], [(wa[:, kc, j * 128:(j + 1) * 128], merged[:, kc, 0:w]) for kc in range(KC)],
                         reads=[wk] + [("merged", kc) for kc in range(KC)], writes=[pk])
                    S.op("dve", lambda e: e.scalar_tensor_tensor(out=hap(t, m), in0=ps[:, 0:w], scalar=Gmod[:, l, 1, m, col:col + 1],
                                                                 in1=hap(t, m), op0=ALU.mult, op1=ALU.add),
                         reads=[pk, ("Gmod", l), hkey(t, m)], writes=[hkey(t, m)])

        epsc = A.alloc(4 * 4, F32)
        S.op("dve", lambda e: e.memset(epsc, EPS), writes=[("epsc",)])
        S.op("act", lambda e: e.activation(out=scT, in_=cTs, func=AF.Silu), reads=[("cTs",)], writes=[("scT",)])
        l0 = layers[0]
        precast_ffn(l0, 0)
        for g in range(18):
            adaln_piece(l0, g)
        adaln_finish(l0)
        precast_mix(l0)
        precast_ffn(l0, 1)
        for l in layers[1:]:
            precast_ffn(l, 0)
            precast_mix(l)
            precast_ffn(l, 1)
            for g in range(18):
                bg.append(lambda l=l, g=g: adaln_piece(l, g))
            bg.append(lambda l=l: adaln_finish(l))

        for b in range(nb):
            for kc in range(KC):
                ks = [hkey(t, kc) for t in _tiles(False)]
                S.dma("sp", hT[:, kc, :], xT[b][:, kc, :], writes=ks)
            S.dma("sp", hcT, ctxT[b], writes=[hkey(_tiles()[-1], kc) for kc in range(KC)])
            for l in layers:
                last = (l == 1)
                aux[0] = "dve" if (b == 0 and l == layers[0]) else "pool"
                if l != layers[0]:
                    run_bg(len(bg))
                if "ffn1" in stages:
                    ffn(l, 0, b, _tiles(True))
                if "mix" in stages:
                    S.barrier()
                    S.op("dve", lambda e: e.memset(Vaug[:, :, :, 64:128], 1.0), writes=[("Vones",)])
                    pass1_all(l, b, _tiles(True))
                    for t in (_tiles(not last) if p2_tiles is None else [_tiles()[i] for i in p2_tiles]):
                        pass2(l, b, t)
                    S.barrier()
                if "ffn2" in stages:
                    ffn(l, 1, b, _tiles(not last))
            if final:
                for t in _tiles(False):
                    st, t0, w = t
                    ps, pk = ps_rot()
                    for kc in range(KC):
                        sq, sk = tmp("sq")
                        S.op("act", lambda e: e.activation(out=sq[:, 0:w], in_=hap(t, kc), func=AF.Square), reads=[hkey(t, kc)], writes=[sk])
                        S.mm(ps[:, 0:w], [(ones, sq[:, 0:w])], reads=[sk, ("ones",)], writes=[pk], start=(kc == 0), stop=(kc == KC - 1))
                    rr, rk = rstd_from_ps(ps[:, 0:w], pk, w, 1.0 / D)
                    for kc in range(KC):
                        S.op("dve", lambda e: e.scalar_tensor_tensor(out=hap(t, kc), in0=hap(t, kc), scalar=vecs[:, V_FN + kc:V_FN + kc + 1],
                                                                     in1=rr[:, 0:w], op0=ALU.mult, op1=ALU.mult),
                             reads=[hkey(t, kc), rk, ("vecs",)], writes=[hkey(t, kc)])
            for t in _tiles(False):
                st, t0, w = t
                S.dma("sp", outT[b][:, :, t0:t0 + w], hT[:, :, t0:t0 + w], reads=[hkey(t, kc) for kc in range(KC)])
            S.dma("sp", hcout[b], hcT, reads=[hkey(_tiles()[-1], kc) for kc in range(KC)])
        S.drain("sp")
    return nc


def _fm(v):
    return np.ascontiguousarray(v.reshape(-1, 128).T)


def _qk_perm():
    old = np.zeros(64, np.int64)
    for r in range(2):
        for a in range(2):
            for i in range(16):
                old[r * 32 + a * 16 + i] = a * 32 + r * 16 + i
    return old


def _rope_tables():
    half = 32
    inv = (10000.0 ** (-np.arange(0, half, 2, dtype=np.float32) / half)).astype(np.float32)
    t = np.arange(L)
    row = (t // 64).astype(np.float32); col = (t % 64).astype(np.float32)
    cos = np.zeros((128, L), np.float32); sin = np.zeros((128, L), np.float32)
    for p in range(128):
        pp = p % 64
        r = pp // 32; a = (pp % 32) // 16; i = pp % 16
        ang = ((row if a == 0 else col) * inv[i]).astype(np.float32)
        cos[p] = np.cos(ang)
        sin[p] = -np.sin(ang) if r == 0 else np.sin(ang)
    return cos, sin, np.zeros((128, 128), np.float32)


def _bands():
    out = np.zeros((128, 20, 128), np.float32)
    n = 384
    for g, w in enumerate((2, 4, 8, 16)):
        hw = w // 2
        M = np.zeros((n, n), np.float64)
        for t in range(n):
            lo = max(t - hw, 0); hi = min(t + hw, n)
            M[lo:hi, t] = 1.0 / (hi - lo)
            M[t, t] -= 1.0
        out[:, g * 5 + 0] = M[0:128, 0:128]
        out[:, g * 5 + 1] = M[128:256, 128:256]
        out[:, g * 5 + 2] = M[256:384, 256:384]
        out[:, g * 5 + 3] = M[0:128, 128:256]
        out[:, g * 5 + 4] = M[128:256, 0:128]
    return out


def _host_prep(inp, core, nb=NB):
    f = lambda a: np.ascontiguousarray(np.asarray(a, dtype=np.float32))
    bs = [core * nb + i for i in range(nb)]
    m = {}
    m["xT"] = np.stack([f(inp["x"][b].T.reshape(KC, 128, L).transpose(1, 0, 2)) for b in bs])
    m["ctxT"] = np.stack([f(inp["ctx"][b].T.reshape(KC, 128, C).transpose(1, 0, 2)) for b in bs])
    cols = [inp["c"][b] for b in bs] + [inp["c_ctx"]]
    while len(cols) < 3:
        cols.insert(-1, cols[0])
    m["cT"] = f(np.stack([_fm(np.asarray(v)) for v in cols], axis=2))
    return m


def _shared_prep(inp):
    f = lambda a: np.ascontiguousarray(np.asarray(a, dtype=np.float32))
    vecs = np.zeros((128, NV), np.float32)
    for l in range(2):
        o = l * VL
        vecs[:, o + V_N1:o + V_N1 + 8] = _fm(inp["norm_ffn1"][l])
        vecs[:, o + V_NM:o + V_NM + 8] = _fm(inp["norm_mix"][l])
        vecs[:, o + V_N2:o + V_N2 + 8] = _fm(inp["norm_ffn2"][l])
        vecs[:, o + V_BADA:o + V_BADA + 72] = _fm(inp["b_ada"][l])
        vecs[:, o + V_PSC:o + V_PSC + 2] = _fm(inp["pool_scale"][l])
        vecs[:, o + V_QN] = np.tile(inp["q_norm"][l][_qk_perm()], 2)
        vecs[:, o + V_KN] = np.tile(inp["k_norm"][l][_qk_perm()], 2)
        sw = np.arange(128) ^ 32
        vecs[:, 208 + 2 * l] = vecs[sw, o + V_QN]
        vecs[:, 209 + 2 * l] = vecs[sw, o + V_KN]
    vecs[:, V_FN:V_FN + 8] = _fm(inp["final_norm"])
    m = {"vecs": vecs}
    for k in ("w_ada", "ffn1_w13", "ffn2_w13", "ffn1_w2", "ffn2_w2", "w_in", "w_br_pool", "w_br_attn", "w_br_sgu", "w_out"):
        m[k] = f(inp[k])
    pbd = np.zeros((2, 128, 2, 128), np.float32)
    for l in range(2):
        for g in range(4):
            c, hh = g // 2, g % 2
            pbd[l, hh * 64:(hh + 1) * 64, c, hh * 64:(hh + 1) * 64] = inp["pool_w"][l, g]
    m["poolbd"] = pbd
    m["sguwT"] = f(np.transpose(inp["sgu_w"], (0, 3, 1, 2)))
    m["sgun"] = f(np.broadcast_to(inp["sgu_norm"][:, None, :], (2, 128, 256)))
    sb = np.zeros((2, 128, 2, 128), np.float32)
    for l in range(2):
        for g in range(4):
            c, hh = g // 2, g % 2
            sb[l, hh * 64:(hh + 1) * 64, c, :] = inp["sgu_b"][l, g][None, :]
    m["sgub"] = sb
    cos, sin, perm = _rope_tables()
    m["ropecos"] = cos; m["ropesin"] = sin; m["ropeperm"] = perm
    m["bands"] = _bands()
    return m


_NC_CACHE = {}


def kernel(**inputs):
    inp = {k: np.asarray(v) for k, v in inputs.items()}
    if "full" not in _NC_CACHE:
        _NC_CACHE["full"] = build()
    nc = _NC_CACHE["full"]
    shared = _shared_prep(inp)
    in_maps = []
    for core in range(NCORES):
        m = dict(shared)
        m.update(_host_prep(inp, core))
        in_maps.append(m)
    res = run_bass_kernel_spmd(nc, in_maps, core_ids=list(range(NCORES)))
    out = np.empty((NCORES * NB, L, D), np.float32)
    for core in range(NCORES):
        o = res.results[core]["outT"]
        for i in range(NB):
            out[core * NB + i] = o[i].transpose(1, 0, 2).reshape(D, L).T
    return out
```

```python
import contextlib
import numpy as np
import concourse.bass as bass
import concourse.mybir as mybir
from concourse.bass_utils import run_bass_kernel_spmd

F32 = mybir.dt.float32
BF16 = mybir.dt.bfloat16
ALU = mybir.AluOpType
AF = mybir.ActivationFunctionType

D = 1024; L = 2048; C = 256; DFF = 2816; NFF = 22; KC = 8
NB = 2
NCORES = 8
EPS = 1e-6
NV = 212
VL = 100
V_N1, V_NM, V_N2, V_BADA, V_PSC, V_QN, V_KN = 0, 8, 16, 24, 96, 98, 99
V_FN = 200
NKS = 18
GELU_K = 1.5957691216057308


DEBUG = {}


class _Buf:
    __slots__ = ("w", "r")

    def __init__(self):
        self.w = None
        self.r = {}


class Sched:
    NDMA = 8

    def __init__(self, nc, es):
        self.nc = nc; self.es = es
        self.E = {"pe": nc.tensor, "act": nc.scalar, "dve": nc.vector, "pool": nc.gpsimd, "sp": nc.sync}
        self.sem = {}; self.cnt = {}; self.nsem = 0
        self.seen = {e: {} for e in self.E}
        self.allsems = {}
        for e in self.E:
            self._newsem(e)
        self.bufs = {}
        self.dq = {}; self.dqi = {}

    def _mk(self, name):
        self.nsem += 1
        s = self.es.enter_context(self.nc.semaphore(f"{name}_{self.nsem}"))
        return s

    def _newsem(self, e):
        self.sem[e] = self._mk("s" + e); self.cnt[e] = 0

    def buf(self, k):
        b = self.bufs.get(k)
        if b is None:
            b = self.bufs[k] = _Buf()
        return b

    def _wait(self, e, tok):
        sem, val = tok[0], tok[1]
        d = self.seen[e]
        if d.get(id(sem), 0) >= val:
            return
        self.E[e].wait_ge(sem, val)
        d[id(sem)] = val

    def _deps(self, e, reads, writes):
        for k in reads:
            b = self.buf(k)
            if b.w is not None and not (b.w[2] == e and e == "pe"):
                self._wait(e, b.w)
        for k in writes:
            b = self.buf(k)
            if b.w is not None and not (b.w[2] == e and e == "pe"):
                self._wait(e, b.w)
            for rk, t in b.r.items():
                if t[2] == e and e == "pe":
                    continue
                self._wait(e, t)

    def _commit(self, tok, reads, writes):
        rk = tok[2] if tok[2] != "dma" else id(tok[0])
        for k in reads:
            self.buf(k).r[rk] = tok
        for k in writes:
            b = self.buf(k); b.w = tok; b.r = {}

    def _tok(self, e, ins):
        if self.cnt[e] >= 4000:
            self._newsem(e)
        self.cnt[e] += 1
        ins.then_inc(self.sem[e], 1)
        tok = (self.sem[e], self.cnt[e], e)
        self.allsems[id(tok[0])] = tok
        return tok

    def op(self, e, fn, reads=(), writes=()):
        self._deps(e, reads, writes)
        ins = fn(self.E[e])
        self._commit(self._tok(e, ins), reads, writes)

    def mm(self, out, pairs, reads, writes, start=True, stop=True):
        self._deps("pe", reads, writes)
        n = len(pairs); ins = None
        for i, (lt, rh) in enumerate(pairs):
            ins = self.nc.tensor.matmul(out, lt, rh, start=(start and i == 0), stop=(stop and i == n - 1))
        self._commit(self._tok("pe", ins), reads, writes)

    def dma(self, q, out, in_, reads=(), writes=()):
        pool = self.dq.setdefault(q, [])
        i = self.dqi.get(q, 0); self.dqi[q] = i + 1
        if len(pool) < (16 if q == "pool" else self.NDMA):
            pool.append([self._mk("d" + q), 0])
        ent = pool[i % (16 if q == "pool" else self.NDMA)]
        if ent[1] > 0:
            self._wait(q, (ent[0], ent[1]))
        self._deps(q, reads, writes)
        ins = self.E[q].dma_start(out=out, in_=in_)
        ent[1] += 16
        ins.then_inc(ent[0], 16)
        tok = (ent[0], ent[1], "dma")
        self.allsems[id(ent[0])] = tok
        self._commit(tok, reads, writes)

    def barrier(self):
        skip = set(id(x[0]) for x in self.dq.get("pool", []))
        for e in self.E:
            for t in list(self.allsems.values()):
                if t[2] == e or id(t[0]) in skip:
                    continue
                self._wait(e, t)

    def drain(self, e):
        for t in list(self.allsems.values()):
            if t[2] == "dma":
                self._wait(e, t)


class Arena:
    def __init__(self, nc, es, nbytes):
        self.t = es.enter_context(nc.sbuf_tensor("arena", [128, nbytes // 2], BF16))
        self.cap = nbytes; self.off = 0

    def alloc(self, nbytes, dt=BF16, at=None, name=None):
        rb = (nbytes + 127) // 128 * 128
        if name:
            DEBUG[name] = (self.off if at is None else at, nbytes)
        if at is None:
            at = self.off; self.off += rb
        assert at + rb <= self.cap, (at, rb, self.cap)
        ap = self.t[:, at // 2:(at + nbytes) // 2]
        if dt == F32:
            ap = ap.bitcast(F32)
        return ap

    def mark(self):
        return self.off

    def reset(self, m):
        self.off = m


def _tiles(include_ctx=True):
    t = [("lat", 0, 512), ("lat", 512, 256), ("lat", 768, 512), ("lat", 1280, 256), ("lat", 1536, 512)]
    if include_ctx:
        t.append(("ctx", 0, 256))
    return t


def build(nb=NB, layers=(0, 1), stages=("ffn1", "mix", "ffn2"), final=True, p2_tiles=None):
    nc = bass.Bass("TRN2", target_bir_lowering=False)

    def din(name, shape, dt=F32):
        return nc.dram_tensor(name, list(shape), dt, kind="ExternalInput").ap()

    def dscr(name, shape, dt=BF16):
        return nc.dram_tensor(name, list(shape), dt, kind="Internal").ap()

    xT = din("xT", [nb, 128, KC, L]); ctxT = din("ctxT", [nb, 128, KC, C])
    cT = din("cT", [128, KC, 3]); vecs_d = din("vecs", [128, NV])
    w_ada = din("w_ada", [2, D, 9 * D])
    w13_d = [din("ffn1_w13", [2, D, 2 * DFF]), din("ffn2_w13", [2, D, 2 * DFF])]
    w2_d = [din("ffn1_w2", [2, DFF, D]), din("ffn2_w2", [2, DFF, D])]
    w_in = din("w_in", [2, D, 4608])
    w_brp = din("w_br_pool", [2, 256, D]); w_bra = din("w_br_attn", [2, 512, D]); w_brs = din("w_br_sgu", [2, 256, D])
    w_out = din("w_out", [2, D, D])
    poolbd_d = din("poolbd", [2, 128, 2, 128]); sguwT_d = din("sguwT", [2, 128, 4, 128])
    sgun_d = din("sgun", [2, 128, 256]); sgub_d = din("sgub", [2, 128, 2, 128])
    cos_d = din("ropecos", [128, L]); sin_d = din("ropesin", [128, L]); perm_d = din("ropeperm", [128, 128])
    band_d = din("bands", [128, 20, 128])
    outT = nc.dram_tensor("outT", [nb, 128, KC, L], F32, kind="ExternalOutput").ap()
    hcout = nc.dram_tensor("hcout", [nb, 128, KC, C], F32, kind="ExternalOutput").ap()

    w13s = [[dscr(f"w13s_{l}_{f}", [11, 128, KC, 512]) for f in range(2)] for l in range(2)]
    w2s = [[dscr(f"w2s_{l}_{f}", [4, 128, NFF, 256]) for f in range(2)] for l in range(2)]
    wins = [dscr(f"wins_{l}", [12, 128, KC, 512]) for l in range(2)]
    wouts = [dscr(f"wouts_{l}", [2, 128, KC, 512]) for l in range(2)]

    with contextlib.ExitStack() as es:
        S = Sched(nc, es)
        A = Arena(nc, es, 207 * 1024)
        PS = [es.enter_context(nc.psum_tensor(f"ps{i}", [128, 512], F32)) for i in range(8)]
        psi = [0]

        def ps_rot():
            i = psi[0] % 6; psi[0] += 1
            return PS[i], ("ps", i)

        hT = A.alloc(KC * L * 4, F32).rearrange("p (k n) -> p k n", k=KC)
        hcT = A.alloc(KC * C * 4, F32).rearrange("p (k n) -> p k n", k=KC)
        cosb = A.alloc(L * 2); sinb = A.alloc(L * 2)
        perm = A.alloc(128 * 4, F32)
        vecs = A.alloc(NV * 4, F32)
        ones = A.alloc(128 * 2); bdones = A.alloc(128 * 2)
        bands = A.alloc(20 * 128 * 2).rearrange("p (k n) -> p k n", k=20)
        poolbd = A.alloc(2 * 2 * 128 * 2).rearrange("p (l c n) -> p l c n", l=2, c=2)
        sguwT = A.alloc(2 * 4 * 128 * 2).rearrange("p (l g n) -> p l g n", l=2, g=4)
        sgun = A.alloc(2 * 256 * 4, F32).rearrange("p (l n) -> p l n", l=2)
        sgub = A.alloc(2 * 2 * 128 * 4, F32).rearrange("p (l c n) -> p l c n", l=2, c=2)
        cTs = A.alloc(KC * 3 * 4, F32).rearrange("p (k n) -> p k n", k=KC)
        scT = A.alloc(KC * 3 * 2).rearrange("p (k n) -> p k n", k=KC)
        modT = A.alloc(2 * 72 * 3 * 4, F32).rearrange("p (l j n) -> p l j n", l=2, j=72)
        Amod = A.alloc(2 * 3 * KC * 3 * 4, F32).rearrange("p (l s k n) -> p l s k n", l=2, s=3, k=KC)
        Gmod = A.alloc(2 * 3 * KC * 3 * 4, F32).rearrange("p (l s k n) -> p l s k n", l=2, s=3, k=KC)
        NWA = 2
        wA = [A.alloc(KC * 512 * 2).rearrange("p (k n) -> p k n", k=KC) for _ in range(NWA)]
        wai = [0]

        def wa_next():
            i = wai[0] % NWA; wai[0] += 1
            return wA[i], ("wA", i)

        sqb = [A.alloc(512 * 2) for _ in range(2)]
        t32 = [A.alloc(512 * 4, F32) for _ in range(4)]
        rrb = [A.alloc(512 * 4, F32) for _ in range(2)]
        tci = {"sq": 0, "t32": 0, "rr": 0}

        def tmp(kind):
            lst = {"sq": sqb, "t32": t32, "rr": rrb}[kind]
            i = tci[kind] % len(lst); tci[kind] += 1
            return lst[i], (kind, i)

        xn = A.alloc(KC * 768 * 2, name="xn").rearrange("p (k n) -> p k n", k=KC)
        xnb = [xn, None]
        phase0 = A.mark()
        xn2 = A.alloc(KC * 768 * 2).rearrange("p (k n) -> p k n", k=KC)
        xnb[1] = xn2
        u_ff = A.alloc(NFF * 768 * 2).rearrange("p (k n) -> p k n", k=NFF)
        wB = [A.alloc(NFF * 256 * 2).rearrange("p (k n) -> p k n", k=NFF) for _ in range(2)]
        ffn_end = A.mark()
        A.reset(phase0)
        kT = A.alloc(2 * NKS * 128 * 2, name="kT").rearrange("p (k n) -> p k n", k=2)
        Vaug = A.alloc(NKS * 2 * 128 * 2, name="Vaug").rearrange("p (s k n) -> p s k n", s=NKS, k=2)
        xpool = A.alloc(NKS * 256 * 2, name="xpool").rearrange("p (s n) -> p s n", s=NKS)
        uT = A.alloc(2 * 512 * 2, name="uT").rearrange("p (k n) -> p k n", k=2)
        qT = A.alloc(4 * 512 * 2, name="qT").rearrange("p (k n) -> p k n", k=4)
        attnT = A.alloc(4 * 512 * 2, name="attnT").rearrange("p (k n) -> p k n", k=4)
        pTb = [A.alloc(512 * 2) for _ in range(4)]
        pooled = A.alloc(512 * 2)
        poolout = A.alloc(2 * 512 * 2, name="poolout").rearrange("p (k n) -> p k n", k=2)
        gates = [A.alloc(512 * 2) for _ in range(6)]
        merged = A.alloc(KC * 512 * 2, name="merged").rearrange("p (k n) -> p k n", k=KC)
        kg32 = [A.alloc(512 * 4, F32) for _ in range(1)]
        vn = [A.alloc(256 * 2) for _ in range(2)]
        gv = [A.alloc(256 * 4, F32) for _ in range(2)]
        small = A.alloc(64 * 4, F32)
        mix_end = A.mark()
        A.reset(max(ffn_end, mix_end))
        print("SBUF bytes/partition: ffn_end", ffn_end, "mix_end", mix_end)

        cnt = {"pt": 0, "gate": 0, "kg": 0, "vn": 0, "sm": 0}

        S.dma("sp", vecs, vecs_d, writes=[("vecs",)])
        S.dma("sp", cTs, cT, writes=[("cTs",)])
        S.dma("sp", perm, perm_d, writes=[("perm",)])
        S.dma("sp", sgun, sgun_d.rearrange("l p n -> p l n"), writes=[("sgun",)])
        S.dma("sp", sgub, sgub_d.rearrange("l p c n -> p l c n"), writes=[("sgub",)])
        S.dma("pool", cosb, cos_d, writes=[("cos",)])
        S.dma("pool", sinb, sin_d, writes=[("sin",)])
        S.dma("pool", bands, band_d, writes=[("bands",)])
        S.dma("pool", poolbd, poolbd_d.rearrange("l p c n -> p l c n"), writes=[("poolbd",)])
        S.dma("pool", sguwT, sguwT_d.rearrange("l p g n -> p l g n"), writes=[("sguwT",)])
        S.op("dve", lambda e: e.memset(ones, 1.0), writes=[("ones",)])
        S.op("dve", lambda e: e.memset(bdones, 0.0), writes=[("bdones",)])
        S.op("dve", lambda e: e.memset(bdones[0:64, 0:64], 1.0), writes=[("bdones",)])
        S.op("dve", lambda e: e.memset(bdones[64:128, 64:128], 1.0), writes=[("bdones",)])

        def precast_ffn(l, f):
            src = w13_d[f][l].rearrange("(k p) n -> p k n", p=128)
            for g in range(11):
                S.dma("pool", w13s[l][f][g][:, :, 0:256], src[:, :, g * 256:(g + 1) * 256], writes=[("w13s", l, f, g)])
                S.dma("pool", w13s[l][f][g][:, :, 256:512], src[:, :, DFF + g * 256:DFF + (g + 1) * 256],
                      writes=[("w13s", l, f, g)])
            src2 = w2_d[f][l].rearrange("(k p) n -> p k n", p=128)
            for g in range(4):
                S.dma("pool", w2s[l][f][g], src2[:, :, g * 256:(g + 1) * 256], writes=[("w2s", l, f, g)])

        def precast_mix(l):
            src = w_in[l].rearrange("(k p) n -> p k n", p=128)
            W = wins[l]
            k0 = ("wins", l, 0)
            for kv in range(2):
                for dup in range(2):
                    for r in range(2):
                        d0 = kv * 128 + dup * 64 + r * 32
                        s0 = 512 + kv * 64 + r * 16
                        for a in range(2):
                            S.dma("pool", W[0][:, :, d0 + a * 16:d0 + a * 16 + 16], src[:, :, s0 + a * 32:s0 + a * 32 + 16], writes=[k0])
            S.dma("pool", W[1][:, :, 0:128], src[:, :, 640:768], writes=[("wins", l, 1)])
            S.dma("pool", W[1][:, :, 128:384], src[:, :, 768:1024], writes=[("wins", l, 1)])
            S.dma("pool", W[2], src[:, :, 1024:1536], writes=[("wins", l, 2)])
            for h in range(8):
                for r in range(2):
                    d0 = h * 64 + r * 32
                    s0 = h * 64 + r * 16
                    for a in range(2):
                        S.dma("pool", W[3][:, :, d0 + a * 16:d0 + a * 16 + 16], src[:, :, s0 + a * 32:s0 + a * 32 + 16], writes=[("wins", l, 3)])
            brp = w_brp[l].rearrange("(k p) n -> p k n", p=128)
            bra = w_bra[l].rearrange("(k p) n -> p k n", p=128)
            brs = w_brs[l].rearrange("(k p) n -> p k n", p=128)
            for m in range(8):
                key = ("wins", l, 4 + m)
                for gi in range(3):
                    S.dma("pool", W[4 + m][:, :, gi * 128:(gi + 1) * 128],
                          src[:, :, 1536 + gi * 1024 + m * 128: 1536 + gi * 1024 + (m + 1) * 128], writes=[key])
                S.dma("pool", W[4 + m][:, 0:2, 384:512], brp[:, :, m * 128:(m + 1) * 128], writes=[key])
                S.dma("pool", W[4 + m][:, 2:6, 384:512], bra[:, :, m * 128:(m + 1) * 128], writes=[key])
                S.dma("pool", W[4 + m][:, 6:8, 384:512], brs[:, :, m * 128:(m + 1) * 128], writes=[key])
            so = w_out[l].rearrange("(k p) n -> p k n", p=128)
            for og in range(2):
                S.dma("pool", wouts[l][og], so[:, :, og * 512:(og + 1) * 512], writes=[("wouts", l, og)])

        def adaln_piece(l, g):
            src = w_ada[l].rearrange("(k p) n -> p k n", p=128)
            wa, wk = wa_next()
            S.dma("pool", wa, src[:, :, g * 512:(g + 1) * 512], writes=[wk])
            pm, pmk = ps_rot()
            for j in range(4):
                S.mm(pm[:, j * 3:(j + 1) * 3], [(wa[:, kc, j * 128:(j + 1) * 128], scT[:, kc, :]) for kc in range(KC)],
                     reads=[wk, ("scT",)], writes=[pmk])
            bo = l * VL + V_BADA + g * 4
            S.op("dve", lambda e: e.tensor_tensor(out=modT[:, l, g * 4:(g + 1) * 4, :], in0=pm[:, 0:12].rearrange("p (j n) -> p j n", n=3),
                                                  in1=vecs[:, bo:bo + 4].unsqueeze(2).to_broadcast([128, 4, 3]), op=ALU.add),
                 reads=[pmk, ("vecs",)], writes=[("modT", l)])

        def adaln_finish(l):
            for s in range(3):
                no = l * VL + (V_N1, V_NM, V_N2)[s]
                S.op("dve", lambda e: e.scalar_tensor_tensor(
                    out=Amod[:, l, s], in0=modT[:, l, (3 * s + 1) * 8:(3 * s + 2) * 8, :], scalar=1.0,
                    in1=vecs[:, no:no + 8].unsqueeze(2).to_broadcast([128, 8, 3]), op0=ALU.add, op1=ALU.mult),
                    reads=[("modT", l), ("vecs",)], writes=[("Amod", l)])
                gs = 1.0 if s == 1 else 0.5
                S.op("dve", lambda e: e.tensor_scalar(out=Gmod[:, l, s], in0=modT[:, l, (3 * s + 2) * 8:(3 * s + 3) * 8, :],
                                                      scalar1=gs, scalar2=None, op0=ALU.mult),
                     reads=[("modT", l)], writes=[("Gmod", l)])

        bg = []
        aux = ["dve"]

        def run_bg(n=1):
            for _ in range(n):
                if bg:
                    bg.pop(0)()

        def hap(t, kc):
            st, t0, w = t
            return (hT if st == "lat" else hcT)[:, kc, t0:t0 + w]

        def hkey(t, kc):
            return ("h", t[0], t[1], kc)

        def rstd_from_ps(ps_ap, pskey, w, scale, npart=128):
            sr, srk = tmp("rr")
            S.op("act", lambda e: e.activation(out=sr[:, 0:w], in_=ps_ap, func=AF.Ln, bias=epsc[:, 0:1], scale=scale),
                 reads=[pskey, ("epsc",)], writes=[srk])
            S.op("act", lambda e: e.activation(out=sr[:, 0:w], in_=sr[:, 0:w], func=AF.Exp, scale=-0.5), reads=[srk], writes=[srk])
            return sr, srk

        def modulate(t, l, s, bcol, xoff, xb=0):
            st, t0, w = t
            xn = xnb[xb]
            col = 2 if st == "ctx" else bcol
            ps, pk = ps_rot()
            for kc in range(KC):
                sq, sk = tmp("sq")
                S.op("act", lambda e: e.activation(out=sq[:, 0:w], in_=hap(t, kc), func=AF.Square),
                     reads=[hkey(t, kc)], writes=[sk])
                S.mm(ps[:, 0:w], [(ones, sq[:, 0:w])], reads=[sk, ("ones",)], writes=[pk], start=(kc == 0), stop=(kc == KC - 1))
            rr, rk = rstd_from_ps(ps[:, 0:w], pk, w, 1.0 / D)
            for kc in range(KC):
                tt, tk = tmp("t32")
                S.op("dve", lambda e: e.scalar_tensor_tensor(out=tt[:, 0:w], in0=hap(t, kc), scalar=Amod[:, l, s, kc, col:col + 1],
                                                             in1=rr[:, 0:w], op0=ALU.mult, op1=ALU.mult),
                     reads=[hkey(t, kc), rk, ("Amod", l)], writes=[tk])
                S.op("act", lambda e: e.activation(out=xn[:, kc, xoff:xoff + w], in_=tt[:, 0:w], func=AF.Identity,
                                                   bias=modT[:, l, 3 * s * 8 + kc, col:col + 1], scale=1.0),
                     reads=[tk, ("modT", l)], writes=[("xn", xb, kc, xoff)])

        def xn_keys(xoff, xb=0):
            return [("xn", xb, kc, xoff) for kc in range(KC)]

        def ffn(l, f, bcol, tiles):
            s = 0 if f == 0 else 2
            sbs = [tiles[i:i + 2] for i in range(0, len(tiles), 2)]

            def offs_of(sb):
                offs = []; o = 0
                for t in sb:
                    offs.append(o); o += t[2]
                return offs

            def mod_sb(i):
                for t, xo in zip(sbs[i], offs_of(sbs[i])):
                    modulate(t, l, s, bcol, xo, xb=i % 2)

            mod_sb(0)
            for i, sb in enumerate(sbs):
                offs = offs_of(sb)
                xb = i % 2
                xn = xnb[xb]
                for g in range(11):
                    wa, wk = wa_next()
                    S.dma("sp", wa, w13s[l][f][g], reads=[("w13s", l, f, g)], writes=[wk])
                    for j in range(2):
                        n = 2 * g + j
                        for t, xo in zip(sb, offs):
                            w = t[2]
                            pa, pak = ps_rot(); pb, pbk = ps_rot()
                            S.mm(pa[:, 0:w], [(wa[:, kc, j * 128:(j + 1) * 128], xn[:, kc, xo:xo + w]) for kc in range(KC)],
                                 reads=[wk] + xn_keys(xo, xb), writes=[pak])
                            S.mm(pb[:, 0:w], [(wa[:, kc, 256 + j * 128:256 + (j + 1) * 128], xn[:, kc, xo:xo + w]) for kc in range(KC)],
                                 reads=[wk] + xn_keys(xo, xb), writes=[pbk])
                            sa, sak = tmp("t32")
                            S.op("act", lambda e: e.activation(out=sa[:, 0:w], in_=pa[:, 0:w], func=AF.Silu), reads=[pak], writes=[sak])
                            S.op("dve", lambda e: e.tensor_tensor(out=u_ff[:, n, xo:xo + w], in0=sa[:, 0:w], in1=pb[:, 0:w], op=ALU.mult),
                                 reads=[sak, pbk], writes=[("u", n, xo)])
                for g in range(4):
                    if g == 2 and i + 1 < len(sbs):
                        mod_sb(i + 1)
                    wb = wB[g % 2]; wbk = ("wB", g % 2)
                    S.dma("sp", wb, w2s[l][f][g], reads=[("w2s", l, f, g)], writes=[wbk])
                    for j in range(2):
                        m = 2 * g + j
                        for t, xo in zip(sb, offs):
                            w = t[2]; col = 2 if t[0] == "ctx" else bcol
                            py, pyk = ps_rot()
                            S.mm(py[:, 0:w], [(wb[:, n, j * 128:(j + 1) * 128], u_ff[:, n, xo:xo + w]) for n in range(NFF)],
                                 reads=[wbk] + [("u", n, xo) for n in range(NFF)], writes=[pyk])
                            S.op("dve", lambda e: e.scalar_tensor_tensor(out=hap(t, m), in0=py[:, 0:w], scalar=Gmod[:, l, s, m, col:col + 1],
                                                                         in1=hap(t, m), op0=ALU.mult, op1=ALU.add),
                                 reads=[pyk, ("Gmod", l), hkey(t, m)], writes=[hkey(t, m)])

        def gelu_to(out_ap, ps_ap, pskey, outkey, npart, w):
            S.op("act", lambda e: e.activation(out=out_ap, in_=ps_ap, func=AF.Gelu_apprx_tanh), reads=[pskey], writes=[outkey])

        def qk_norm_rope(ps, pk, w, gain_col, out_ap, outkey, rope, tok0):
            sq, sk = tmp("sq")
            S.op("act", lambda e: e.activation(out=sq[:, 0:w], in_=ps[:, 0:w], func=AF.Square), reads=[pk], writes=[sk])
            p2, p2k = ps_rot()
            S.mm(p2[:, 0:w], [(bdones, sq[:, 0:w])], reads=[sk, ("bdones",)], writes=[p2k])
            rr, rk = rstd_from_ps(p2[:, 0:w], p2k, w, 1.0 / 64)
            if not rope:
                S.op("dve", lambda e: e.scalar_tensor_tensor(out=out_ap, in0=ps[:, 0:w], scalar=vecs[:, gain_col:gain_col + 1],
                                                             in1=rr[:, 0:w], op0=ALU.mult, op1=ALU.mult),
                     reads=[pk, rk, ("vecs",)], writes=[outkey])
                return
            i = 0
            kg = kg32[i]; kk = ("kg", i)
            S.op("dve", lambda e: e.scalar_tensor_tensor(out=kg[:, 0:w], in0=ps[:, 0:w], scalar=vecs[:, gain_col:gain_col + 1],
                                                         in1=rr[:, 0:w], op0=ALU.mult, op1=ALU.mult),
                 reads=[pk, rk, ("vecs",)], writes=[kk])
            w1, w1k = tmp("t32")
            S.op(aux[0], lambda e: e.tensor_tensor(out=w1[:, 0:w], in0=rr[:, 0:w], in1=sinb[:, tok0:tok0 + w], op=ALU.mult),
                 reads=[rk, ("sin",)], writes=[w1k])
            t1, t1k = tmp("t32")
            gsw = 208 + 2 * (gain_col // VL) + (1 if gain_col % VL == V_KN else 0)
            for qd in range(4):
                o = qd * 32; so = (qd ^ 1) * 32
                S.op("dve", lambda e: e.scalar_tensor_tensor(out=t1[o:o + 32, 0:w], in0=ps[so:so + 32, 0:w], scalar=vecs[o:o + 32, gsw:gsw + 1],
                                                             in1=w1[o:o + 32, 0:w], op0=ALU.mult, op1=ALU.mult),
                     reads=[pk, w1k, ("vecs",)], writes=[t1k])
            S.op(aux[0], lambda e: e.tensor_tensor(out=kg[:, 0:w], in0=kg[:, 0:w], in1=cosb[:, tok0:tok0 + w], op=ALU.mult),
                 reads=[kk, ("cos",)], writes=[kk])
            S.op(aux[0], lambda e: e.tensor_tensor(out=out_ap, in0=kg[:, 0:w], in1=t1[:, 0:w], op=ALU.add),
                 reads=[kk, t1k], writes=[outkey])

        def koff(t):
            return t[1] if t[0] == "lat" else L + t[1]

        def pass1_all(l, bcol, tiles):
            kps_of = {}

            def p1_mod(t):
                modulate(t, l, 1, bcol, 0)

            def p1_proj(t):
                st, t0, w = t
                ko = koff(t)
                wa, wk = wa_next()
                S.dma("sp", wa[:, :, 0:256], wins[l][0][:, :, 0:256], reads=[("wins", l, 0)], writes=[wk])
                kps = []
                for j in range(2):
                    ps, pk = PS[6 + j], ("ps", 6 + j)
                    S.mm(ps[:, 0:w], [(wa[:, kc, j * 128:(j + 1) * 128], xn[:, kc, 0:w]) for kc in range(KC)],
                         reads=[wk] + xn_keys(0), writes=[pk])
                    kps.append((ps, pk))
                kps_of[t] = kps
                wa, wk = wa_next()
                S.dma("sp", wa[:, :, 0:384], wins[l][1][:, :, 0:384], reads=[("wins", l, 1)], writes=[wk])
                for si in range(w // 128):
                    gs = (ko + si * 128) // 128
                    ps, pk = ps_rot()
                    S.mm(ps[:, 0:384], [(xn[:, kc, si * 128:(si + 1) * 128], wa[:, kc, 0:384]) for kc in range(KC)],
                         reads=[wk] + xn_keys(0), writes=[pk])
                    S.op("act", lambda e: e.activation(out=Vaug[:, gs, :, 0:64], in_=ps[:, 0:128].rearrange("p (k n) -> p k n", k=2),
                                                       func=AF.Copy), reads=[pk], writes=[("V", gs)])
                    S.op("act", lambda e: e.activation(out=xpool[:, gs, :], in_=ps[:, 128:384], func=AF.Copy), reads=[pk], writes=[("xpool", gs)])

            def p1_chain(t):
                st, t0, w = t
                ko = koff(t)
                kps = kps_of[t]
                for j in range(2):
                    qk_norm_rope(kps[j][0], kps[j][1], w, l * VL + V_KN, kT[:, j, ko:ko + w], ("kT", j, ko), st != "ctx", t0)

            p1_mod(tiles[0]); p1_proj(tiles[0])
            for k in range(1, len(tiles)):
                p1_mod(tiles[k])
                p1_chain(tiles[k - 1])
                p1_proj(tiles[k])
            p1_chain(tiles[-1])

        def sgu_items(l, t):
            st, t0, w = t
            hold = {}

            def item_u():
                wa, wk = wa_next()
                hold["wa"] = (wa, wk)
                S.dma("sp", wa, wins[l][2], reads=[("wins", l, 2)], writes=[wk])
                for j in range(2):
                    ps, pk = ps_rot()
                    S.mm(ps[:, 0:w], [(wa[:, kc, j * 128:(j + 1) * 128], xn[:, kc, 0:w]) for kc in range(KC)],
                         reads=[wk] + xn_keys(0), writes=[pk])
                    gelu_to(uT[:, j, 0:w], ps[:, 0:w], pk, ("uT", j), 128, w)

            def item_v(si):
                wa, wk = hold["wa"]
                ps, pk = ps_rot()
                S.mm(ps[:, 0:256], [(xn[:, kc, si * 128:(si + 1) * 128], wa[:, kc, 256:512]) for kc in range(KC)],
                     reads=[wk] + xn_keys(0), writes=[pk])
                i = cnt["vn"] % 2; cnt["vn"] += 1
                g_ = gv[i]; gk = ("gv", i); v_ = vn[i]; vk = ("vn", i)
                gelu_to(g_, ps[:, 0:256], pk, gk, 128, 256)
                sq, sk = tmp("sq")
                sm = small[:, (cnt["sm"] % 8) * 2:(cnt["sm"] % 8) * 2 + 1]; smk = ("sm", cnt["sm"] % 8); cnt["sm"] += 1
                S.op("act", lambda e: e.activation(out=sq[:, 0:256], in_=g_, func=AF.Square, accum_out=sm),
                     reads=[gk], writes=[sk, smk])
                S.op("act", lambda e: e.activation(out=sm, in_=sm, func=AF.Sqrt, bias=epsc[:, 0:1], scale=1.0 / 256),
                     reads=[smk, ("epsc",)], writes=[smk])
                S.op("dve", lambda e: e.reciprocal(out=sm, in_=sm), reads=[smk], writes=[smk])
                S.op("dve", lambda e: e.scalar_tensor_tensor(out=v_, in0=g_, scalar=sm, in1=sgun[:, l, :], op0=ALU.mult, op1=ALU.mult),
                     reads=[gk, smk, ("sgun",)], writes=[vk])
                p2, p2k = ps_rot()
                for gi in range(4):
                    S.mm(p2[(gi % 2) * 64:(gi % 2) * 64 + 64, (gi // 2) * 128:(gi // 2) * 128 + 128],
                         [(v_[:, gi * 64:(gi + 1) * 64], sguwT[:, l, gi, :])], reads=[vk, ("sguwT",)], writes=[p2k])
                for c in range(2):
                    tt, tk = tmp("t32")
                    S.op("dve", lambda e: e.tensor_tensor(out=tt[:, 0:128], in0=p2[:, c * 128:(c + 1) * 128], in1=sgub[:, l, c, :], op=ALU.add),
                         reads=[p2k, ("sgub",)], writes=[tk])
                    us = uT[:, c, si * 128:(si + 1) * 128]
                    S.op("dve", lambda e: e.tensor_tensor(out=us, in0=tt[:, 0:128], in1=us, op=ALU.mult),
                         reads=[tk, ("uT", c)], writes=[("uT", c)])

            return [item_u] + [(lambda si=si: item_v(si)) for si in range(w // 128)]

        def pass2(l, bcol, t):
            st, t0, w = t
            ko = koff(t)
            isctx = st == "ctx"
            col = 2 if isctx else bcol
            modulate(t, l, 1, bcol, 0)
            wa, wk = wa_next()
            S.dma("sp", wa, wins[l][3], reads=[("wins", l, 3)], writes=[wk])
            for j in range(4):
                ps, pk = ps_rot()
                S.mm(ps[:, 0:w], [(wa[:, kc, j * 128:(j + 1) * 128], xn[:, kc, 0:w]) for kc in range(KC)],
                     reads=[wk] + xn_keys(0), writes=[pk])
                qk_norm_rope(ps, pk, w, l * VL + V_QN, qT[:, j, 0:w], ("qT", j), not isctx, t0)
            side = sgu_items(l, t)
            nblk = 2 if isctx else 16
            sbase = 16 if isctx else 0

            def pool_item(c):
                pp, ppk = ps_rot()
                for tb in range(w // 128):
                    i = t0 // 128 + tb
                    for half in range(2):
                        g = 2 * c + half
                        srcs = []
                        if i > 0:
                            srcs.append((i - 1, 3))
                        srcs.append((i, 0 if i == 0 else (2 if i == nblk - 1 else 1)))
                        if i < nblk - 1:
                            srcs.append((i + 1, 4))
                        S.mm(pp[half * 64:half * 64 + 64, tb * 128:(tb + 1) * 128],
                             [(xpool[:, sbase + si, c * 128 + half * 64:c * 128 + half * 64 + 64], bands[:, g * 5 + kind, :]) for si, kind in srcs],
                             reads=[("xpool", sbase + si) for si, _ in srcs] + [("bands",)], writes=[ppk])
                S.op("act", lambda e: e.activation(out=pooled[:, 0:w], in_=pp[:, 0:w], func=AF.Copy), reads=[ppk], writes=[("pooled",)])
                po_, pok = ps_rot()
                S.mm(po_[:, 0:w], [(poolbd[:, l, c, :], pooled[:, 0:w])], reads=[("pooled",), ("poolbd",)], writes=[pok])
                pc = l * VL + V_PSC + c
                S.op("act", lambda e: e.activation(out=poolout[:, c, 0:w], in_=po_[:, 0:w], func=AF.Copy, scale=vecs[:, pc:pc + 1]),
                     reads=[pok, ("vecs",)], writes=[("poolout", c)])

            kslices = list(range(16, 18)) if isctx else list(range(NKS))
            steps = [(h, gs) for h in range(8) for gs in kslices]
            pend = []

            def kkeys(kv, gs):
                o = gs * 128
                for tt in _tiles():
                    k0 = koff(tt)
                    if k0 <= o < k0 + tt[2]:
                        return ("kT", kv, k0)
                raise AssertionError

            def emit_S(h, gs):
                ps, pk = ps_rot()
                pr = slice((h % 2) * 64, (h % 2) * 64 + 64)
                S.mm(ps[:, 0:w], [(kT[pr, h // 4, gs * 128:(gs + 1) * 128], qT[pr, h // 2, 0:w])],
                     reads=[kkeys(h // 4, gs), ("qT", h // 2)], writes=[pk])
                i = cnt["pt"] % 4; cnt["pt"] += 1
                pt = pTb[i]; ptk = ("pT", i)
                S.op("act", lambda e: e.activation(out=pt[:, 0:w], in_=ps[:, 0:w], func=AF.Exp, scale=0.125), reads=[pk], writes=[ptk])
                return pt, ptk

            def emit_PV(h, gs, pt, ptk):
                pv = PS[6 + (h % 2)]; pvk = ("ps", 6 + (h % 2))
                S.mm(pv[:, 0:w], [(Vaug[:, gs, h // 4, :], pt[:, 0:w])], reads=[("V", gs), ("Vones",), ptk], writes=[pvk],
                     start=(gs == kslices[0]), stop=(gs == kslices[-1]))
                if gs == kslices[-1]:
                    rd, rdk = tmp("rr")
                    S.op("dve", lambda e: e.reciprocal(out=rd[64:128, 0:w], in_=pv[64:128, 0:w]), reads=[pvk], writes=[rdk])
                    po = (h % 2) * 64
                    S.op("dve", lambda e: e.tensor_tensor(out=attnT[po:po + 64, h // 2, 0:w], in0=pv[0:64, 0:w], in1=rd[64:128, 0:w], op=ALU.mult),
                         reads=[pvk, rdk], writes=[("attnT", h // 2, h % 2)])

            nblk = 2 if isctx else 16
            sbase = 16 if isctx else 0
            for c in range(2):
                side.append(lambda c=c: pool_item(c))
            LA = 2
            for i, (h, gs) in enumerate(steps):
                pend.append((h, gs) + emit_S(h, gs))
                if len(pend) > LA:
                    emit_PV(*pend.pop(0))
                if gs == kslices[-1] and side and h >= 1:
                    side.pop(0)()
            while pend:
                emit_PV(*pend.pop(0))
            while side:
                side.pop(0)()
            utk = [("uT", 0), ("uT", 1)]
            for m in range(8):
                if t[1] >= 1280 or isctx:
                    run_bg()
                wa, wk = wa_next()
                S.dma("sp", wa, wins[l][4 + m], reads=[("wins", l, 4 + m)], writes=[wk])
                gts = []
                for gi in range(3):
                    ps, pk = ps_rot()
                    S.mm(ps[:, 0:w], [(wa[:, kc, gi * 128:(gi + 1) * 128], xn[:, kc, 0:w]) for kc in range(KC)],
                         reads=[wk] + xn_keys(0), writes=[pk])
                    i = cnt["gate"] % 6; cnt["gate"] += 1
                    S.op("act", lambda e: e.activation(out=gates[i][:, 0:w], in_=ps[:, 0:w], func=AF.Sigmoid), reads=[pk], writes=[("gate", i)])
                    gts.append((gates[i], ("gate", i)))
                brs = [([(wa[:, kc, 384:512], poolout[:, kc, 0:w]) for kc in range(2)], [("poolout", 0), ("poolout", 1)]),
                       ([(wa[:, 2 + kc, 384:512], attnT[:, kc, 0:w]) for kc in range(4)], [("attnT", kc, hh) for kc in range(4) for hh in range(2)]),
                       ([(wa[:, 6 + kc, 384:512], uT[:, kc, 0:w]) for kc in range(2)], utk)]
                prods = []
                for gi in range(3):
                    ps, pk = ps_rot()
                    S.mm(ps[:, 0:w], brs[gi][0], reads=[wk] + brs[gi][1], writes=[pk])
                    g_, gk = gts[gi]
                    pr_, prk = tmp("t32")
                    S.op("dve", lambda e: e.tensor_tensor(out=pr_[:, 0:w], in0=g_[:, 0:w], in1=ps[:, 0:w], op=ALU.mult),
                         reads=[gk, pk], writes=[prk])
                    prods.append((pr_, prk))
                    if gi == 1:
                        S.op(aux[0], lambda e: e.tensor_tensor(out=prods[0][0][:, 0:w], in0=prods[0][0][:, 0:w], in1=prods[1][0][:, 0:w], op=ALU.add),
                             reads=[prods[0][1], prods[1][1]], writes=[prods[0][1]])
                S.op(aux[0], lambda e: e.tensor_tensor(out=merged[:, m, 0:w], in0=prods[0][0][:, 0:w], in1=prods[2][0][:, 0:w], op=ALU.add),
                     reads=[prods[0][1], prods[2][1]], writes=[("merged", m)])
            for og in range(2):
                wa, wk = wa_next()
                S.dma("sp", wa, wouts[l][og], reads=[("wouts", l, og)], writes=[wk])
                for j in range(4):
                    m = og * 4 + j
                    ps, pk = ps_rot()
                    S.mm(ps[:, 0:w], [(wa[:, kc, j * 128:(j + 1) * 128], merged[:, kc, 0:w]) for kc in range(KC)],
                         reads=[wk] + [("merged", kc) for kc in range(KC)], writes=[pk])
                    S.op("dve", lambda e: e.scalar_tensor_tensor(out=hap(t, m), in0=ps[:, 0:w], scalar=Gmod[:, l, 1, m, col:col + 1],
                                                                 in1=hap(t, m), op0=ALU.mult, op1=ALU.add),
                         reads=[pk, ("Gmod", l), hkey(t, m)], writes=[hkey(t, m)])

        epsc = A.alloc(4 * 4, F32)
        S.op("dve", lambda e: e.memset(epsc, EPS), writes=[("epsc",)])
        S.op("act", lambda e: e.activation(out=scT, in_=cTs, func=AF.Silu), reads=[("cTs",)], writes=[("scT",)])
        l0 = layers[0]
        precast_ffn(l0, 0)
        for g in range(18):
            adaln_piece(l0, g)
        adaln_finish(l0)
        precast_mix(l0)
        precast_ffn(l0, 1)
        for l in layers[1:]:
            precast_ffn(l, 0)
            precast_mix(l)
            precast_ffn(l, 1)
            for g in range(18):
                bg.append(lambda l=l, g=g: adaln_piece(l, g))
            bg.append(lambda l=l: adaln_finish(l))

        for b in range(nb):
            for kc in range(KC):
                ks = [hkey(t, kc) for t in _tiles(False)]
                S.dma("sp", hT[:, kc, :], xT[b][:, kc, :], writes=ks)
            S.dma("sp", hcT, ctxT[b], writes=[hkey(_tiles()[-1], kc) for kc in range(KC)])
            for l in layers:
                last = (l == 1)
                aux[0] = "dve" if (b == 0 and l == layers[0]) else "pool"
                if l != layers[0]:
                    run_bg(len(bg))
                if "ffn1" in stages:
                    ffn(l, 0, b, _tiles(True))
                if "mix" in stages:
                    S.barrier()
                    S.op("dve", lambda e: e.memset(Vaug[:, :, :, 64:128], 1.0), writes=[("Vones",)])
                    pass1_all(l, b, _tiles(True))
                    for t in (_tiles(not last) if p2_tiles is None else [_tiles()[i] for i in p2_tiles]):
                        pass2(l, b, t)
                    S.barrier()
                if "ffn2" in stages:
                    ffn(l, 1, b, _tiles(not last))
            if final:
                for t in _tiles(False):
                    st, t0, w = t
                    ps, pk = ps_rot()
                    for kc in range(KC):
                        sq, sk = tmp("sq")
                        S.op("act", lambda e: e.activation(out=sq[:, 0:w], in_=hap(t, kc), func=AF.Square), reads=[hkey(t, kc)], writes=[sk])
                        S.mm(ps[:, 0:w], [(ones, sq[:, 0:w])], reads=[sk, ("ones",)], writes=[pk], start=(kc == 0), stop=(kc == KC - 1))
                    rr, rk = rstd_from_ps(ps[:, 0:w], pk, w, 1.0 / D)
                    for kc in range(KC):
                        S.op("dve", lambda e: e.scalar_tensor_tensor(out=hap(t, kc), in0=hap(t, kc), scalar=vecs[:, V_FN + kc:V_FN + kc + 1],
                                                                     in1=rr[:, 0:w], op0=ALU.mult, op1=ALU.mult),
                             reads=[hkey(t, kc), rk, ("vecs",)], writes=[hkey(t, kc)])
            for t in _tiles(False):
                st, t0, w = t
                S.dma("sp", outT[b][:, :, t0:t0 + w], hT[:, :, t0:t0 + w], reads=[hkey(t, kc) for kc in range(KC)])
            S.dma("sp", hcout[b], hcT, reads=[hkey(_tiles()[-1], kc) for kc in range(KC)])
        S.drain("sp")
    return nc


def _fm(v):
    return np.ascontiguousarray(v.reshape(-1, 128).T)


def _qk_perm():
    old = np.zeros(64, np.int64)
    for r in range(2):
        for a in range(2):
            for i in range(16):
                old[r * 32 + a * 16 + i] = a * 32 + r * 16 + i
    return old


def _rope_tables():
    half = 32
    inv = (10000.0 ** (-np.arange(0, half, 2, dtype=np.float32) / half)).astype(np.float32)
    t = np.arange(L)
    row = (t // 64).astype(np.float32); col = (t % 64).astype(np.float32)
    cos = np.zeros((128, L), np.float32); sin = np.zeros((128, L), np.float32)
    for p in range(128):
        pp = p % 64
        r = pp // 32; a = (pp % 32) // 16; i = pp % 16
        ang = ((row if a == 0 else col) * inv[i]).astype(np.float32)
        cos[p] = np.cos(ang)
        sin[p] = -np.sin(ang) if r == 0 else np.sin(ang)
    return cos, sin, np.zeros((128, 128), np.float32)


def _bands():
    out = np.zeros((128, 20, 128), np.float32)
    n = 384
    for g, w in enumerate((2, 4, 8, 16)):
        hw = w // 2
        M = np.zeros((n, n), np.float64)
        for t in range(n):
            lo = max(t - hw, 0); hi = min(t + hw, n)
            M[lo:hi, t] = 1.0 / (hi - lo)
            M[t, t] -= 1.0
        out[:, g * 5 + 0] = M[0:128, 0:128]
        out[:, g * 5 + 1] = M[128:256, 128:256]
        out[:, g * 5 + 2] = M[256:384, 256:384]
        out[:, g * 5 + 3] = M[0:128, 128:256]
        out[:, g * 5 + 4] = M[128:256, 0:128]
    return out


def _host_prep(inp, core, nb=NB):
    f = lambda a: np.ascontiguousarray(np.asarray(a, dtype=np.float32))
    bs = [core * nb + i for i in range(nb)]
    m = {}
    m["xT"] = np.stack([f(inp["x"][b].T.reshape(KC, 128, L).transpose(1, 0, 2)) for b in bs])
    m["ctxT"] = np.stack([f(inp["ctx"][b].T.reshape(KC, 128, C).transpose(1, 0, 2)) for b in bs])
    cols = [inp["c"][b] for b in bs] + [inp["c_ctx"]]
    while len(cols) < 3:
        cols.insert(-1, cols[0])
    m["cT"] = f(np.stack([_fm(np.asarray(v)) for v in cols], axis=2))
    return m


def _shared_prep(inp):
    f = lambda a: np.ascontiguousarray(np.asarray(a, dtype=np.float32))
    vecs = np.zeros((128, NV), np.float32)
    for l in range(2):
        o = l * VL
        vecs[:, o + V_N1:o + V_N1 + 8] = _fm(inp["norm_ffn1"][l])
        vecs[:, o + V_NM:o + V_NM + 8] = _fm(inp["norm_mix"][l])
        vecs[:, o + V_N2:o + V_N2 + 8] = _fm(inp["norm_ffn2"][l])
        vecs[:, o + V_BADA:o + V_BADA + 72] = _fm(inp["b_ada"][l])
        vecs[:, o + V_PSC:o + V_PSC + 2] = _fm(inp["pool_scale"][l])
        vecs[:, o + V_QN] = np.tile(inp["q_norm"][l][_qk_perm()], 2)
        vecs[:, o + V_KN] = np.tile(inp["k_norm"][l][_qk_perm()], 2)
        sw = np.arange(128) ^ 32
        vecs[:, 208 + 2 * l] = vecs[sw, o + V_QN]
        vecs[:, 209 + 2 * l] = vecs[sw, o + V_KN]
    vecs[:, V_FN:V_FN + 8] = _fm(inp["final_norm"])
    m = {"vecs": vecs}
    for k in ("w_ada", "ffn1_w13", "ffn2_w13", "ffn1_w2", "ffn2_w2", "w_in", "w_br_pool", "w_br_attn", "w_br_sgu", "w_out"):
        m[k] = f(inp[k])
    pbd = np.zeros((2, 128, 2, 128), np.float32)
    for l in range(2):
        for g in range(4):
            c, hh = g // 2, g % 2
            pbd[l, hh * 64:(hh + 1) * 64, c, hh * 64:(hh + 1) * 64] = inp["pool_w"][l, g]
    m["poolbd"] = pbd
    m["sguwT"] = f(np.transpose(inp["sgu_w"], (0, 3, 1, 2)))
    m["sgun"] = f(np.broadcast_to(inp["sgu_norm"][:, None, :], (2, 128, 256)))
    sb = np.zeros((2, 128, 2, 128), np.float32)
    for l in range(2):
        for g in range(4):
            c, hh = g // 2, g % 2
            sb[l, hh * 64:(hh + 1) * 64, c, :] = inp["sgu_b"][l, g][None, :]
    m["sgub"] = sb
    cos, sin, perm = _rope_tables()
    m["ropecos"] = cos; m["ropesin"] = sin; m["ropeperm"] = perm
    m["bands"] = _bands()
    return m


_NC_CACHE = {}


def kernel(**inputs):
    inp = {k: np.asarray(v) for k, v in inputs.items()}
    if "full" not in _NC_CACHE:
        _NC_CACHE["full"] = build()
    nc = _NC_CACHE["full"]
    shared = _shared_prep(inp)
    in_maps = []
    for core in range(NCORES):
        m = dict(shared)
        m.update(_host_prep(inp, core))
        in_maps.append(m)
    res = run_bass_kernel_spmd(nc, in_maps, core_ids=list(range(NCORES)))
    out = np.empty((NCORES * NB, L, D), np.float32)
    for core in range(NCORES):
        o = res.results[core]["outT"]
        for i in range(NB):
            out[core * NB + i] = o[i].transpose(1, 0, 2).reshape(D, L).T
    return out
```

```python
import contextlib
import numpy as np
import concourse.bass as bass
import concourse.mybir as mybir
from concourse.bass_utils import run_bass_kernel_spmd

F32 = mybir.dt.float32
BF16 = mybir.dt.bfloat16
ALU = mybir.AluOpType
AF = mybir.ActivationFunctionType

D = 1024; L = 2048; C = 256; DFF = 2816; NFF = 22; KC = 8
NB = 2
NCORES = 8
EPS = 1e-6
NV = 212
VL = 100
V_N1, V_NM, V_N2, V_BADA, V_PSC, V_QN, V_KN = 0, 8, 16, 24, 96, 98, 99
V_FN = 200
NKS = 18
GELU_K = 1.5957691216057308


DEBUG = {}


class _Buf:
    __slots__ = ("w", "r")

    def __init__(self):
        self.w = None
        self.r = {}


class Sched:
    NDMA = 8

    def __init__(self, nc, es):
        self.nc = nc; self.es = es
        self.E = {"pe": nc.tensor, "act": nc.scalar, "dve": nc.vector, "pool": nc.gpsimd, "sp": nc.sync}
        self.sem = {}; self.cnt = {}; self.nsem = 0
        self.seen = {e: {} for e in self.E}
        self.allsems = {}
        for e in self.E:
            self._newsem(e)
        self.bufs = {}
        self.dq = {}; self.dqi = {}

    def _mk(self, name):
        self.nsem += 1
        s = self.es.enter_context(self.nc.semaphore(f"{name}_{self.nsem}"))
        return s

    def _newsem(self, e):
        self.sem[e] = self._mk("s" + e); self.cnt[e] = 0

    def buf(self, k):
        b = self.bufs.get(k)
        if b is None:
            b = self.bufs[k] = _Buf()
        return b

    def _wait(self, e, tok):
        sem, val = tok[0], tok[1]
        d = self.seen[e]
        if d.get(id(sem), 0) >= val:
            return
        self.E[e].wait_ge(sem, val)
        d[id(sem)] = val

    def _deps(self, e, reads, writes):
        for k in reads:
            b = self.buf(k)
            if b.w is not None and not (b.w[2] == e and e == "pe"):
                self._wait(e, b.w)
        for k in writes:
            b = self.buf(k)
            if b.w is not None and not (b.w[2] == e and e == "pe"):
                self._wait(e, b.w)
            for rk, t in b.r.items():
                if t[2] == e and e == "pe":
                    continue
                self._wait(e, t)

    def _commit(self, tok, reads, writes):
        rk = tok[2] if tok[2] != "dma" else id(tok[0])
        for k in reads:
            self.buf(k).r[rk] = tok
        for k in writes:
            b = self.buf(k); b.w = tok; b.r = {}

    def _tok(self, e, ins):
        if self.cnt[e] >= 4000:
            self._newsem(e)
        self.cnt[e] += 1
        ins.then_inc(self.sem[e], 1)
        tok = (self.sem[e], self.cnt[e], e)
        self.allsems[id(tok[0])] = tok
        return tok

    def op(self, e, fn, reads=(), writes=()):
        self._deps(e, reads, writes)
        ins = fn(self.E[e])
        self._commit(self._tok(e, ins), reads, writes)

    def mm(self, out, pairs, reads, writes, start=True, stop=True):
        self._deps("pe", reads, writes)
        n = len(pairs); ins = None
        for i, (lt, rh) in enumerate(pairs):
            ins = self.nc.tensor.matmul(out, lt, rh, start=(start and i == 0), stop=(stop and i == n - 1))
        self._commit(self._tok("pe", ins), reads, writes)

    def dma(self, q, out, in_, reads=(), writes=()):
        pool = self.dq.setdefault(q, [])
        i = self.dqi.get(q, 0); self.dqi[q] = i + 1
        if len(pool) < (16 if q == "pool" else self.NDMA):
            pool.append([self._mk("d" + q), 0])
        ent = pool[i % (16 if q == "pool" else self.NDMA)]
        if ent[1] > 0:
            self._wait(q, (ent[0], ent[1]))
        self._deps(q, reads, writes)
        ins = self.E[q].dma_start(out=out, in_=in_)
        ent[1] += 16
        ins.then_inc(ent[0], 16)
        tok = (ent[0], ent[1], "dma")
        self.allsems[id(ent[0])] = tok
        self._commit(tok, reads, writes)

    def barrier(self):
        skip = set(id(x[0]) for x in self.dq.get("pool", []))
        for e in self.E:
            for t in list(self.allsems.values()):
                if t[2] == e or id(t[0]) in skip:
                    continue
                self._wait(e, t)

    def drain(self, e):
        for t in list(self.allsems.values()):
            if t[2] == "dma":
                self._wait(e, t)


class Arena:
    def __init__(self, nc, es, nbytes):
        self.t = es.enter_context(nc.sbuf_tensor("arena", [128, nbytes // 2], BF16))
        self.cap = nbytes; self.off = 0

    def alloc(self, nbytes, dt=BF16, at=None, name=None):
        rb = (nbytes + 127) // 128 * 128
        if name:
            DEBUG[name] = (self.off if at is None else at, nbytes)
        if at is None:
            at = self.off; self.off += rb
        assert at + rb <= self.cap, (at, rb, self.cap)
        ap = self.t[:, at // 2:(at + nbytes) // 2]
        if dt == F32:
            ap = ap.bitcast(F32)
        return ap

    def mark(self):
        return self.off

    def reset(self, m):
        self.off = m


def _tiles(include_ctx=True):
    t = [("lat", 0, 512), ("lat", 512, 256), ("lat", 768, 512), ("lat", 1280, 256), ("lat", 1536, 512)]
    if include_ctx:
        t.append(("ctx", 0, 256))
    return t


def build(nb=NB, layers=(0, 1), stages=("ffn1", "mix", "ffn2"), final=True, p2_tiles=None):
    nc = bass.Bass("TRN2", target_bir_lowering=False)

    def din(name, shape, dt=F32):
        return nc.dram_tensor(name, list(shape), dt, kind="ExternalInput").ap()

    def dscr(name, shape, dt=BF16):
        return nc.dram_tensor(name, list(shape), dt, kind="Internal").ap()

    xT = din("xT", [nb, 128, KC, L]); ctxT = din("ctxT", [nb, 128, KC, C])
    cT = din("cT", [128, KC, 3]); vecs_d = din("vecs", [128, NV])
    w_ada = din("w_ada", [2, D, 9 * D])
    w13_d = [din("ffn1_w13", [2, D, 2 * DFF]), din("ffn2_w13", [2, D, 2 * DFF])]
    w2_d = [din("ffn1_w2", [2, DFF, D]), din("ffn2_w2", [2, DFF, D])]
    w_in = din("w_in", [2, D, 4608])
    w_brp = din("w_br_pool", [2, 256, D]); w_bra = din("w_br_attn", [2, 512, D]); w_brs = din("w_br_sgu", [2, 256, D])
    w_out = din("w_out", [2, D, D])
    poolbd_d = din("poolbd", [2, 128, 2, 128]); sguwT_d = din("sguwT", [2, 128, 4, 128])
    sgun_d = din("sgun", [2, 128, 256]); sgub_d = din("sgub", [2, 128, 2, 128])
    cos_d = din("ropecos", [128, L]); sin_d = din("ropesin", [128, L]); perm_d = din("ropeperm", [128, 128])
    band_d = din("bands", [128, 20, 128])
    outT = nc.dram_tensor("outT", [nb, 128, KC, L], F32, kind="ExternalOutput").ap()
    hcout = nc.dram_tensor("hcout", [nb, 128, KC, C], F32, kind="ExternalOutput").ap()

    w13s = [[dscr(f"w13s_{l}_{f}", [11, 128, KC, 512]) for f in range(2)] for l in range(2)]
    w2s = [[dscr(f"w2s_{l}_{f}", [4, 128, NFF, 256]) for f in range(2)] for l in range(2)]
    wins = [dscr(f"wins_{l}", [12, 128, KC, 512]) for l in range(2)]
    wouts = [dscr(f"wouts_{l}", [2, 128, KC, 512]) for l in range(2)]

    with contextlib.ExitStack() as es:
        S = Sched(nc, es)
        A = Arena(nc, es, 207 * 1024)
        PS = [es.enter_context(nc.psum_tensor(f"ps{i}", [128, 512], F32)) for i in range(8)]
        psi = [0]

        def ps_rot():
            i = psi[0] % 6; psi[0] += 1
            return PS[i], ("ps", i)

        hT = A.alloc(KC * L * 4, F32).rearrange("p (k n) -> p k n", k=KC)
        hcT = A.alloc(KC * C * 4, F32).rearrange("p (k n) -> p k n", k=KC)
        cosb = A.alloc(L * 2); sinb = A.alloc(L * 2)
        perm = A.alloc(128 * 4, F32)
        vecs = A.alloc(NV * 4, F32)
        ones = A.alloc(128 * 2); bdones = A.alloc(128 * 2)
        bands = A.alloc(20 * 128 * 2).rearrange("p (k n) -> p k n", k=20)
        poolbd = A.alloc(2 * 2 * 128 * 2).rearrange("p (l c n) -> p l c n", l=2, c=2)
        sguwT = A.alloc(2 * 4 * 128 * 2).rearrange("p (l g n) -> p l g n", l=2, g=4)
        sgun = A.alloc(2 * 256 * 4, F32).rearrange("p (l n) -> p l n", l=2)
        sgub = A.alloc(2 * 2 * 128 * 4, F32).rearrange("p (l c n) -> p l c n", l=2, c=2)
        cTs = A.alloc(KC * 3 * 4, F32).rearrange("p (k n) -> p k n", k=KC)
        scT = A.alloc(KC * 3 * 2).rearrange("p (k n) -> p k n", k=KC)
        modT = A.alloc(2 * 72 * 3 * 4, F32).rearrange("p (l j n) -> p l j n", l=2, j=72)
        Amod = A.alloc(2 * 3 * KC * 3 * 4, F32).rearrange("p (l s k n) -> p l s k n", l=2, s=3, k=KC)
        Gmod = A.alloc(2 * 3 * KC * 3 * 4, F32).rearrange("p (l s k n) -> p l s k n", l=2, s=3, k=KC)
        NWA = 2
        wA = [A.alloc(KC * 512 * 2).rearrange("p (k n) -> p k n", k=KC) for _ in range(NWA)]
        wai = [0]

        def wa_next():
            i = wai[0] % NWA; wai[0] += 1
            return wA[i], ("wA", i)

        sqb = [A.alloc(512 * 2) for _ in range(2)]
        t32 = [A.alloc(512 * 4, F32) for _ in range(4)]
        rrb = [A.alloc(512 * 4, F32) for _ in range(2)]
        tci = {"sq": 0, "t32": 0, "rr": 0}

        def tmp(kind):
            lst = {"sq": sqb, "t32": t32, "rr": rrb}[kind]
            i = tci[kind] % len(lst); tci[kind] += 1
            return lst[i], (kind, i)

        xn = A.alloc(KC * 768 * 2, name="xn").rearrange("p (k n) -> p k n", k=KC)
        xnb = [xn, None]
        phase0 = A.mark()
        xn2 = A.alloc(KC * 768 * 2).rearrange("p (k n) -> p k n", k=KC)
        xnb[1] = xn2
        u_ff = A.alloc(NFF * 768 * 2).rearrange("p (k n) -> p k n", k=NFF)
        wB = [A.alloc(NFF * 256 * 2).rearrange("p (k n) -> p k n", k=NFF) for _ in range(2)]
        ffn_end = A.mark()
        A.reset(phase0)
        kT = A.alloc(2 * NKS * 128 * 2, name="kT").rearrange("p (k n) -> p k n", k=2)
        Vaug = A.alloc(NKS * 2 * 128 * 2, name="Vaug").rearrange("p (s k n) -> p s k n", s=NKS, k=2)
        xpool = A.alloc(NKS * 256 * 2, name="xpool").rearrange("p (s n) -> p s n", s=NKS)
        uT = A.alloc(2 * 512 * 2, name="uT").rearrange("p (k n) -> p k n", k=2)
        qT = A.alloc(4 * 512 * 2, name="qT").rearrange("p (k n) -> p k n", k=4)
        attnT = A.alloc(4 * 512 * 2, name="attnT").rearrange("p (k n) -> p k n", k=4)
        pTb = [A.alloc(512 * 2) for _ in range(4)]
        pooled = A.alloc(512 * 2)
        poolout = A.alloc(2 * 512 * 2, name="poolout").rearrange("p (k n) -> p k n", k=2)
        gates = [A.alloc(512 * 2) for _ in range(6)]
        merged = A.alloc(KC * 512 * 2, name="merged").rearrange("p (k n) -> p k n", k=KC)
        xnb.append(merged)
        kg32 = [A.alloc(512 * 4, F32) for _ in range(1)]
        vn = [A.alloc(256 * 2) for _ in range(2)]
        gv = [A.alloc(256 * 4, F32) for _ in range(2)]
        small = A.alloc(64 * 4, F32)
        mix_end = A.mark()
        A.reset(max(ffn_end, mix_end))
        print("SBUF bytes/partition: ffn_end", ffn_end, "mix_end", mix_end)

        cnt = {"pt": 0, "gate": 0, "kg": 0, "vn": 0, "sm": 0}

        S.dma("sp", vecs, vecs_d, writes=[("vecs",)])
        S.dma("sp", cTs, cT, writes=[("cTs",)])
        S.dma("sp", perm, perm_d, writes=[("perm",)])
        S.dma("sp", sgun, sgun_d.rearrange("l p n -> p l n"), writes=[("sgun",)])
        S.dma("sp", sgub, sgub_d.rearrange("l p c n -> p l c n"), writes=[("sgub",)])
        S.dma("pool", cosb, cos_d, writes=[("cos",)])
        S.dma("pool", sinb, sin_d, writes=[("sin",)])
        S.dma("pool", bands, band_d, writes=[("bands",)])
        S.dma("pool", poolbd, poolbd_d.rearrange("l p c n -> p l c n"), writes=[("poolbd",)])
        S.dma("pool", sguwT, sguwT_d.rearrange("l p g n -> p l g n"), writes=[("sguwT",)])
        S.op("dve", lambda e: e.memset(ones, 1.0), writes=[("ones",)])
        S.op("dve", lambda e: e.memset(bdones, 0.0), writes=[("bdones",)])
        S.op("dve", lambda e: e.memset(bdones[0:64, 0:64], 1.0), writes=[("bdones",)])
        S.op("dve", lambda e: e.memset(bdones[64:128, 64:128], 1.0), writes=[("bdones",)])

        def precast_ffn(l, f):
            src = w13_d[f][l].rearrange("(k p) n -> p k n", p=128)
            for g in range(11):
                S.dma("pool", w13s[l][f][g][:, :, 0:256], src[:, :, g * 256:(g + 1) * 256], writes=[("w13s", l, f, g)])
                S.dma("pool", w13s[l][f][g][:, :, 256:512], src[:, :, DFF + g * 256:DFF + (g + 1) * 256],
                      writes=[("w13s", l, f, g)])
            src2 = w2_d[f][l].rearrange("(k p) n -> p k n", p=128)
            for g in range(4):
                S.dma("pool", w2s[l][f][g], src2[:, :, g * 256:(g + 1) * 256], writes=[("w2s", l, f, g)])

        def precast_mix(l):
            src = w_in[l].rearrange("(k p) n -> p k n", p=128)
            W = wins[l]
            k0 = ("wins", l, 0)
            for kv in range(2):
                for dup in range(2):
                    for r in range(2):
                        d0 = kv * 128 + dup * 64 + r * 32
                        s0 = 512 + kv * 64 + r * 16
                        for a in range(2):
                            S.dma("pool", W[0][:, :, d0 + a * 16:d0 + a * 16 + 16], src[:, :, s0 + a * 32:s0 + a * 32 + 16], writes=[k0])
            S.dma("pool", W[1][:, :, 0:128], src[:, :, 640:768], writes=[("wins", l, 1)])
            S.dma("pool", W[1][:, :, 128:384], src[:, :, 768:1024], writes=[("wins", l, 1)])
            S.dma("pool", W[2], src[:, :, 1024:1536], writes=[("wins", l, 2)])
            for h in range(8):
                for r in range(2):
                    d0 = h * 64 + r * 32
                    s0 = h * 64 + r * 16
                    for a in range(2):
                        S.dma("pool", W[3][:, :, d0 + a * 16:d0 + a * 16 + 16], src[:, :, s0 + a * 32:s0 + a * 32 + 16], writes=[("wins", l, 3)])
            brp = w_brp[l].rearrange("(k p) n -> p k n", p=128)
            bra = w_bra[l].rearrange("(k p) n -> p k n", p=128)
            brs = w_brs[l].rearrange("(k p) n -> p k n", p=128)
            for m in range(8):
                key = ("wins", l, 4 + m)
                for gi in range(3):
                    S.dma("pool", W[4 + m][:, :, gi * 128:(gi + 1) * 128],
                          src[:, :, 1536 + gi * 1024 + m * 128: 1536 + gi * 1024 + (m + 1) * 128], writes=[key])
                S.dma("pool", W[4 + m][:, 0:2, 384:512], brp[:, :, m * 128:(m + 1) * 128], writes=[key])
                S.dma("pool", W[4 + m][:, 2:6, 384:512], bra[:, :, m * 128:(m + 1) * 128], writes=[key])
                S.dma("pool", W[4 + m][:, 6:8, 384:512], brs[:, :, m * 128:(m + 1) * 128], writes=[key])
            so = w_out[l].rearrange("(k p) n -> p k n", p=128)
            for og in range(2):
                S.dma("pool", wouts[l][og], so[:, :, og * 512:(og + 1) * 512], writes=[("wouts", l, og)])

        def adaln_piece(l, g):
            src = w_ada[l].rearrange("(k p) n -> p k n", p=128)
            wa, wk = wa_next()
            S.dma("pool", wa, src[:, :, g * 512:(g + 1) * 512], writes=[wk])
            pm, pmk = ps_rot()
            for j in range(4):
                S.mm(pm[:, j * 3:(j + 1) * 3], [(wa[:, kc, j * 128:(j + 1) * 128], scT[:, kc, :]) for kc in range(KC)],
                     reads=[wk, ("scT",)], writes=[pmk])
            bo = l * VL + V_BADA + g * 4
            S.op("dve", lambda e: e.tensor_tensor(out=modT[:, l, g * 4:(g + 1) * 4, :], in0=pm[:, 0:12].rearrange("p (j n) -> p j n", n=3),
                                                  in1=vecs[:, bo:bo + 4].unsqueeze(2).to_broadcast([128, 4, 3]), op=ALU.add),
                 reads=[pmk, ("vecs",)], writes=[("modT", l)])

        def adaln_finish(l):
            for s in range(3):
                no = l * VL + (V_N1, V_NM, V_N2)[s]
                S.op("dve", lambda e: e.scalar_tensor_tensor(
                    out=Amod[:, l, s], in0=modT[:, l, (3 * s + 1) * 8:(3 * s + 2) * 8, :], scalar=1.0,
                    in1=vecs[:, no:no + 8].unsqueeze(2).to_broadcast([128, 8, 3]), op0=ALU.add, op1=ALU.mult),
                    reads=[("modT", l), ("vecs",)], writes=[("Amod", l)])
                gs = 1.0 if s == 1 else 0.5
                S.op("dve", lambda e: e.tensor_scalar(out=Gmod[:, l, s], in0=modT[:, l, (3 * s + 2) * 8:(3 * s + 3) * 8, :],
                                                      scalar1=gs, scalar2=None, op0=ALU.mult),
                     reads=[("modT", l)], writes=[("Gmod", l)])

        bg = []
        aux = ["dve"]

        def run_bg(n=1):
            for _ in range(n):
                if bg:
                    bg.pop(0)()

        def hap(t, kc):
            st, t0, w = t
            return (hT if st == "lat" else hcT)[:, kc, t0:t0 + w]

        def hkey(t, kc):
            return ("h", t[0], t[1], kc)

        def rstd_from_ps(ps_ap, pskey, w, scale, npart=128):
            sr, srk = tmp("rr")
            S.op("act", lambda e: e.activation(out=sr[:, 0:w], in_=ps_ap, func=AF.Ln, bias=epsc[:, 0:1], scale=scale),
                 reads=[pskey, ("epsc",)], writes=[srk])
            S.op("act", lambda e: e.activation(out=sr[:, 0:w], in_=sr[:, 0:w], func=AF.Exp, scale=-0.5), reads=[srk], writes=[srk])
            return sr, srk

        def modulate(t, l, s, bcol, xoff, xb=0):
            st, t0, w = t
            xn = xnb[xb]
            col = 2 if st == "ctx" else bcol
            ps, pk = ps_rot()
            for kc in range(KC):
                S.op("act", lambda e: e.activation(out=xn[:, kc, xoff:xoff + w], in_=hap(t, kc), func=AF.Square),
                     reads=[hkey(t, kc)], writes=[("xn", xb, kc, xoff)])
            for kc in range(KC):
                S.mm(ps[:, 0:w], [(ones, xn[:, kc, xoff:xoff + w])], reads=[("xn", xb, kc, xoff), ("ones",)], writes=[pk],
                     start=(kc == 0), stop=(kc == KC - 1))
            rr, rk = rstd_from_ps(ps[:, 0:w], pk, w, 1.0 / D)
            for kc in range(KC):
                tt, tk = tmp("t32")
                S.op("dve", lambda e: e.scalar_tensor_tensor(out=tt[:, 0:w], in0=hap(t, kc), scalar=Amod[:, l, s, kc, col:col + 1],
                                                             in1=rr[:, 0:w], op0=ALU.mult, op1=ALU.mult),
                     reads=[hkey(t, kc), rk, ("Amod", l)], writes=[tk])
                S.op("act", lambda e: e.activation(out=xn[:, kc, xoff:xoff + w], in_=tt[:, 0:w], func=AF.Identity,
                                                   bias=modT[:, l, 3 * s * 8 + kc, col:col + 1], scale=1.0),
                     reads=[tk, ("modT", l)], writes=[("xn", xb, kc, xoff)])

        def xn_keys(xoff, xb=0):
            return [("xn", xb, kc, xoff) for kc in range(KC)]

        def ffn(l, f, bcol, tiles):
            s = 0 if f == 0 else 2
            sbs = [tiles[i:i + 2] for i in range(0, len(tiles), 2)]

            def offs_of(sb):
                offs = []; o = 0
                for t in sb:
                    offs.append(o); o += t[2]
                return offs

            def mod_sb(i):
                for t, xo in zip(sbs[i], offs_of(sbs[i])):
                    modulate(t, l, s, bcol, xo, xb=i % 2)

            mod_sb(0)
            for i, sb in enumerate(sbs):
                offs = offs_of(sb)
                xb = i % 2
                xn = xnb[xb]
                for g in range(11):
                    wa, wk = wa_next()
                    S.dma("sp", wa, w13s[l][f][g], reads=[("w13s", l, f, g)], writes=[wk])
                    for j in range(2):
                        n = 2 * g + j
                        for t, xo in zip(sb, offs):
                            w = t[2]
                            pa, pak = ps_rot(); pb, pbk = ps_rot()
                            S.mm(pa[:, 0:w], [(wa[:, kc, j * 128:(j + 1) * 128], xn[:, kc, xo:xo + w]) for kc in range(KC)],
                                 reads=[wk] + xn_keys(xo, xb), writes=[pak])
                            S.mm(pb[:, 0:w], [(wa[:, kc, 256 + j * 128:256 + (j + 1) * 128], xn[:, kc, xo:xo + w]) for kc in range(KC)],
                                 reads=[wk] + xn_keys(xo, xb), writes=[pbk])
                            sa, sak = tmp("t32")
                            S.op("act", lambda e: e.activation(out=sa[:, 0:w], in_=pa[:, 0:w], func=AF.Silu), reads=[pak], writes=[sak])
                            S.op("dve", lambda e: e.tensor_tensor(out=u_ff[:, n, xo:xo + w], in0=sa[:, 0:w], in1=pb[:, 0:w], op=ALU.mult),
                                 reads=[sak, pbk], writes=[("u", n, xo)])
                for g in range(4):
                    if g == 2 and i + 1 < len(sbs):
                        mod_sb(i + 1)
                    wb = wB[g % 2]; wbk = ("wB", g % 2)
                    S.dma("sp", wb, w2s[l][f][g], reads=[("w2s", l, f, g)], writes=[wbk])
                    for j in range(2):
                        m = 2 * g + j
                        for t, xo in zip(sb, offs):
                            w = t[2]; col = 2 if t[0] == "ctx" else bcol
                            py, pyk = ps_rot()
                            S.mm(py[:, 0:w], [(wb[:, n, j * 128:(j + 1) * 128], u_ff[:, n, xo:xo + w]) for n in range(NFF)],
                                 reads=[wbk] + [("u", n, xo) for n in range(NFF)], writes=[pyk])
                            S.op("dve", lambda e: e.scalar_tensor_tensor(out=hap(t, m), in0=py[:, 0:w], scalar=Gmod[:, l, s, m, col:col + 1],
                                                                         in1=hap(t, m), op0=ALU.mult, op1=ALU.add),
                                 reads=[pyk, ("Gmod", l), hkey(t, m)], writes=[hkey(t, m)])

        def gelu_to(out_ap, ps_ap, pskey, outkey, npart, w):
            S.op("act", lambda e: e.activation(out=out_ap, in_=ps_ap, func=AF.Gelu_apprx_tanh), reads=[pskey], writes=[outkey])

        def qk_norm_rope(ps, pk, w, gain_col, out_ap, outkey, rope, tok0):
            sq, sk = tmp("sq")
            S.op("act", lambda e: e.activation(out=sq[:, 0:w], in_=ps[:, 0:w], func=AF.Square), reads=[pk], writes=[sk])
            p2, p2k = ps_rot()
            S.mm(p2[:, 0:w], [(bdones, sq[:, 0:w])], reads=[sk, ("bdones",)], writes=[p2k])
            rr, rk = rstd_from_ps(p2[:, 0:w], p2k, w, 1.0 / 64)
            if not rope:
                S.op("dve", lambda e: e.scalar_tensor_tensor(out=out_ap, in0=ps[:, 0:w], scalar=vecs[:, gain_col:gain_col + 1],
                                                             in1=rr[:, 0:w], op0=ALU.mult, op1=ALU.mult),
                     reads=[pk, rk, ("vecs",)], writes=[outkey])
                return
            i = 0
            kg = kg32[i]; kk = ("kg", i)
            S.op("dve", lambda e: e.scalar_tensor_tensor(out=kg[:, 0:w], in0=ps[:, 0:w], scalar=vecs[:, gain_col:gain_col + 1],
                                                         in1=rr[:, 0:w], op0=ALU.mult, op1=ALU.mult),
                 reads=[pk, rk, ("vecs",)], writes=[kk])
            w1, w1k = tmp("t32")
            S.op(aux[0], lambda e: e.tensor_tensor(out=w1[:, 0:w], in0=rr[:, 0:w], in1=sinb[:, tok0:tok0 + w], op=ALU.mult),
                 reads=[rk, ("sin",)], writes=[w1k])
            t1, t1k = tmp("t32")
            gsw = 208 + 2 * (gain_col // VL) + (1 if gain_col % VL == V_KN else 0)
            for qd in range(4):
                o = qd * 32; so = (qd ^ 1) * 32
                S.op("dve", lambda e: e.scalar_tensor_tensor(out=t1[o:o + 32, 0:w], in0=ps[so:so + 32, 0:w], scalar=vecs[o:o + 32, gsw:gsw + 1],
                                                             in1=w1[o:o + 32, 0:w], op0=ALU.mult, op1=ALU.mult),
                     reads=[pk, w1k, ("vecs",)], writes=[t1k])
            S.op(aux[0], lambda e: e.tensor_tensor(out=kg[:, 0:w], in0=kg[:, 0:w], in1=cosb[:, tok0:tok0 + w], op=ALU.mult),
                 reads=[kk, ("cos",)], writes=[kk])
            S.op(aux[0], lambda e: e.tensor_tensor(out=out_ap, in0=kg[:, 0:w], in1=t1[:, 0:w], op=ALU.add),
                 reads=[kk, t1k], writes=[outkey])

        def koff(t):
            return t[1] if t[0] == "lat" else L + t[1]

        def pass1_all(l, bcol, tiles):
            kps_of = {}

            xb_of = {}

            def p1_mod(t):
                xb_of[t] = 0 if len(xb_of) % 2 == 0 else 2
                modulate(t, l, 1, bcol, 0, xb=xb_of[t])

            def p1_proj(t):
                st, t0, w = t
                ko = koff(t)
                xb = xb_of[t]
                xn = xnb[xb]
                wa, wk = wa_next()
                S.dma("sp", wa[:, :, 0:256], wins[l][0][:, :, 0:256], reads=[("wins", l, 0)], writes=[wk])
                kps = []
                for j in range(2):
                    ps, pk = PS[6 + j], ("ps", 6 + j)
                    S.mm(ps[:, 0:w], [(wa[:, kc, j * 128:(j + 1) * 128], xn[:, kc, 0:w]) for kc in range(KC)],
                         reads=[wk] + xn_keys(0, xb), writes=[pk])
                    kps.append((ps, pk))
                kps_of[t] = kps
                wa, wk = wa_next()
                S.dma("sp", wa[:, :, 0:384], wins[l][1][:, :, 0:384], reads=[("wins", l, 1)], writes=[wk])
                for si in range(w // 128):
                    gs = (ko + si * 128) // 128
                    ps, pk = ps_rot()
                    S.mm(ps[:, 0:384], [(xn[:, kc, si * 128:(si + 1) * 128], wa[:, kc, 0:384]) for kc in range(KC)],
                         reads=[wk] + xn_keys(0, xb), writes=[pk])
                    S.op("act", lambda e: e.activation(out=Vaug[:, gs, :, 0:64], in_=ps[:, 0:128].rearrange("p (k n) -> p k n", k=2),
                                                       func=AF.Copy), reads=[pk], writes=[("V", gs)])
                    S.op("act", lambda e: e.activation(out=xpool[:, gs, :], in_=ps[:, 128:384], func=AF.Copy), reads=[pk], writes=[("xpool", gs)])

            def p1_chain(t):
                st, t0, w = t
                ko = koff(t)
                kps = kps_of[t]
                for j in range(2):
                    qk_norm_rope(kps[j][0], kps[j][1], w, l * VL + V_KN, kT[:, j, ko:ko + w], ("kT", j, ko), st != "ctx", t0)

            p1_mod(tiles[0]); p1_proj(tiles[0])
            for k in range(1, len(tiles)):
                p1_mod(tiles[k])
                p1_chain(tiles[k - 1])
                p1_proj(tiles[k])
            p1_chain(tiles[-1])

        def sgu_items(l, t):
            st, t0, w = t
            hold = {}

            def item_u():
                wa, wk = wa_next()
                hold["wa"] = (wa, wk)
                S.dma("sp", wa, wins[l][2], reads=[("wins", l, 2)], writes=[wk])
                for j in range(2):
                    ps, pk = ps_rot()
                    S.mm(ps[:, 0:w], [(wa[:, kc, j * 128:(j + 1) * 128], xn[:, kc, 0:w]) for kc in range(KC)],
                         reads=[wk] + xn_keys(0), writes=[pk])
                    gelu_to(uT[:, j, 0:w], ps[:, 0:w], pk, ("uT", j), 128, w)

            def item_v(si):
                wa, wk = hold["wa"]
                ps, pk = ps_rot()
                S.mm(ps[:, 0:256], [(xn[:, kc, si * 128:(si + 1) * 128], wa[:, kc, 256:512]) for kc in range(KC)],
                     reads=[wk] + xn_keys(0), writes=[pk])
                i = cnt["vn"] % 2; cnt["vn"] += 1
                g_ = gv[i]; gk = ("gv", i); v_ = vn[i]; vk = ("vn", i)
                gelu_to(g_, ps[:, 0:256], pk, gk, 128, 256)
                sq, sk = tmp("sq")
                sm = small[:, (cnt["sm"] % 8) * 2:(cnt["sm"] % 8) * 2 + 1]; smk = ("sm", cnt["sm"] % 8); cnt["sm"] += 1
                S.op("act", lambda e: e.activation(out=sq[:, 0:256], in_=g_, func=AF.Square, accum_out=sm),
                     reads=[gk], writes=[sk, smk])
                S.op("act", lambda e: e.activation(out=sm, in_=sm, func=AF.Sqrt, bias=epsc[:, 0:1], scale=1.0 / 256),
                     reads=[smk, ("epsc",)], writes=[smk])
                S.op("dve", lambda e: e.reciprocal(out=sm, in_=sm), reads=[smk], writes=[smk])
                S.op("dve", lambda e: e.scalar_tensor_tensor(out=v_, in0=g_, scalar=sm, in1=sgun[:, l, :], op0=ALU.mult, op1=ALU.mult),
                     reads=[gk, smk, ("sgun",)], writes=[vk])
                p2, p2k = ps_rot()
                for gi in range(4):
                    S.mm(p2[(gi % 2) * 64:(gi % 2) * 64 + 64, (gi // 2) * 128:(gi // 2) * 128 + 128],
                         [(v_[:, gi * 64:(gi + 1) * 64], sguwT[:, l, gi, :])], reads=[vk, ("sguwT",)], writes=[p2k])
                for c in range(2):
                    tt, tk = tmp("t32")
                    S.op("dve", lambda e: e.tensor_tensor(out=tt[:, 0:128], in0=p2[:, c * 128:(c + 1) * 128], in1=sgub[:, l, c, :], op=ALU.add),
                         reads=[p2k, ("sgub",)], writes=[tk])
                    us = uT[:, c, si * 128:(si + 1) * 128]
                    S.op("dve", lambda e: e.tensor_tensor(out=us, in0=tt[:, 0:128], in1=us, op=ALU.mult),
                         reads=[tk, ("uT", c)], writes=[("uT", c)])

            return [item_u] + [(lambda si=si: item_v(si)) for si in range(w // 128)]

        def pass2(l, bcol, t):
            st, t0, w = t
            ko = koff(t)
            isctx = st == "ctx"
            col = 2 if isctx else bcol
            modulate(t, l, 1, bcol, 0)
            wa, wk = wa_next()
            S.dma("sp", wa, wins[l][3], reads=[("wins", l, 3)], writes=[wk])
            for j in range(4):
                ps, pk = ps_rot()
                S.mm(ps[:, 0:w], [(wa[:, kc, j * 128:(j + 1) * 128], xn[:, kc, 0:w]) for kc in range(KC)],
                     reads=[wk] + xn_keys(0), writes=[pk])
                qk_norm_rope(ps, pk, w, l * VL + V_QN, qT[:, j, 0:w], ("qT", j), not isctx, t0)
            side = sgu_items(l, t)
            nblk = 2 if isctx else 16
            sbase = 16 if isctx else 0

            def pool_item(c):
                pp, ppk = ps_rot()
                for tb in range(w // 128):
                    i = t0 // 128 + tb
                    for half in range(2):
                        g = 2 * c + half
                        srcs = []
                        if i > 0:
                            srcs.append((i - 1, 3))
                        srcs.append((i, 0 if i == 0 else (2 if i == nblk - 1 else 1)))
                        if i < nblk - 1:
                            srcs.append((i + 1, 4))
                        S.mm(pp[half * 64:half * 64 + 64, tb * 128:(tb + 1) * 128],
                             [(xpool[:, sbase + si, c * 128 + half * 64:c * 128 + half * 64 + 64], bands[:, g * 5 + kind, :]) for si, kind in srcs],
                             reads=[("xpool", sbase + si) for si, _ in srcs] + [("bands",)], writes=[ppk])
                S.op("act", lambda e: e.activation(out=pooled[:, 0:w], in_=pp[:, 0:w], func=AF.Copy), reads=[ppk], writes=[("pooled",)])
                po_, pok = ps_rot()
                S.mm(po_[:, 0:w], [(poolbd[:, l, c, :], pooled[:, 0:w])], reads=[("pooled",), ("poolbd",)], writes=[pok])
                pc = l * VL + V_PSC + c
                S.op("act", lambda e: e.activation(out=poolout[:, c, 0:w], in_=po_[:, 0:w], func=AF.Copy, scale=vecs[:, pc:pc + 1]),
                     reads=[pok, ("vecs",)], writes=[("poolout", c)])

            kslices = list(range(16, 18)) if isctx else list(range(NKS))
            steps = [(h, gs) for h in range(8) for gs in kslices]
            pend = []

            def kkeys(kv, gs):
                o = gs * 128
                for tt in _tiles():
                    k0 = koff(tt)
                    if k0 <= o < k0 + tt[2]:
                        return ("kT", kv, k0)
                raise AssertionError

            def emit_S(h, gs):
                ps, pk = ps_rot()
                pr = slice((h % 2) * 64, (h % 2) * 64 + 64)
                S.mm(ps[:, 0:w], [(kT[pr, h // 4, gs * 128:(gs + 1) * 128], qT[pr, h // 2, 0:w])],
                     reads=[kkeys(h // 4, gs), ("qT", h // 2)], writes=[pk])
                i = cnt["pt"] % 4; cnt["pt"] += 1
                pt = pTb[i]; ptk = ("pT", i)
                S.op("act", lambda e: e.activation(out=pt[:, 0:w], in_=ps[:, 0:w], func=AF.Exp, scale=0.125), reads=[pk], writes=[ptk])
                return pt, ptk

            def emit_PV(h, gs, pt, ptk):
                pv = PS[6 + (h % 2)]; pvk = ("ps", 6 + (h % 2))
                S.mm(pv[:, 0:w], [(Vaug[:, gs, h // 4, :], pt[:, 0:w])], reads=[("V", gs), ("Vones",), ptk], writes=[pvk],
                     start=(gs == kslices[0]), stop=(gs == kslices[-1]))
                if gs == kslices[-1]:
                    rd, rdk = tmp("rr")
                    S.op("dve", lambda e: e.reciprocal(out=rd[64:128, 0:w], in_=pv[64:128, 0:w]), reads=[pvk], writes=[rdk])
                    po = (h % 2) * 64
                    S.op("dve", lambda e: e.tensor_tensor(out=attnT[po:po + 64, h // 2, 0:w], in0=pv[0:64, 0:w], in1=rd[64:128, 0:w], op=ALU.mult),
                         reads=[pvk, rdk], writes=[("attnT", h // 2, h % 2)])

            nblk = 2 if isctx else 16
            sbase = 16 if isctx else 0
            for c in range(2):
                side.append(lambda c=c: pool_item(c))
            LA = 2
            for i, (h, gs) in enumerate(steps):
                pend.append((h, gs) + emit_S(h, gs))
                if len(pend) > LA:
                    emit_PV(*pend.pop(0))
                if gs == kslices[-1] and side and h >= 1:
                    side.pop(0)()
            while pend:
                emit_PV(*pend.pop(0))
            while side:
                side.pop(0)()
            utk = [("uT", 0), ("uT", 1)]
            for m in range(8):
                if t[1] >= 1280 or isctx:
                    run_bg()
                wa, wk = wa_next()
                S.dma("sp", wa, wins[l][4 + m], reads=[("wins", l, 4 + m)], writes=[wk])
                gts = []
                for gi in range(3):
                    ps, pk = ps_rot()
                    S.mm(ps[:, 0:w], [(wa[:, kc, gi * 128:(gi + 1) * 128], xn[:, kc, 0:w]) for kc in range(KC)],
                         reads=[wk] + xn_keys(0), writes=[pk])
                    i = cnt["gate"] % 6; cnt["gate"] += 1
                    S.op("act", lambda e: e.activation(out=gates[i][:, 0:w], in_=ps[:, 0:w], func=AF.Sigmoid), reads=[pk], writes=[("gate", i)])
                    gts.append((gates[i], ("gate", i)))
                brs = [([(wa[:, kc, 384:512], poolout[:, kc, 0:w]) for kc in range(2)], [("poolout", 0), ("poolout", 1)]),
                       ([(wa[:, 2 + kc, 384:512], attnT[:, kc, 0:w]) for kc in range(4)], [("attnT", kc, hh) for kc in range(4) for hh in range(2)]),
                       ([(wa[:, 6 + kc, 384:512], uT[:, kc, 0:w]) for kc in range(2)], utk)]
                prods = []
                for gi in range(3):
                    ps, pk = ps_rot()
                    S.mm(ps[:, 0:w], brs[gi][0], reads=[wk] + brs[gi][1], writes=[pk])
                    g_, gk = gts[gi]
                    pr_, prk = tmp("t32")
                    S.op("dve", lambda e: e.tensor_tensor(out=pr_[:, 0:w], in0=g_[:, 0:w], in1=ps[:, 0:w], op=ALU.mult),
                         reads=[gk, pk], writes=[prk])
                    prods.append((pr_, prk))
                    if gi == 1:
                        S.op(aux[0], lambda e: e.tensor_tensor(out=prods[0][0][:, 0:w], in0=prods[0][0][:, 0:w], in1=prods[1][0][:, 0:w], op=ALU.add),
                             reads=[prods[0][1], prods[1][1]], writes=[prods[0][1]])
                S.op(aux[0], lambda e: e.tensor_tensor(out=merged[:, m, 0:w], in0=prods[0][0][:, 0:w], in1=prods[2][0][:, 0:w], op=ALU.add),
                     reads=[prods[0][1], prods[2][1]], writes=[("merged", m)])
            for og in range(2):
                wa, wk = wa_next()
                S.dma("sp", wa, wouts[l][og], reads=[("wouts", l, og)], writes=[wk])
                for j in range(4):
                    m = og * 4 + j
                    ps, pk = ps_rot()
                    S.mm(ps[:, 0:w], [(wa[:, kc, j * 128:(j + 1) * 128], merged[:, kc, 0:w]) for kc in range(KC)],
                         reads=[wk] + [("merged", kc) for kc in range(KC)], writes=[pk])
                    S.op("dve", lambda e: e.scalar_tensor_tensor(out=hap(t, m), in0=ps[:, 0:w], scalar=Gmod[:, l, 1, m, col:col + 1],
                                                                 in1=hap(t, m), op0=ALU.mult, op1=ALU.add),
                         reads=[pk, ("Gmod", l), hkey(t, m)], writes=[hkey(t, m)])

        epsc = A.alloc(4 * 4, F32)
        S.op("dve", lambda e: e.memset(epsc, EPS), writes=[("epsc",)])
        S.op("act", lambda e: e.activation(out=scT, in_=cTs, func=AF.Silu), reads=[("cTs",)], writes=[("scT",)])
        l0 = layers[0]
        precast_ffn(l0, 0)
        for g in range(18):
            adaln_piece(l0, g)
        adaln_finish(l0)
        precast_mix(l0)
        precast_ffn(l0, 1)
        for l in layers[1:]:
            precast_ffn(l, 0)
            precast_mix(l)
            precast_ffn(l, 1)
            for g in range(18):
                bg.append(lambda l=l, g=g: adaln_piece(l, g))
            bg.append(lambda l=l: adaln_finish(l))

        for b in range(nb):
            for kc in range(KC):
                ks = [hkey(t, kc) for t in _tiles(False)]
                S.dma("sp", hT[:, kc, :], xT[b][:, kc, :], writes=ks)
            S.dma("sp", hcT, ctxT[b], writes=[hkey(_tiles()[-1], kc) for kc in range(KC)])
            for l in layers:
                last = (l == 1)
                aux[0] = "dve" if (b == 0 and l == layers[0]) else "pool"
                if l != layers[0]:
                    run_bg(len(bg))
                if "ffn1" in stages:
                    ffn(l, 0, b, _tiles(True))
                if "mix" in stages:
                    S.barrier()
                    S.op("dve", lambda e: e.memset(Vaug[:, :, :, 64:128], 1.0), writes=[("Vones",)])
                    pass1_all(l, b, _tiles(True))
                    S.barrier()
                    for t in (_tiles(not last) if p2_tiles is None else [_tiles()[i] for i in p2_tiles]):
                        pass2(l, b, t)
                    S.barrier()
                if "ffn2" in stages:
                    ffn(l, 1, b, _tiles(not last))
            if final:
                for t in _tiles(False):
                    st, t0, w = t
                    ps, pk = ps_rot()
                    for kc in range(KC):
                        sq, sk = tmp("sq")
                        S.op("act", lambda e: e.activation(out=sq[:, 0:w], in_=hap(t, kc), func=AF.Square), reads=[hkey(t, kc)], writes=[sk])
                        S.mm(ps[:, 0:w], [(ones, sq[:, 0:w])], reads=[sk, ("ones",)], writes=[pk], start=(kc == 0), stop=(kc == KC - 1))
                    rr, rk = rstd_from_ps(ps[:, 0:w], pk, w, 1.0 / D)
                    for kc in range(KC):
                        S.op("dve", lambda e: e.scalar_tensor_tensor(out=hap(t, kc), in0=hap(t, kc), scalar=vecs[:, V_FN + kc:V_FN + kc + 1],
                                                                     in1=rr[:, 0:w], op0=ALU.mult, op1=ALU.mult),
                             reads=[hkey(t, kc), rk, ("vecs",)], writes=[hkey(t, kc)])
            for t in _tiles(False):
                st, t0, w = t
                S.dma("sp", outT[b][:, :, t0:t0 + w], hT[:, :, t0:t0 + w], reads=[hkey(t, kc) for kc in range(KC)])
            S.dma("sp", hcout[b], hcT, reads=[hkey(_tiles()[-1], kc) for kc in range(KC)])
        S.drain("sp")
    return nc


def _fm(v):
    return np.ascontiguousarray(v.reshape(-1, 128).T)


def _qk_perm():
    old = np.zeros(64, np.int64)
    for r in range(2):
        for a in range(2):
            for i in range(16):
                old[r * 32 + a * 16 + i] = a * 32 + r * 16 + i
    return old


def _rope_tables():
    half = 32
    inv = (10000.0 ** (-np.arange(0, half, 2, dtype=np.float32) / half)).astype(np.float32)
    t = np.arange(L)
    row = (t // 64).astype(np.float32); col = (t % 64).astype(np.float32)
    cos = np.zeros((128, L), np.float32); sin = np.zeros((128, L), np.float32)
    for p in range(128):
        pp = p % 64
        r = pp // 32; a = (pp % 32) // 16; i = pp % 16
        ang = ((row if a == 0 else col) * inv[i]).astype(np.float32)
        cos[p] = np.cos(ang)
        sin[p] = -np.sin(ang) if r == 0 else np.sin(ang)
    return cos, sin, np.zeros((128, 128), np.float32)


def _bands():
    out = np.zeros((128, 20, 128), np.float32)
    n = 384
    for g, w in enumerate((2, 4, 8, 16)):
        hw = w // 2
        M = np.zeros((n, n), np.float64)
        for t in range(n):
            lo = max(t - hw, 0); hi = min(t + hw, n)
            M[lo:hi, t] = 1.0 / (hi - lo)
            M[t, t] -= 1.0
        out[:, g * 5 + 0] = M[0:128, 0:128]
        out[:, g * 5 + 1] = M[128:256, 128:256]
        out[:, g * 5 + 2] = M[256:384, 256:384]
        out[:, g * 5 + 3] = M[0:128, 128:256]
        out[:, g * 5 + 4] = M[128:256, 0:128]
    return out


def _host_prep(inp, core, nb=NB):
    f = lambda a: np.ascontiguousarray(np.asarray(a, dtype=np.float32))
    bs = [core * nb + i for i in range(nb)]
    m = {}
    m["xT"] = np.stack([f(inp["x"][b].T.reshape(KC, 128, L).transpose(1, 0, 2)) for b in bs])
    m["ctxT"] = np.stack([f(inp["ctx"][b].T.reshape(KC, 128, C).transpose(1, 0, 2)) for b in bs])
    cols = [inp["c"][b] for b in bs] + [inp["c_ctx"]]
    while len(cols) < 3:
        cols.insert(-1, cols[0])
    m["cT"] = f(np.stack([_fm(np.asarray(v)) for v in cols], axis=2))
    return m


def _shared_prep(inp):
    f = lambda a: np.ascontiguousarray(np.asarray(a, dtype=np.float32))
    vecs = np.zeros((128, NV), np.float32)
    for l in range(2):
        o = l * VL
        vecs[:, o + V_N1:o + V_N1 + 8] = _fm(inp["norm_ffn1"][l])
        vecs[:, o + V_NM:o + V_NM + 8] = _fm(inp["norm_mix"][l])
        vecs[:, o + V_N2:o + V_N2 + 8] = _fm(inp["norm_ffn2"][l])
        vecs[:, o + V_BADA:o + V_BADA + 72] = _fm(inp["b_ada"][l])
        vecs[:, o + V_PSC:o + V_PSC + 2] = _fm(inp["pool_scale"][l])
        vecs[:, o + V_QN] = np.tile(inp["q_norm"][l][_qk_perm()], 2)
        vecs[:, o + V_KN] = np.tile(inp["k_norm"][l][_qk_perm()], 2)
        sw = np.arange(128) ^ 32
        vecs[:, 208 + 2 * l] = vecs[sw, o + V_QN]
        vecs[:, 209 + 2 * l] = vecs[sw, o + V_KN]
    vecs[:, V_FN:V_FN + 8] = _fm(inp["final_norm"])
    m = {"vecs": vecs}
    for k in ("w_ada", "ffn1_w13", "ffn2_w13", "ffn1_w2", "ffn2_w2", "w_in", "w_br_pool", "w_br_attn", "w_br_sgu", "w_out"):
        m[k] = f(inp[k])
    pbd = np.zeros((2, 128, 2, 128), np.float32)
    for l in range(2):
        for g in range(4):
            c, hh = g // 2, g % 2
            pbd[l, hh * 64:(hh + 1) * 64, c, hh * 64:(hh + 1) * 64] = inp["pool_w"][l, g]
    m["poolbd"] = pbd
    m["sguwT"] = f(np.transpose(inp["sgu_w"], (0, 3, 1, 2)))
    m["sgun"] = f(np.broadcast_to(inp["sgu_norm"][:, None, :], (2, 128, 256)))
    sb = np.zeros((2, 128, 2, 128), np.float32)
    for l in range(2):
        for g in range(4):
            c, hh = g // 2, g % 2
            sb[l, hh * 64:(hh + 1) * 64, c, :] = inp["sgu_b"][l, g][None, :]
    m["sgub"] = sb
    cos, sin, perm = _rope_tables()
    m["ropecos"] = cos; m["ropesin"] = sin; m["ropeperm"] = perm
    m["bands"] = _bands()
    return m


_NC_CACHE = {}


def kernel(**inputs):
    inp = {k: np.asarray(v) for k, v in inputs.items()}
    if "full" not in _NC_CACHE:
        _NC_CACHE["full"] = build()
    nc = _NC_CACHE["full"]
    shared = _shared_prep(inp)
    in_maps = []
    for core in range(NCORES):
        m = dict(shared)
        m.update(_host_prep(inp, core))
        in_maps.append(m)
    res = run_bass_kernel_spmd(nc, in_maps, core_ids=list(range(NCORES)))
    out = np.empty((NCORES * NB, L, D), np.float32)
    for core in range(NCORES):
        o = res.results[core]["outT"]
        for i in range(NB):
            out[core * NB + i] = o[i].transpose(1, 0, 2).reshape(D, L).T
    return out
```

```python
import contextlib
import numpy as np
import concourse.bass as bass
import concourse.mybir as mybir
from concourse.bass_utils import run_bass_kernel_spmd

F32 = mybir.dt.float32
BF16 = mybir.dt.bfloat16
ALU = mybir.AluOpType
AF = mybir.ActivationFunctionType

D = 1024; L = 2048; C = 256; DFF = 2816; NFF = 22; KC = 8
NB = 2
NCORES = 8
EPS = 1e-6
NV = 212
VL = 100
V_N1, V_NM, V_N2, V_BADA, V_PSC, V_QN, V_KN = 0, 8, 16, 24, 96, 98, 99
V_FN = 200
NKS = 18
GELU_K = 1.5957691216057308


DEBUG = {}


class _Buf:
    __slots__ = ("w", "r")

    def __init__(self):
        self.w = None
        self.r = {}


class Sched:
    NDMA = 8

    def __init__(self, nc, es):
        self.nc = nc; self.es = es
        self.E = {"pe": nc.tensor, "act": nc.scalar, "dve": nc.vector, "pool": nc.gpsimd, "sp": nc.sync}
        self.sem = {}; self.cnt = {}; self.nsem = 0
        self.seen = {e: {} for e in self.E}
        self.allsems = {}
        for e in self.E:
            self._newsem(e)
        self.bufs = {}
        self.dq = {}; self.dqi = {}

    def _mk(self, name):
        self.nsem += 1
        s = self.es.enter_context(self.nc.semaphore(f"{name}_{self.nsem}"))
        return s

    def _newsem(self, e):
        self.sem[e] = self._mk("s" + e); self.cnt[e] = 0

    def buf(self, k):
        b = self.bufs.get(k)
        if b is None:
            b = self.bufs[k] = _Buf()
        return b

    def _wait(self, e, tok):
        sem, val = tok[0], tok[1]
        d = self.seen[e]
        if d.get(id(sem), 0) >= val:
            return
        self.E[e].wait_ge(sem, val)
        d[id(sem)] = val

    def _deps(self, e, reads, writes):
        for k in reads:
            b = self.buf(k)
            if b.w is not None and not (b.w[2] == e and e == "pe"):
                self._wait(e, b.w)
        for k in writes:
            b = self.buf(k)
            if b.w is not None and not (b.w[2] == e and e == "pe"):
                self._wait(e, b.w)
            for rk, t in b.r.items():
                if t[2] == e and e == "pe":
                    continue
                self._wait(e, t)

    def _commit(self, tok, reads, writes):
        rk = tok[2] if tok[2] != "dma" else id(tok[0])
        for k in reads:
            self.buf(k).r[rk] = tok
        for k in writes:
            b = self.buf(k); b.w = tok; b.r = {}

    def _tok(self, e, ins):
        if self.cnt[e] >= 4000:
            self._newsem(e)
        self.cnt[e] += 1
        ins.then_inc(self.sem[e], 1)
        tok = (self.sem[e], self.cnt[e], e)
        self.allsems[id(tok[0])] = tok
        return tok

    def op(self, e, fn, reads=(), writes=()):
        self._deps(e, reads, writes)
        ins = fn(self.E[e])
        self._commit(self._tok(e, ins), reads, writes)

    def mm(self, out, pairs, reads, writes, start=True, stop=True):
        self._deps("pe", reads, writes)
        n = len(pairs); ins = None
        for i, (lt, rh) in enumerate(pairs):
            ins = self.nc.tensor.matmul(out, lt, rh, start=(start and i == 0), stop=(stop and i == n - 1))
        self._commit(self._tok("pe", ins), reads, writes)

    def dma(self, q, out, in_, reads=(), writes=()):
        pool = self.dq.setdefault(q, [])
        i = self.dqi.get(q, 0); self.dqi[q] = i + 1
        if len(pool) < (16 if q == "pool" else self.NDMA):
            pool.append([self._mk("d" + q), 0])
        ent = pool[i % (16 if q == "pool" else self.NDMA)]
        if ent[1] > 0:
            self._wait(q, (ent[0], ent[1]))
        self._deps(q, reads, writes)
        ins = self.E[q].dma_start(out=out, in_=in_)
        ent[1] += 16
        ins.then_inc(ent[0], 16)
        tok = (ent[0], ent[1], "dma")
        self.allsems[id(ent[0])] = tok
        self._commit(tok, reads, writes)

    def barrier(self):
        skip = set(id(x[0]) for x in self.dq.get("pool", []))
        for e in self.E:
            for t in list(self.allsems.values()):
                if t[2] == e or id(t[0]) in skip:
                    continue
                self._wait(e, t)

    def drain(self, e):
        for t in list(self.allsems.values()):
            if t[2] == "dma":
                self._wait(e, t)


class Arena:
    def __init__(self, nc, es, nbytes):
        self.t = es.enter_context(nc.sbuf_tensor("arena", [128, nbytes // 2], BF16))
        self.cap = nbytes; self.off = 0

    def alloc(self, nbytes, dt=BF16, at=None, name=None):
        rb = (nbytes + 127) // 128 * 128
        if name:
            DEBUG[name] = (self.off if at is None else at, nbytes)
        if at is None:
            at = self.off; self.off += rb
        assert at + rb <= self.cap, (at, rb, self.cap)
        ap = self.t[:, at // 2:(at + nbytes) // 2]
        if dt == F32:
            ap = ap.bitcast(F32)
        return ap

    def mark(self):
        return self.off

    def reset(self, m):
        self.off = m


def _tiles(include_ctx=True):
    t = [("lat", 0, 512), ("lat", 512, 256), ("lat", 768, 512), ("lat", 1280, 256), ("lat", 1536, 512)]
    if include_ctx:
        t.append(("ctx", 0, 256))
    return t


def build(nb=NB, layers=(0, 1), stages=("ffn1", "mix", "ffn2"), final=True, p2_tiles=None):
    nc = bass.Bass("TRN2", target_bir_lowering=False)

    def din(name, shape, dt=F32):
        return nc.dram_tensor(name, list(shape), dt, kind="ExternalInput").ap()

    def dscr(name, shape, dt=BF16):
        return nc.dram_tensor(name, list(shape), dt, kind="Internal").ap()

    xT = din("xT", [nb, 128, KC, L]); ctxT = din("ctxT", [nb, 128, KC, C])
    cT = din("cT", [128, KC, 3]); vecs_d = din("vecs", [128, NV])
    w_ada = din("w_ada", [2, D, 9 * D])
    w13_d = [din("ffn1_w13", [2, D, 2 * DFF]), din("ffn2_w13", [2, D, 2 * DFF])]
    w2_d = [din("ffn1_w2", [2, DFF, D]), din("ffn2_w2", [2, DFF, D])]
    w_in = din("w_in", [2, D, 4608])
    w_brp = din("w_br_pool", [2, 256, D]); w_bra = din("w_br_attn", [2, 512, D]); w_brs = din("w_br_sgu", [2, 256, D])
    w_out = din("w_out", [2, D, D])
    poolbd_d = din("poolbd", [2, 128, 2, 128]); sguwT_d = din("sguwT", [2, 128, 4, 128])
    sgun_d = din("sgun", [2, 128, 256]); sgub_d = din("sgub", [2, 128, 2, 128])
    cos_d = din("ropecos", [128, L]); sin_d = din("ropesin", [128, L]); perm_d = din("ropeperm", [128, 128])
    band_d = din("bands", [128, 20, 128])
    outT = nc.dram_tensor("outT", [nb, 128, KC, L], F32, kind="ExternalOutput").ap()
    hcout = nc.dram_tensor("hcout", [nb, 128, KC, C], F32, kind="ExternalOutput").ap()

    w13s = [[dscr(f"w13s_{l}_{f}", [11, 128, KC, 512]) for f in range(2)] for l in range(2)]
    w2s = [[dscr(f"w2s_{l}_{f}", [4, 128, NFF, 256]) for f in range(2)] for l in range(2)]
    wins = [dscr(f"wins_{l}", [12, 128, KC, 512]) for l in range(2)]
    wouts = [dscr(f"wouts_{l}", [2, 128, KC, 512]) for l in range(2)]

    with contextlib.ExitStack() as es:
        S = Sched(nc, es)
        A = Arena(nc, es, 207 * 1024)
        PS = [es.enter_context(nc.psum_tensor(f"ps{i}", [128, 512], F32)) for i in range(8)]
        psi = [0]

        def ps_rot():
            i = psi[0] % 6; psi[0] += 1
            return PS[i], ("ps", i)

        hT = A.alloc(KC * L * 4, F32).rearrange("p (k n) -> p k n", k=KC)
        hcT = A.alloc(KC * C * 4, F32).rearrange("p (k n) -> p k n", k=KC)
        cosb = A.alloc(L * 2); sinb = A.alloc(L * 2)
        perm = A.alloc(128 * 4, F32)
        vecs = A.alloc(NV * 4, F32)
        ones = A.alloc(128 * 2); bdones = A.alloc(128 * 2)
        bands = A.alloc(20 * 128 * 2).rearrange("p (k n) -> p k n", k=20)
        poolbd = A.alloc(2 * 2 * 128 * 2).rearrange("p (l c n) -> p l c n", l=2, c=2)
        sguwT = A.alloc(2 * 4 * 128 * 2).rearrange("p (l g n) -> p l g n", l=2, g=4)
        sgun = A.alloc(2 * 256 * 4, F32).rearrange("p (l n) -> p l n", l=2)
        sgub = A.alloc(2 * 2 * 128 * 4, F32).rearrange("p (l c n) -> p l c n", l=2, c=2)
        cTs = A.alloc(KC * 3 * 4, F32).rearrange("p (k n) -> p k n", k=KC)
        scT = A.alloc(KC * 3 * 2).rearrange("p (k n) -> p k n", k=KC)
        modT = A.alloc(2 * 72 * 3 * 4, F32).rearrange("p (l j n) -> p l j n", l=2, j=72)
        Amod = A.alloc(2 * 3 * KC * 3 * 4, F32).rearrange("p (l s k n) -> p l s k n", l=2, s=3, k=KC)
        Gmod = A.alloc(2 * 3 * KC * 3 * 4, F32).rearrange("p (l s k n) -> p l s k n", l=2, s=3, k=KC)
        NWA = 2
        wA = [A.alloc(KC * 512 * 2).rearrange("p (k n) -> p k n", k=KC) for _ in range(NWA)]
        wai = [0]

        def wa_next():
            i = wai[0] % NWA; wai[0] += 1
            return wA[i], ("wA", i)

        sqb = [A.alloc(512 * 2) for _ in range(2)]
        t32 = [A.alloc(512 * 4, F32) for _ in range(4)]
        rrb = [A.alloc(512 * 4, F32) for _ in range(2)]
        tci = {"sq": 0, "t32": 0, "rr": 0}

        def tmp(kind):
            lst = {"sq": sqb, "t32": t32, "rr": rrb}[kind]
            i = tci[kind] % len(lst); tci[kind] += 1
            return lst[i], (kind, i)

        xn = A.alloc(KC * 768 * 2, name="xn").rearrange("p (k n) -> p k n", k=KC)
        xnb = [xn, None]
        phase0 = A.mark()
        xn2 = A.alloc(KC * 768 * 2).rearrange("p (k n) -> p k n", k=KC)
        xnb[1] = xn2
        u_ff = A.alloc(NFF * 768 * 2).rearrange("p (k n) -> p k n", k=NFF)
        wB = [A.alloc(NFF * 256 * 2).rearrange("p (k n) -> p k n", k=NFF) for _ in range(2)]
        ffn_end = A.mark()
        A.reset(phase0)
        kT = A.alloc(2 * NKS * 128 * 2, name="kT").rearrange("p (k n) -> p k n", k=2)
        Vaug = A.alloc(NKS * 2 * 128 * 2, name="Vaug").rearrange("p (s k n) -> p s k n", s=NKS, k=2)
        xpool = A.alloc(NKS * 256 * 2, name="xpool").rearrange("p (s n) -> p s n", s=NKS)
        uT = A.alloc(2 * 512 * 2, name="uT").rearrange("p (k n) -> p k n", k=2)
        qT = A.alloc(4 * 512 * 2, name="qT").rearrange("p (k n) -> p k n", k=4)
        attnT = A.alloc(4 * 512 * 2, name="attnT").rearrange("p (k n) -> p k n", k=4)
        pTb = [A.alloc(512 * 2) for _ in range(4)]
        pooled = A.alloc(512 * 2)
        poolout = A.alloc(2 * 512 * 2, name="poolout").rearrange("p (k n) -> p k n", k=2)
        gates = [A.alloc(512 * 2) for _ in range(6)]
        merged = A.alloc(KC * 512 * 2, name="merged").rearrange("p (k n) -> p k n", k=KC)
        xnb.append(merged)
        kg32 = [A.alloc(512 * 4, F32) for _ in range(1)]
        vn = [A.alloc(256 * 2) for _ in range(2)]
        gv = [A.alloc(256 * 4, F32) for _ in range(2)]
        small = A.alloc(64 * 4, F32)
        mix_end = A.mark()
        A.reset(max(ffn_end, mix_end))
        print("SBUF bytes/partition: ffn_end", ffn_end, "mix_end", mix_end)

        cnt = {"pt": 0, "gate": 0, "kg": 0, "vn": 0, "sm": 0}

        S.dma("sp", vecs, vecs_d, writes=[("vecs",)])
        S.dma("sp", cTs, cT, writes=[("cTs",)])
        S.dma("sp", perm, perm_d, writes=[("perm",)])
        S.dma("sp", sgun, sgun_d.rearrange("l p n -> p l n"), writes=[("sgun",)])
        S.dma("sp", sgub, sgub_d.rearrange("l p c n -> p l c n"), writes=[("sgub",)])
        S.dma("pool", cosb, cos_d, writes=[("cos",)])
        S.dma("pool", sinb, sin_d, writes=[("sin",)])
        S.dma("pool", bands, band_d, writes=[("bands",)])
        S.dma("pool", poolbd, poolbd_d.rearrange("l p c n -> p l c n"), writes=[("poolbd",)])
        S.dma("pool", sguwT, sguwT_d.rearrange("l p g n -> p l g n"), writes=[("sguwT",)])
        S.op("dve", lambda e: e.memset(ones, 1.0), writes=[("ones",)])
        S.op("dve", lambda e: e.memset(bdones, 0.0), writes=[("bdones",)])
        S.op("dve", lambda e: e.memset(bdones[0:64, 0:64], 1.0), writes=[("bdones",)])
        S.op("dve", lambda e: e.memset(bdones[64:128, 64:128], 1.0), writes=[("bdones",)])

        def precast_ffn(l, f):
            src = w13_d[f][l].rearrange("(k p) n -> p k n", p=128)
            for g in range(11):
                S.dma("pool", w13s[l][f][g][:, :, 0:256], src[:, :, g * 256:(g + 1) * 256], writes=[("w13s", l, f, g)])
                S.dma("pool", w13s[l][f][g][:, :, 256:512], src[:, :, DFF + g * 256:DFF + (g + 1) * 256],
                      writes=[("w13s", l, f, g)])
            src2 = w2_d[f][l].rearrange("(k p) n -> p k n", p=128)
            for g in range(4):
                S.dma("pool", w2s[l][f][g], src2[:, :, g * 256:(g + 1) * 256], writes=[("w2s", l, f, g)])

        def precast_mix(l):
            src = w_in[l].rearrange("(k p) n -> p k n", p=128)
            W = wins[l]
            k0 = ("wins", l, 0)
            for kv in range(2):
                for dup in range(2):
                    for r in range(2):
                        d0 = kv * 128 + dup * 64 + r * 32
                        s0 = 512 + kv * 64 + r * 16
                        for a in range(2):
                            S.dma("pool", W[0][:, :, d0 + a * 16:d0 + a * 16 + 16], src[:, :, s0 + a * 32:s0 + a * 32 + 16], writes=[k0])
            S.dma("pool", W[1][:, :, 0:128], src[:, :, 640:768], writes=[("wins", l, 1)])
            S.dma("pool", W[1][:, :, 128:384], src[:, :, 768:1024], writes=[("wins", l, 1)])
            S.dma("pool", W[2], src[:, :, 1024:1536], writes=[("wins", l, 2)])
            for h in range(8):
                for r in range(2):
                    d0 = h * 64 + r * 32
                    s0 = h * 64 + r * 16
                    for a in range(2):
                        S.dma("pool", W[3][:, :, d0 + a * 16:d0 + a * 16 + 16], src[:, :, s0 + a * 32:s0 + a * 32 + 16], writes=[("wins", l, 3)])
            brp = w_brp[l].rearrange("(k p) n -> p k n", p=128)
            bra = w_bra[l].rearrange("(k p) n -> p k n", p=128)
            brs = w_brs[l].rearrange("(k p) n -> p k n", p=128)
            for m in range(8):
                key = ("wins", l, 4 + m)
                for gi in range(3):
                    S.dma("pool", W[4 + m][:, :, gi * 128:(gi + 1) * 128],
                          src[:, :, 1536 + gi * 1024 + m * 128: 1536 + gi * 1024 + (m + 1) * 128], writes=[key])
                S.dma("pool", W[4 + m][:, 0:2, 384:512], brp[:, :, m * 128:(m + 1) * 128], writes=[key])
                S.dma("pool", W[4 + m][:, 2:6, 384:512], bra[:, :, m * 128:(m + 1) * 128], writes=[key])
                S.dma("pool", W[4 + m][:, 6:8, 384:512], brs[:, :, m * 128:(m + 1) * 128], writes=[key])
            so = w_out[l].rearrange("(k p) n -> p k n", p=128)
            for og in range(2):
                S.dma("pool", wouts[l][og], so[:, :, og * 512:(og + 1) * 512], writes=[("wouts", l, og)])

        def adaln_piece(l, g):
            src = w_ada[l].rearrange("(k p) n -> p k n", p=128)
            wa, wk = wa_next()
            S.dma("pool", wa, src[:, :, g * 512:(g + 1) * 512], writes=[wk])
            pm, pmk = ps_rot()
            for j in range(4):
                S.mm(pm[:, j * 3:(j + 1) * 3], [(wa[:, kc, j * 128:(j + 1) * 128], scT[:, kc, :]) for kc in range(KC)],
                     reads=[wk, ("scT",)], writes=[pmk])
            bo = l * VL + V_BADA + g * 4
            S.op("dve", lambda e: e.tensor_tensor(out=modT[:, l, g * 4:(g + 1) * 4, :], in0=pm[:, 0:12].rearrange("p (j n) -> p j n", n=3),
                                                  in1=vecs[:, bo:bo + 4].unsqueeze(2).to_broadcast([128, 4, 3]), op=ALU.add),
                 reads=[pmk, ("vecs",)], writes=[("modT", l)])

        def adaln_finish(l, subs=(0, 1, 2)):
            for s in subs:
                no = l * VL + (V_N1, V_NM, V_N2)[s]
                S.op("dve", lambda e: e.scalar_tensor_tensor(
                    out=Amod[:, l, s], in0=modT[:, l, (3 * s + 1) * 8:(3 * s + 2) * 8, :], scalar=1.0,
                    in1=vecs[:, no:no + 8].unsqueeze(2).to_broadcast([128, 8, 3]), op0=ALU.add, op1=ALU.mult),
                    reads=[("modT", l), ("vecs",)], writes=[("Amod", l)])
                gs = 1.0 if s == 1 else 0.5
                S.op("dve", lambda e: e.tensor_scalar(out=Gmod[:, l, s], in0=modT[:, l, (3 * s + 2) * 8:(3 * s + 3) * 8, :],
                                                      scalar1=gs, scalar2=None, op0=ALU.mult),
                     reads=[("modT", l)], writes=[("Gmod", l)])

        bg = []
        aux = ["dve"]

        def run_bg(n=1):
            for _ in range(n):
                if bg:
                    bg.pop(0)()

        def hap(t, kc):
            st, t0, w = t
            return (hT if st == "lat" else hcT)[:, kc, t0:t0 + w]

        def hkey(t, kc):
            return ("h", t[0], t[1], kc)

        def rstd_from_ps(ps_ap, pskey, w, scale, npart=128):
            sr, srk = tmp("rr")
            S.op("act", lambda e: e.activation(out=sr[:, 0:w], in_=ps_ap, func=AF.Ln, bias=epsc[:, 0:1], scale=scale),
                 reads=[pskey, ("epsc",)], writes=[srk])
            S.op("act", lambda e: e.activation(out=sr[:, 0:w], in_=sr[:, 0:w], func=AF.Exp, scale=-0.5), reads=[srk], writes=[srk])
            return sr, srk

        def modulate(t, l, s, bcol, xoff, xb=0):
            st, t0, w = t
            xn = xnb[xb]
            col = 2 if st == "ctx" else bcol
            ps, pk = ps_rot()
            for kc in range(KC):
                S.op("act", lambda e: e.activation(out=xn[:, kc, xoff:xoff + w], in_=hap(t, kc), func=AF.Square),
                     reads=[hkey(t, kc)], writes=[("xn", xb, kc, xoff)])
            for kc in range(KC):
                S.mm(ps[:, 0:w], [(ones, xn[:, kc, xoff:xoff + w])], reads=[("xn", xb, kc, xoff), ("ones",)], writes=[pk],
                     start=(kc == 0), stop=(kc == KC - 1))
            rr, rk = rstd_from_ps(ps[:, 0:w], pk, w, 1.0 / D)
            for kc in range(KC):
                tt, tk = tmp("t32")
                S.op("dve", lambda e: e.scalar_tensor_tensor(out=tt[:, 0:w], in0=hap(t, kc), scalar=Amod[:, l, s, kc, col:col + 1],
                                                             in1=rr[:, 0:w], op0=ALU.mult, op1=ALU.mult),
                     reads=[hkey(t, kc), rk, ("Amod", l)], writes=[tk])
                S.op("act", lambda e: e.activation(out=xn[:, kc, xoff:xoff + w], in_=tt[:, 0:w], func=AF.Identity,
                                                   bias=modT[:, l, 3 * s * 8 + kc, col:col + 1], scale=1.0),
                     reads=[tk, ("modT", l)], writes=[("xn", xb, kc, xoff)])

        def xn_keys(xoff, xb=0):
            return [("xn", xb, kc, xoff) for kc in range(KC)]

        def ffn(l, f, bcol, tiles):
            s = 0 if f == 0 else 2
            sbs = [tiles[i:i + 2] for i in range(0, len(tiles), 2)]

            def offs_of(sb):
                offs = []; o = 0
                for t in sb:
                    offs.append(o); o += t[2]
                return offs

            def mod_sb(i):
                for t, xo in zip(sbs[i], offs_of(sbs[i])):
                    modulate(t, l, s, bcol, xo, xb=i % 2)

            mod_sb(0)
            for i, sb in enumerate(sbs):
                offs = offs_of(sb)
                xb = i % 2
                xn = xnb[xb]
                for g in range(11):
                    if early:
                        early.pop(0)()
                    wa, wk = wa_next()
                    S.dma("sp", wa, w13s[l][f][g], reads=[("w13s", l, f, g)], writes=[wk])
                    for j in range(2):
                        n = 2 * g + j
                        for t, xo in zip(sb, offs):
                            w = t[2]
                            pa, pak = ps_rot(); pb, pbk = ps_rot()
                            S.mm(pa[:, 0:w], [(wa[:, kc, j * 128:(j + 1) * 128], xn[:, kc, xo:xo + w]) for kc in range(KC)],
                                 reads=[wk] + xn_keys(xo, xb), writes=[pak])
                            S.mm(pb[:, 0:w], [(wa[:, kc, 256 + j * 128:256 + (j + 1) * 128], xn[:, kc, xo:xo + w]) for kc in range(KC)],
                                 reads=[wk] + xn_keys(xo, xb), writes=[pbk])
                            sa, sak = tmp("t32")
                            S.op("act", lambda e: e.activation(out=sa[:, 0:w], in_=pa[:, 0:w], func=AF.Silu), reads=[pak], writes=[sak])
                            S.op("dve", lambda e: e.tensor_tensor(out=u_ff[:, n, xo:xo + w], in0=sa[:, 0:w], in1=pb[:, 0:w], op=ALU.mult),
                                 reads=[sak, pbk], writes=[("u", n, xo)])
                for g in range(4):
                    if g == 2 and i + 1 < len(sbs):
                        mod_sb(i + 1)
                    wb = wB[g % 2]; wbk = ("wB", g % 2)
                    S.dma("sp", wb, w2s[l][f][g], reads=[("w2s", l, f, g)], writes=[wbk])
                    for j in range(2):
                        m = 2 * g + j
                        for t, xo in zip(sb, offs):
                            w = t[2]; col = 2 if t[0] == "ctx" else bcol
                            py, pyk = ps_rot()
                            S.mm(py[:, 0:w], [(wb[:, n, j * 128:(j + 1) * 128], u_ff[:, n, xo:xo + w]) for n in range(NFF)],
                                 reads=[wbk] + [("u", n, xo) for n in range(NFF)], writes=[pyk])
                            S.op("dve", lambda e: e.scalar_tensor_tensor(out=hap(t, m), in0=py[:, 0:w], scalar=Gmod[:, l, s, m, col:col + 1],
                                                                         in1=hap(t, m), op0=ALU.mult, op1=ALU.add),
                                 reads=[pyk, ("Gmod", l), hkey(t, m)], writes=[hkey(t, m)])

        def gelu_to(out_ap, ps_ap, pskey, outkey, npart, w):
            S.op("act", lambda e: e.activation(out=out_ap, in_=ps_ap, func=AF.Gelu_apprx_tanh), reads=[pskey], writes=[outkey])

        def qk_norm_rope(ps, pk, w, gain_col, out_ap, outkey, rope, tok0):
            sq, sk = tmp("sq")
            S.op("act", lambda e: e.activation(out=sq[:, 0:w], in_=ps[:, 0:w], func=AF.Square), reads=[pk], writes=[sk])
            p2, p2k = ps_rot()
            S.mm(p2[:, 0:w], [(bdones, sq[:, 0:w])], reads=[sk, ("bdones",)], writes=[p2k])
            rr, rk = rstd_from_ps(p2[:, 0:w], p2k, w, 1.0 / 64)
            if not rope:
                S.op("dve", lambda e: e.scalar_tensor_tensor(out=out_ap, in0=ps[:, 0:w], scalar=vecs[:, gain_col:gain_col + 1],
                                                             in1=rr[:, 0:w], op0=ALU.mult, op1=ALU.mult),
                     reads=[pk, rk, ("vecs",)], writes=[outkey])
                return
            i = 0
            kg = kg32[i]; kk = ("kg", i)
            S.op("dve", lambda e: e.scalar_tensor_tensor(out=kg[:, 0:w], in0=ps[:, 0:w], scalar=vecs[:, gain_col:gain_col + 1],
                                                         in1=rr[:, 0:w], op0=ALU.mult, op1=ALU.mult),
                 reads=[pk, rk, ("vecs",)], writes=[kk])
            w1, w1k = tmp("t32")
            S.op(aux[0], lambda e: e.tensor_tensor(out=w1[:, 0:w], in0=rr[:, 0:w], in1=sinb[:, tok0:tok0 + w], op=ALU.mult),
                 reads=[rk, ("sin",)], writes=[w1k])
            t1, t1k = tmp("t32")
            gsw = 208 + 2 * (gain_col // VL) + (1 if gain_col % VL == V_KN else 0)
            for qd in range(4):
                o = qd * 32; so = (qd ^ 1) * 32
                S.op("dve", lambda e: e.scalar_tensor_tensor(out=t1[o:o + 32, 0:w], in0=ps[so:so + 32, 0:w], scalar=vecs[o:o + 32, gsw:gsw + 1],
                                                             in1=w1[o:o + 32, 0:w], op0=ALU.mult, op1=ALU.mult),
                     reads=[pk, w1k, ("vecs",)], writes=[t1k])
            S.op(aux[0], lambda e: e.tensor_tensor(out=kg[:, 0:w], in0=kg[:, 0:w], in1=cosb[:, tok0:tok0 + w], op=ALU.mult),
                 reads=[kk, ("cos",)], writes=[kk])
            S.op(aux[0], lambda e: e.tensor_tensor(out=out_ap, in0=kg[:, 0:w], in1=t1[:, 0:w], op=ALU.add),
                 reads=[kk, t1k], writes=[outkey])

        def koff(t):
            return t[1] if t[0] == "lat" else L + t[1]

        def pass1_all(l, bcol, tiles):
            kps_of = {}

            xb_of = {}

            def p1_mod(t):
                xb_of[t] = 0 if len(xb_of) % 2 == 0 else 2
                modulate(t, l, 1, bcol, 0, xb=xb_of[t])

            def p1_proj(t):
                st, t0, w = t
                ko = koff(t)
                xb = xb_of[t]
                xn = xnb[xb]
                wa, wk = wa_next()
                S.dma("sp", wa[:, :, 0:256], wins[l][0][:, :, 0:256], reads=[("wins", l, 0)], writes=[wk])
                kps = []
                for j in range(2):
                    ps, pk = PS[6 + j], ("ps", 6 + j)
                    S.mm(ps[:, 0:w], [(wa[:, kc, j * 128:(j + 1) * 128], xn[:, kc, 0:w]) for kc in range(KC)],
                         reads=[wk] + xn_keys(0, xb), writes=[pk])
                    kps.append((ps, pk))
                kps_of[t] = kps
                wa, wk = wa_next()
                S.dma("sp", wa[:, :, 0:384], wins[l][1][:, :, 0:384], reads=[("wins", l, 1)], writes=[wk])
                for si in range(w // 128):
                    gs = (ko + si * 128) // 128
                    ps, pk = ps_rot()
                    S.mm(ps[:, 0:384], [(xn[:, kc, si * 128:(si + 1) * 128], wa[:, kc, 0:384]) for kc in range(KC)],
                         reads=[wk] + xn_keys(0, xb), writes=[pk])
                    S.op("act", lambda e: e.activation(out=Vaug[:, gs, :, 0:64], in_=ps[:, 0:128].rearrange("p (k n) -> p k n", k=2),
                                                       func=AF.Copy), reads=[pk], writes=[("V", gs)])
                    S.op("act", lambda e: e.activation(out=xpool[:, gs, :], in_=ps[:, 128:384], func=AF.Copy), reads=[pk], writes=[("xpool", gs)])

            def p1_chain(t):
                st, t0, w = t
                ko = koff(t)
                kps = kps_of[t]
                for j in range(2):
                    qk_norm_rope(kps[j][0], kps[j][1], w, l * VL + V_KN, kT[:, j, ko:ko + w], ("kT", j, ko), st != "ctx", t0)

            p1_mod(tiles[0]); p1_proj(tiles[0])
            for k in range(1, len(tiles)):
                p1_mod(tiles[k])
                p1_chain(tiles[k - 1])
                p1_proj(tiles[k])
            p1_chain(tiles[-1])

        def sgu_items(l, t):
            st, t0, w = t
            hold = {}

            def item_u():
                wa, wk = wa_next()
                hold["wa"] = (wa, wk)
                S.dma("sp", wa, wins[l][2], reads=[("wins", l, 2)], writes=[wk])
                for j in range(2):
                    ps, pk = ps_rot()
                    S.mm(ps[:, 0:w], [(wa[:, kc, j * 128:(j + 1) * 128], xn[:, kc, 0:w]) for kc in range(KC)],
                         reads=[wk] + xn_keys(0), writes=[pk])
                    gelu_to(uT[:, j, 0:w], ps[:, 0:w], pk, ("uT", j), 128, w)

            def item_v(si):
                wa, wk = hold["wa"]
                ps, pk = ps_rot()
                S.mm(ps[:, 0:256], [(xn[:, kc, si * 128:(si + 1) * 128], wa[:, kc, 256:512]) for kc in range(KC)],
                     reads=[wk] + xn_keys(0), writes=[pk])
                i = cnt["vn"] % 2; cnt["vn"] += 1
                g_ = gv[i]; gk = ("gv", i); v_ = vn[i]; vk = ("vn", i)
                gelu_to(g_, ps[:, 0:256], pk, gk, 128, 256)
                sq, sk = tmp("sq")
                sm = small[:, (cnt["sm"] % 8) * 2:(cnt["sm"] % 8) * 2 + 1]; smk = ("sm", cnt["sm"] % 8); cnt["sm"] += 1
                S.op("act", lambda e: e.activation(out=sq[:, 0:256], in_=g_, func=AF.Square, accum_out=sm),
                     reads=[gk], writes=[sk, smk])
                S.op("act", lambda e: e.activation(out=sm, in_=sm, func=AF.Sqrt, bias=epsc[:, 0:1], scale=1.0 / 256),
                     reads=[smk, ("epsc",)], writes=[smk])
                S.op("dve", lambda e: e.reciprocal(out=sm, in_=sm), reads=[smk], writes=[smk])
                S.op("dve", lambda e: e.scalar_tensor_tensor(out=v_, in0=g_, scalar=sm, in1=sgun[:, l, :], op0=ALU.mult, op1=ALU.mult),
                     reads=[gk, smk, ("sgun",)], writes=[vk])
                p2, p2k = ps_rot()
                for gi in range(4):
                    S.mm(p2[(gi % 2) * 64:(gi % 2) * 64 + 64, (gi // 2) * 128:(gi // 2) * 128 + 128],
                         [(v_[:, gi * 64:(gi + 1) * 64], sguwT[:, l, gi, :])], reads=[vk, ("sguwT",)], writes=[p2k])
                for c in range(2):
                    tt, tk = tmp("t32")
                    S.op("dve", lambda e: e.tensor_tensor(out=tt[:, 0:128], in0=p2[:, c * 128:(c + 1) * 128], in1=sgub[:, l, c, :], op=ALU.add),
                         reads=[p2k, ("sgub",)], writes=[tk])
                    us = uT[:, c, si * 128:(si + 1) * 128]
                    S.op("dve", lambda e: e.tensor_tensor(out=us, in0=tt[:, 0:128], in1=us, op=ALU.mult),
                         reads=[tk, ("uT", c)], writes=[("uT", c)])

            return [item_u] + [(lambda si=si: item_v(si)) for si in range(w // 128)]

        def pass2(l, bcol, t):
            st, t0, w = t
            ko = koff(t)
            isctx = st == "ctx"
            col = 2 if isctx else bcol
            modulate(t, l, 1, bcol, 0)
            wa, wk = wa_next()
            S.dma("sp", wa, wins[l][3], reads=[("wins", l, 3)], writes=[wk])
            for j in range(4):
                ps, pk = ps_rot()
                S.mm(ps[:, 0:w], [(wa[:, kc, j * 128:(j + 1) * 128], xn[:, kc, 0:w]) for kc in range(KC)],
                     reads=[wk] + xn_keys(0), writes=[pk])
                qk_norm_rope(ps, pk, w, l * VL + V_QN, qT[:, j, 0:w], ("qT", j), not isctx, t0)
            side = sgu_items(l, t)
            nblk = 2 if isctx else 16
            sbase = 16 if isctx else 0

            def pool_item(c):
                pp, ppk = ps_rot()
                for tb in range(w // 128):
                    i = t0 // 128 + tb
                    for half in range(2):
                        g = 2 * c + half
                        srcs = []
                        if i > 0:
                            srcs.append((i - 1, 3))
                        srcs.append((i, 0 if i == 0 else (2 if i == nblk - 1 else 1)))
                        if i < nblk - 1:
                            srcs.append((i + 1, 4))
                        S.mm(pp[half * 64:half * 64 + 64, tb * 128:(tb + 1) * 128],
                             [(xpool[:, sbase + si, c * 128 + half * 64:c * 128 + half * 64 + 64], bands[:, g * 5 + kind, :]) for si, kind in srcs],
                             reads=[("xpool", sbase + si) for si, _ in srcs] + [("bands",)], writes=[ppk])
                S.op("act", lambda e: e.activation(out=pooled[:, 0:w], in_=pp[:, 0:w], func=AF.Copy), reads=[ppk], writes=[("pooled",)])
                po_, pok = ps_rot()
                S.mm(po_[:, 0:w], [(poolbd[:, l, c, :], pooled[:, 0:w])], reads=[("pooled",), ("poolbd",)], writes=[pok])
                pc = l * VL + V_PSC + c
                S.op("act", lambda e: e.activation(out=poolout[:, c, 0:w], in_=po_[:, 0:w], func=AF.Copy, scale=vecs[:, pc:pc + 1]),
                     reads=[pok, ("vecs",)], writes=[("poolout", c)])

            kslices = list(range(16, 18)) if isctx else list(range(NKS))
            steps = [(h, gs) for h in range(8) for gs in kslices]
            pend = []

            def kkeys(kv, gs):
                o = gs * 128
                for tt in _tiles():
                    k0 = koff(tt)
                    if k0 <= o < k0 + tt[2]:
                        return ("kT", kv, k0)
                raise AssertionError

            def emit_S(h, gs):
                ps, pk = ps_rot()
                pr = slice((h % 2) * 64, (h % 2) * 64 + 64)
                S.mm(ps[:, 0:w], [(kT[pr, h // 4, gs * 128:(gs + 1) * 128], qT[pr, h // 2, 0:w])],
                     reads=[kkeys(h // 4, gs), ("qT", h // 2)], writes=[pk])
                i = cnt["pt"] % 4; cnt["pt"] += 1
                pt = pTb[i]; ptk = ("pT", i)
                S.op("act", lambda e: e.activation(out=pt[:, 0:w], in_=ps[:, 0:w], func=AF.Exp, scale=0.125), reads=[pk], writes=[ptk])
                return pt, ptk

            def emit_PV(h, gs, pt, ptk):
                pv = PS[6 + (h % 2)]; pvk = ("ps", 6 + (h % 2))
                S.mm(pv[:, 0:w], [(Vaug[:, gs, h // 4, :], pt[:, 0:w])], reads=[("V", gs), ("Vones",), ptk], writes=[pvk],
                     start=(gs == kslices[0]), stop=(gs == kslices[-1]))
                if gs == kslices[-1]:
                    rd, rdk = tmp("rr")
                    S.op("dve", lambda e: e.reciprocal(out=rd[64:128, 0:w], in_=pv[64:128, 0:w]), reads=[pvk], writes=[rdk])
                    po = (h % 2) * 64
                    S.op("dve", lambda e: e.tensor_tensor(out=attnT[po:po + 64, h // 2, 0:w], in0=pv[0:64, 0:w], in1=rd[64:128, 0:w], op=ALU.mult),
                         reads=[pvk, rdk], writes=[("attnT", h // 2, h % 2)])

            nblk = 2 if isctx else 16
            sbase = 16 if isctx else 0
            for c in range(2):
                side.append(lambda c=c: pool_item(c))
            LA = 2
            for i, (h, gs) in enumerate(steps):
                pend.append((h, gs) + emit_S(h, gs))
                if len(pend) > LA:
                    emit_PV(*pend.pop(0))
                if gs == kslices[-1] and side and h >= 1:
                    side.pop(0)()
            while pend:
                emit_PV(*pend.pop(0))
            while side:
                side.pop(0)()
            utk = [("uT", 0), ("uT", 1)]
            for m in range(8):
                if t[1] >= 1280 or isctx:
                    run_bg()
                wa, wk = wa_next()
                S.dma("sp", wa, wins[l][4 + m], reads=[("wins", l, 4 + m)], writes=[wk])
                gts = []
                for gi in range(3):
                    ps, pk = ps_rot()
                    S.mm(ps[:, 0:w], [(wa[:, kc, gi * 128:(gi + 1) * 128], xn[:, kc, 0:w]) for kc in range(KC)],
                         reads=[wk] + xn_keys(0), writes=[pk])
                    i = cnt["gate"] % 6; cnt["gate"] += 1
                    S.op("act", lambda e: e.activation(out=gates[i][:, 0:w], in_=ps[:, 0:w], func=AF.Sigmoid), reads=[pk], writes=[("gate", i)])
                    gts.append((gates[i], ("gate", i)))
                brs = [([(wa[:, kc, 384:512], poolout[:, kc, 0:w]) for kc in range(2)], [("poolout", 0), ("poolout", 1)]),
                       ([(wa[:, 2 + kc, 384:512], attnT[:, kc, 0:w]) for kc in range(4)], [("attnT", kc, hh) for kc in range(4) for hh in range(2)]),
                       ([(wa[:, 6 + kc, 384:512], uT[:, kc, 0:w]) for kc in range(2)], utk)]
                prods = []
                for gi in range(3):
                    ps, pk = ps_rot()
                    S.mm(ps[:, 0:w], brs[gi][0], reads=[wk] + brs[gi][1], writes=[pk])
                    g_, gk = gts[gi]
                    pr_, prk = tmp("t32")
                    S.op("dve", lambda e: e.tensor_tensor(out=pr_[:, 0:w], in0=g_[:, 0:w], in1=ps[:, 0:w], op=ALU.mult),
                         reads=[gk, pk], writes=[prk])
                    prods.append((pr_, prk))
                    if gi == 1:
                        S.op(aux[0], lambda e: e.tensor_tensor(out=prods[0][0][:, 0:w], in0=prods[0][0][:, 0:w], in1=prods[1][0][:, 0:w], op=ALU.add),
                             reads=[prods[0][1], prods[1][1]], writes=[prods[0][1]])
                S.op(aux[0], lambda e: e.tensor_tensor(out=merged[:, m, 0:w], in0=prods[0][0][:, 0:w], in1=prods[2][0][:, 0:w], op=ALU.add),
                     reads=[prods[0][1], prods[2][1]], writes=[("merged", m)])
            for og in range(2):
                wa, wk = wa_next()
                S.dma("sp", wa, wouts[l][og], reads=[("wouts", l, og)], writes=[wk])
                for j in range(4):
                    m = og * 4 + j
                    ps, pk = ps_rot()
                    S.mm(ps[:, 0:w], [(wa[:, kc, j * 128:(j + 1) * 128], merged[:, kc, 0:w]) for kc in range(KC)],
                         reads=[wk] + [("merged", kc) for kc in range(KC)], writes=[pk])
                    S.op("dve", lambda e: e.scalar_tensor_tensor(out=hap(t, m), in0=ps[:, 0:w], scalar=Gmod[:, l, 1, m, col:col + 1],
                                                                 in1=hap(t, m), op0=ALU.mult, op1=ALU.add),
                         reads=[pk, ("Gmod", l), hkey(t, m)], writes=[hkey(t, m)])

        epsc = A.alloc(4 * 4, F32)
        S.op("dve", lambda e: e.memset(epsc, EPS), writes=[("epsc",)])
        S.op("act", lambda e: e.activation(out=scT, in_=cTs, func=AF.Silu), reads=[("cTs",)], writes=[("scT",)])
        l0 = layers[0]
        early = []
        for g in range(6):
            adaln_piece(l0, g)
        adaln_finish(l0, (0,))
        precast_ffn(l0, 0)
        for g in range(6, 18):
            early.append(lambda g=g: adaln_piece(l0, g))
        early.append(lambda: adaln_finish(l0, (1, 2)))
        early.append(lambda: precast_mix(l0))
        early.append(lambda: precast_ffn(l0, 1))
        for l in layers[1:]:
            early.append(lambda l=l: precast_ffn(l, 0))
            early.append(lambda l=l: precast_mix(l))
            early.append(lambda l=l: precast_ffn(l, 1))
            for g in range(18):
                bg.append(lambda l=l, g=g: adaln_piece(l, g))
            bg.append(lambda l=l: adaln_finish(l))

        for b in range(nb):
            for kc in range(KC):
                ks = [hkey(t, kc) for t in _tiles(False)]
                S.dma("sp", hT[:, kc, :], xT[b][:, kc, :], writes=ks)
            S.dma("sp", hcT, ctxT[b], writes=[hkey(_tiles()[-1], kc) for kc in range(KC)])
            for l in layers:
                last = (l == 1)
                aux[0] = "dve" if (b == 0 and l == layers[0]) else "pool"
                if l != layers[0]:
                    run_bg(len(bg))
                if "ffn1" in stages:
                    ffn(l, 0, b, _tiles(True))
                while early:
                    early.pop(0)()
                if "mix" in stages:
                    S.barrier()
                    S.op("dve", lambda e: e.memset(Vaug[:, :, :, 64:128], 1.0), writes=[("Vones",)])
                    pass1_all(l, b, _tiles(True))
                    S.barrier()
                    for t in (_tiles(not last) if p2_tiles is None else [_tiles()[i] for i in p2_tiles]):
                        pass2(l, b, t)
                    S.barrier()
                if "ffn2" in stages:
                    ffn(l, 1, b, _tiles(not last))
            if final:
                for t in _tiles(False):
                    st, t0, w = t
                    ps, pk = ps_rot()
                    for kc in range(KC):
                        sq, sk = tmp("sq")
                        S.op("act", lambda e: e.activation(out=sq[:, 0:w], in_=hap(t, kc), func=AF.Square), reads=[hkey(t, kc)], writes=[sk])
                        S.mm(ps[:, 0:w], [(ones, sq[:, 0:w])], reads=[sk, ("ones",)], writes=[pk], start=(kc == 0), stop=(kc == KC - 1))
                    rr, rk = rstd_from_ps(ps[:, 0:w], pk, w, 1.0 / D)
                    for kc in range(KC):
                        S.op("dve", lambda e: e.scalar_tensor_tensor(out=hap(t, kc), in0=hap(t, kc), scalar=vecs[:, V_FN + kc:V_FN + kc + 1],
                                                                     in1=rr[:, 0:w], op0=ALU.mult, op1=ALU.mult),
                             reads=[hkey(t, kc), rk, ("vecs",)], writes=[hkey(t, kc)])
            for t in _tiles(False):
                st, t0, w = t
                S.dma("sp", outT[b][:, :, t0:t0 + w], hT[:, :, t0:t0 + w], reads=[hkey(t, kc) for kc in range(KC)])
            S.dma("sp", hcout[b], hcT, reads=[hkey(_tiles()[-1], kc) for kc in range(KC)])
        S.drain("sp")
    return nc


def _fm(v):
    return np.ascontiguousarray(v.reshape(-1, 128).T)


def _qk_perm():
    old = np.zeros(64, np.int64)
    for r in range(2):
        for a in range(2):
            for i in range(16):
                old[r * 32 + a * 16 + i] = a * 32 + r * 16 + i
    return old


def _rope_tables():
    half = 32
    inv = (10000.0 ** (-np.arange(0, half, 2, dtype=np.float32) / half)).astype(np.float32)
    t = np.arange(L)
    row = (t // 64).astype(np.float32); col = (t % 64).astype(np.float32)
    cos = np.zeros((128, L), np.float32); sin = np.zeros((128, L), np.float32)
    for p in range(128):
        pp = p % 64
        r = pp // 32; a = (pp % 32) // 16; i = pp % 16
        ang = ((row if a == 0 else col) * inv[i]).astype(np.float32)
        cos[p] = np.cos(ang)
        sin[p] = -np.sin(ang) if r == 0 else np.sin(ang)
    return cos, sin, np.zeros((128, 128), np.float32)


def _bands():
    out = np.zeros((128, 20, 128), np.float32)
    n = 384
    for g, w in enumerate((2, 4, 8, 16)):
        hw = w // 2
        M = np.zeros((n, n), np.float64)
        for t in range(n):
            lo = max(t - hw, 0); hi = min(t + hw, n)
            M[lo:hi, t] = 1.0 / (hi - lo)
            M[t, t] -= 1.0
        out[:, g * 5 + 0] = M[0:128, 0:128]
        out[:, g * 5 + 1] = M[128:256, 128:256]
        out[:, g * 5 + 2] = M[256:384, 256:384]
        out[:, g * 5 + 3] = M[0:128, 128:256]
        out[:, g * 5 + 4] = M[128:256, 0:128]
    return out


def _host_prep(inp, core, nb=NB):
    f = lambda a: np.ascontiguousarray(np.asarray(a, dtype=np.float32))
    bs = [core * nb + i for i in range(nb)]
    m = {}
    m["xT"] = np.stack([f(inp["x"][b].T.reshape(KC, 128, L).transpose(1, 0, 2)) for b in bs])
    m["ctxT"] = np.stack([f(inp["ctx"][b].T.reshape(KC, 128, C).transpose(1, 0, 2)) for b in bs])
    cols = [inp["c"][b] for b in bs] + [inp["c_ctx"]]
    while len(cols) < 3:
        cols.insert(-1, cols[0])
    m["cT"] = f(np.stack([_fm(np.asarray(v)) for v in cols], axis=2))
    return m


def _shared_prep(inp):
    f = lambda a: np.ascontiguousarray(np.asarray(a, dtype=np.float32))
    vecs = np.zeros((128, NV), np.float32)
    for l in range(2):
        o = l * VL
        vecs[:, o + V_N1:o + V_N1 + 8] = _fm(inp["norm_ffn1"][l])
        vecs[:, o + V_NM:o + V_NM + 8] = _fm(inp["norm_mix"][l])
        vecs[:, o + V_N2:o + V_N2 + 8] = _fm(inp["norm_ffn2"][l])
        vecs[:, o + V_BADA:o + V_BADA + 72] = _fm(inp["b_ada"][l])
        vecs[:, o + V_PSC:o + V_PSC + 2] = _fm(inp["pool_scale"][l])
        vecs[:, o + V_QN] = np.tile(inp["q_norm"][l][_qk_perm()], 2)
        vecs[:, o + V_KN] = np.tile(inp["k_norm"][l][_qk_perm()], 2)
        sw = np.arange(128) ^ 32
        vecs[:, 208 + 2 * l] = vecs[sw, o + V_QN]
        vecs[:, 209 + 2 * l] = vecs[sw, o + V_KN]
    vecs[:, V_FN:V_FN + 8] = _fm(inp["final_norm"])
    m = {"vecs": vecs}
    for k in ("w_ada", "ffn1_w13", "ffn2_w13", "ffn1_w2", "ffn2_w2", "w_in", "w_br_pool", "w_br_attn", "w_br_sgu", "w_out"):
        m[k] = f(inp[k])
    pbd = np.zeros((2, 128, 2, 128), np.float32)
    for l in range(2):
        for g in range(4):
            c, hh = g // 2, g % 2
            pbd[l, hh * 64:(hh + 1) * 64, c, hh * 64:(hh + 1) * 64] = inp["pool_w"][l, g]
    m["poolbd"] = pbd
    m["sguwT"] = f(np.transpose(inp["sgu_w"], (0, 3, 1, 2)))
    m["sgun"] = f(np.broadcast_to(inp["sgu_norm"][:, None, :], (2, 128, 256)))
    sb = np.zeros((2, 128, 2, 128), np.float32)
    for l in range(2):
        for g in range(4):
            c, hh = g // 2, g % 2
            sb[l, hh * 64:(hh + 1) * 64, c, :] = inp["sgu_b"][l, g][None, :]
    m["sgub"] = sb
    cos, sin, perm = _rope_tables()
    m["ropecos"] = cos; m["ropesin"] = sin; m["ropeperm"] = perm
    m["bands"] = _bands()
    return m


_NC_CACHE = {}


def kernel(**inputs):
    inp = {k: np.asarray(v) for k, v in inputs.items()}
    if "full" not in _NC_CACHE:
        _NC_CACHE["full"] = build()
    nc = _NC_CACHE["full"]
    shared = _shared_prep(inp)
    in_maps = []
    for core in range(NCORES):
        m = dict(shared)
        m.update(_host_prep(inp, core))
        in_maps.append(m)
    res = run_bass_kernel_spmd(nc, in_maps, core_ids=list(range(NCORES)))
    out = np.empty((NCORES * NB, L, D), np.float32)
    for core in range(NCORES):
        o = res.results[core]["outT"]
        for i in range(NB):
            out[core * NB + i] = o[i].transpose(1, 0, 2).reshape(D, L).T
    return out
```

```python
import contextlib
import numpy as np
import concourse.bass as bass
import concourse.mybir as mybir
from concourse.bass_utils import run_bass_kernel_spmd

F32 = mybir.dt.float32
BF16 = mybir.dt.bfloat16
ALU = mybir.AluOpType
AF = mybir.ActivationFunctionType

D = 1024; L = 2048; C = 256; DFF = 2816; NFF = 22; KC = 8
NB = 2
NCORES = 8
EPS = 1e-6
NV = 212
VL = 100
V_N1, V_NM, V_N2, V_BADA, V_PSC, V_QN, V_KN = 0, 8, 16, 24, 96, 98, 99
V_FN = 200
NKS = 18
GELU_K = 1.5957691216057308


DEBUG = {}


class _Buf:
    __slots__ = ("w", "r")

    def __init__(self):
        self.w = None
        self.r = {}


class Sched:
    NDMA = 8

    def __init__(self, nc, es):
        self.nc = nc; self.es = es
        self.E = {"pe": nc.tensor, "act": nc.scalar, "dve": nc.vector, "pool": nc.gpsimd, "sp": nc.sync}
        self.sem = {}; self.cnt = {}; self.nsem = 0
        self.seen = {e: {} for e in self.E}
        self.allsems = {}
        for e in self.E:
            self._newsem(e)
        self.bufs = {}
        self.dq = {}; self.dqi = {}

    def _mk(self, name):
        self.nsem += 1
        s = self.es.enter_context(self.nc.semaphore(f"{name}_{self.nsem}"))
        return s

    def _newsem(self, e):
        self.sem[e] = self._mk("s" + e); self.cnt[e] = 0

    def buf(self, k):
        b = self.bufs.get(k)
        if b is None:
            b = self.bufs[k] = _Buf()
        return b

    def _wait(self, e, tok):
        sem, val = tok[0], tok[1]
        d = self.seen[e]
        if d.get(id(sem), 0) >= val:
            return
        self.E[e].wait_ge(sem, val)
        d[id(sem)] = val

    def _deps(self, e, reads, writes):
        for k in reads:
            b = self.buf(k)
            if b.w is not None and not (b.w[2] == e and e == "pe"):
                self._wait(e, b.w)
        for k in writes:
            b = self.buf(k)
            if b.w is not None and not (b.w[2] == e and e == "pe"):
                self._wait(e, b.w)
            for rk, t in b.r.items():
                if t[2] == e and e == "pe":
                    continue
                self._wait(e, t)

    def _commit(self, tok, reads, writes):
        rk = tok[2] if tok[2] != "dma" else id(tok[0])
        for k in reads:
            self.buf(k).r[rk] = tok
        for k in writes:
            b = self.buf(k); b.w = tok; b.r = {}

    def _tok(self, e, ins):
        if self.cnt[e] >= 4000:
            self._newsem(e)
        self.cnt[e] += 1
        ins.then_inc(self.sem[e], 1)
        tok = (self.sem[e], self.cnt[e], e)
        self.allsems[id(tok[0])] = tok
        return tok

    def op(self, e, fn, reads=(), writes=()):
        self._deps(e, reads, writes)
        ins = fn(self.E[e])
        self._commit(self._tok(e, ins), reads, writes)

    def mm(self, out, pairs, reads, writes, start=True, stop=True):
        self._deps("pe", reads, writes)
        n = len(pairs); ins = None
        for i, (lt, rh) in enumerate(pairs):
            ins = self.nc.tensor.matmul(out, lt, rh, start=(start and i == 0), stop=(stop and i == n - 1))
        self._commit(self._tok("pe", ins), reads, writes)

    def dma(self, q, out, in_, reads=(), writes=()):
        pool = self.dq.setdefault(q, [])
        i = self.dqi.get(q, 0); self.dqi[q] = i + 1
        if len(pool) < (16 if q == "pool" else self.NDMA):
            pool.append([self._mk("d" + q), 0])
        ent = pool[i % (16 if q == "pool" else self.NDMA)]
        if ent[1] > 0:
            self._wait(q, (ent[0], ent[1]))
        self._deps(q, reads, writes)
        ins = self.E[q].dma_start(out=out, in_=in_)
        ent[1] += 16
        ins.then_inc(ent[0], 16)
        tok = (ent[0], ent[1], "dma")
        self.allsems[id(ent[0])] = tok
        self._commit(tok, reads, writes)

    def barrier(self):
        skip = set(id(x[0]) for x in self.dq.get("pool", []))
        for e in self.E:
            for t in list(self.allsems.values()):
                if t[2] == e or id(t[0]) in skip:
                    continue
                self._wait(e, t)

    def drain(self, e):
        for t in list(self.allsems.values()):
            if t[2] == "dma":
                self._wait(e, t)


class Arena:
    def __init__(self, nc, es, nbytes):
        self.t = es.enter_context(nc.sbuf_tensor("arena", [128, nbytes // 2], BF16))
        self.cap = nbytes; self.off = 0

    def alloc(self, nbytes, dt=BF16, at=None, name=None):
        rb = (nbytes + 127) // 128 * 128
        if name:
            DEBUG[name] = (self.off if at is None else at, nbytes)
        if at is None:
            at = self.off; self.off += rb
        assert at + rb <= self.cap, (at, rb, self.cap)
        ap = self.t[:, at // 2:(at + nbytes) // 2]
        if dt == F32:
            ap = ap.bitcast(F32)
        return ap

    def mark(self):
        return self.off

    def reset(self, m):
        self.off = m


def _tiles(include_ctx=True):
    t = [("lat", 0, 512), ("lat", 512, 256), ("lat", 768, 512), ("lat", 1280, 256), ("lat", 1536, 512)]
    if include_ctx:
        t.append(("ctx", 0, 256))
    return t


def build(nb=NB, layers=(0, 1), stages=("ffn1", "mix", "ffn2"), final=True, p2_tiles=None):
    nc = bass.Bass("TRN2", target_bir_lowering=False)

    def din(name, shape, dt=F32):
        return nc.dram_tensor(name, list(shape), dt, kind="ExternalInput").ap()

    def dscr(name, shape, dt=BF16):
        return nc.dram_tensor(name, list(shape), dt, kind="Internal").ap()

    xT = din("xT", [nb, 128, KC, L]); ctxT = din("ctxT", [nb, 128, KC, C])
    cT = din("cT", [128, KC, 3]); vecs_d = din("vecs", [128, NV])
    w_ada = din("w_ada", [2, D, 9 * D])
    w13_d = [din("ffn1_w13", [2, D, 2 * DFF]), din("ffn2_w13", [2, D, 2 * DFF])]
    w2_d = [din("ffn1_w2", [2, DFF, D]), din("ffn2_w2", [2, DFF, D])]
    w_in = din("w_in", [2, D, 4608])
    w_brp = din("w_br_pool", [2, 256, D]); w_bra = din("w_br_attn", [2, 512, D]); w_brs = din("w_br_sgu", [2, 256, D])
    w_out = din("w_out", [2, D, D])
    poolbd_d = din("poolbd", [2, 128, 2, 128]); sguwT_d = din("sguwT", [2, 128, 4, 128])
    sgun_d = din("sgun", [2, 128, 256]); sgub_d = din("sgub", [2, 128, 2, 128])
    cos_d = din("ropecos", [128, L]); sin_d = din("ropesin", [128, L]); perm_d = din("ropeperm", [128, 128])
    band_d = din("bands", [128, 20, 128])
    outT = nc.dram_tensor("outT", [nb, 128, KC, L], F32, kind="ExternalOutput").ap()
    hcout = nc.dram_tensor("hcout", [nb, 128, KC, C], F32, kind="ExternalOutput").ap()

    w13s = [[dscr(f"w13s_{l}_{f}", [11, 128, KC, 512]) for f in range(2)] for l in range(2)]
    w2s = [[dscr(f"w2s_{l}_{f}", [4, 128, NFF, 256]) for f in range(2)] for l in range(2)]
    wins = [dscr(f"wins_{l}", [12, 128, KC, 512]) for l in range(2)]
    wouts = [dscr(f"wouts_{l}", [2, 128, KC, 512]) for l in range(2)]

    with contextlib.ExitStack() as es:
        S = Sched(nc, es)
        A = Arena(nc, es, 207 * 1024)
        PS = [es.enter_context(nc.psum_tensor(f"ps{i}", [128, 512], F32)) for i in range(8)]
        psi = [0]

        def ps_rot():
            i = psi[0] % 6; psi[0] += 1
            return PS[i], ("ps", i)

        hT = A.alloc(KC * L * 4, F32).rearrange("p (k n) -> p k n", k=KC)
        hcT = A.alloc(KC * C * 4, F32).rearrange("p (k n) -> p k n", k=KC)
        cosb = A.alloc(L * 2); sinb = A.alloc(L * 2)
        perm = A.alloc(128 * 4, F32)
        vecs = A.alloc(NV * 4, F32)
        ones = A.alloc(128 * 2); bdones = A.alloc(128 * 2)
        bands = A.alloc(20 * 128 * 2).rearrange("p (k n) -> p k n", k=20)
        poolbd = A.alloc(2 * 2 * 128 * 2).rearrange("p (l c n) -> p l c n", l=2, c=2)
        sguwT = A.alloc(2 * 4 * 128 * 2).rearrange("p (l g n) -> p l g n", l=2, g=4)
        sgun = A.alloc(2 * 256 * 4, F32).rearrange("p (l n) -> p l n", l=2)
        sgub = A.alloc(2 * 2 * 128 * 4, F32).rearrange("p (l c n) -> p l c n", l=2, c=2)
        cTs = A.alloc(KC * 3 * 4, F32).rearrange("p (k n) -> p k n", k=KC)
        scT = A.alloc(KC * 3 * 2).rearrange("p (k n) -> p k n", k=KC)
        modT = A.alloc(2 * 72 * 3 * 4, F32).rearrange("p (l j n) -> p l j n", l=2, j=72)
        Amod = A.alloc(2 * 3 * KC * 3 * 4, F32).rearrange("p (l s k n) -> p l s k n", l=2, s=3, k=KC)
        Gmod = A.alloc(2 * 3 * KC * 3 * 4, F32).rearrange("p (l s k n) -> p l s k n", l=2, s=3, k=KC)
        NWA = 2
        wA = [A.alloc(KC * 512 * 2).rearrange("p (k n) -> p k n", k=KC) for _ in range(NWA)]
        wai = [0]

        def wa_next():
            i = wai[0] % NWA; wai[0] += 1
            return wA[i], ("wA", i)

        sqb = [A.alloc(512 * 2) for _ in range(2)]
        t32 = [A.alloc(512 * 4, F32) for _ in range(4)]
        rrb = [A.alloc(512 * 4, F32) for _ in range(2)]
        tci = {"sq": 0, "t32": 0, "rr": 0}

        def tmp(kind):
            lst = {"sq": sqb, "t32": t32, "rr": rrb}[kind]
            i = tci[kind] % len(lst); tci[kind] += 1
            return lst[i], (kind, i)

        xn = A.alloc(KC * 768 * 2, name="xn").rearrange("p (k n) -> p k n", k=KC)
        xnb = [xn, None]
        phase0 = A.mark()
        xn2 = A.alloc(KC * 768 * 2).rearrange("p (k n) -> p k n", k=KC)
        xnb[1] = xn2
        u_ff = A.alloc(NFF * 768 * 2).rearrange("p (k n) -> p k n", k=NFF)
        wB = [A.alloc(NFF * 256 * 2).rearrange("p (k n) -> p k n", k=NFF) for _ in range(2)]
        ffn_end = A.mark()
        A.reset(phase0)
        kT = A.alloc(2 * NKS * 128 * 2, name="kT").rearrange("p (k n) -> p k n", k=2)
        Vaug = A.alloc(NKS * 2 * 128 * 2, name="Vaug").rearrange("p (s k n) -> p s k n", s=NKS, k=2)
        xpool = A.alloc(NKS * 256 * 2, name="xpool").rearrange("p (s n) -> p s n", s=NKS)
        uT = A.alloc(2 * 512 * 2, name="uT").rearrange("p (k n) -> p k n", k=2)
        qT = A.alloc(4 * 512 * 2, name="qT").rearrange("p (k n) -> p k n", k=4)
        attnT = A.alloc(4 * 512 * 2, name="attnT").rearrange("p (k n) -> p k n", k=4)
        pTb = [A.alloc(512 * 2) for _ in range(4)]
        pooled = A.alloc(512 * 2)
        poolout = A.alloc(2 * 512 * 2, name="poolout").rearrange("p (k n) -> p k n", k=2)
        gates = [A.alloc(512 * 2) for _ in range(6)]
        merged = A.alloc(KC * 512 * 2, name="merged").rearrange("p (k n) -> p k n", k=KC)
        xnb.append(merged)
        kg32 = [A.alloc(512 * 4, F32) for _ in range(1)]
        vn = [A.alloc(256 * 2) for _ in range(2)]
        gv = [A.alloc(256 * 4, F32) for _ in range(2)]
        small = A.alloc(64 * 4, F32)
        mix_end = A.mark()
        A.reset(max(ffn_end, mix_end))
        print("SBUF bytes/partition: ffn_end", ffn_end, "mix_end", mix_end)

        cnt = {"pt": 0, "gate": 0, "kg": 0, "vn": 0, "sm": 0}

        S.dma("sp", vecs, vecs_d, writes=[("vecs",)])
        S.dma("sp", cTs, cT, writes=[("cTs",)])
        S.dma("sp", perm, perm_d, writes=[("perm",)])
        S.dma("sp", sgun, sgun_d.rearrange("l p n -> p l n"), writes=[("sgun",)])
        S.dma("sp", sgub, sgub_d.rearrange("l p c n -> p l c n"), writes=[("sgub",)])
        S.dma("pool", cosb, cos_d, writes=[("cos",)])
        S.dma("pool", sinb, sin_d, writes=[("sin",)])
        S.dma("pool", bands, band_d, writes=[("bands",)])
        S.dma("pool", poolbd, poolbd_d.rearrange("l p c n -> p l c n"), writes=[("poolbd",)])
        S.dma("pool", sguwT, sguwT_d.rearrange("l p g n -> p l g n"), writes=[("sguwT",)])
        S.op("dve", lambda e: e.memset(ones, 1.0), writes=[("ones",)])
        S.op("dve", lambda e: e.memset(bdones, 0.0), writes=[("bdones",)])
        S.op("dve", lambda e: e.memset(bdones[0:64, 0:64], 1.0), writes=[("bdones",)])
        S.op("dve", lambda e: e.memset(bdones[64:128, 64:128], 1.0), writes=[("bdones",)])

        def precast_ffn(l, f):
            src = w13_d[f][l].rearrange("(k p) n -> p k n", p=128)
            for g in range(11):
                S.dma("pool", w13s[l][f][g][:, :, 0:256], src[:, :, g * 256:(g + 1) * 256], writes=[("w13s", l, f, g)])
                S.dma("pool", w13s[l][f][g][:, :, 256:512], src[:, :, DFF + g * 256:DFF + (g + 1) * 256],
                      writes=[("w13s", l, f, g)])
            src2 = w2_d[f][l].rearrange("(k p) n -> p k n", p=128)
            for g in range(4):
                S.dma("pool", w2s[l][f][g], src2[:, :, g * 256:(g + 1) * 256], writes=[("w2s", l, f, g)])

        def precast_mix(l):
            src = w_in[l].rearrange("(k p) n -> p k n", p=128)
            W = wins[l]
            k0 = ("wins", l, 0)
            for kv in range(2):
                for dup in range(2):
                    for r in range(2):
                        d0 = kv * 128 + dup * 64 + r * 32
                        s0 = 512 + kv * 64 + r * 16
                        for a in range(2):
                            S.dma("pool", W[0][:, :, d0 + a * 16:d0 + a * 16 + 16], src[:, :, s0 + a * 32:s0 + a * 32 + 16], writes=[k0])
            S.dma("pool", W[1][:, :, 0:128], src[:, :, 640:768], writes=[("wins", l, 1)])
            S.dma("pool", W[1][:, :, 128:384], src[:, :, 768:1024], writes=[("wins", l, 1)])
            S.dma("pool", W[2], src[:, :, 1024:1536], writes=[("wins", l, 2)])
            for h in range(8):
                for r in range(2):
                    d0 = h * 64 + r * 32
                    s0 = h * 64 + r * 16
                    for a in range(2):
                        S.dma("pool", W[3][:, :, d0 + a * 16:d0 + a * 16 + 16], src[:, :, s0 + a * 32:s0 + a * 32 + 16], writes=[("wins", l, 3)])
            brp = w_brp[l].rearrange("(k p) n -> p k n", p=128)
            bra = w_bra[l].rearrange("(k p) n -> p k n", p=128)
            brs = w_brs[l].rearrange("(k p) n -> p k n", p=128)
            for m in range(8):
                key = ("wins", l, 4 + m)
                for gi in range(3):
                    S.dma("pool", W[4 + m][:, :, gi * 128:(gi + 1) * 128],
                          src[:, :, 1536 + gi * 1024 + m * 128: 1536 + gi * 1024 + (m + 1) * 128], writes=[key])
                S.dma("pool", W[4 + m][:, 0:2, 384:512], brp[:, :, m * 128:(m + 1) * 128], writes=[key])
                S.dma("pool", W[4 + m][:, 2:6, 384:512], bra[:, :, m * 128:(m + 1) * 128], writes=[key])
                S.dma("pool", W[4 + m][:, 6:8, 384:512], brs[:, :, m * 128:(m + 1) * 128], writes=[key])
            so = w_out[l].rearrange("(k p) n -> p k n", p=128)
            for og in range(2):
                S.dma("pool", wouts[l][og], so[:, :, og * 512:(og + 1) * 512], writes=[("wouts", l, og)])

        def adaln_piece(l, g):
            src = w_ada[l].rearrange("(k p) n -> p k n", p=128)
            wa, wk = wa_next()
            S.dma("pool", wa, src[:, :, g * 512:(g + 1) * 512], writes=[wk])
            pm, pmk = ps_rot()
            for j in range(4):
                S.mm(pm[:, j * 3:(j + 1) * 3], [(wa[:, kc, j * 128:(j + 1) * 128], scT[:, kc, :]) for kc in range(KC)],
                     reads=[wk, ("scT",)], writes=[pmk])
            bo = l * VL + V_BADA + g * 4
            S.op("dve", lambda e: e.tensor_tensor(out=modT[:, l, g * 4:(g + 1) * 4, :], in0=pm[:, 0:12].rearrange("p (j n) -> p j n", n=3),
                                                  in1=vecs[:, bo:bo + 4].unsqueeze(2).to_broadcast([128, 4, 3]), op=ALU.add),
                 reads=[pmk, ("vecs",)], writes=[("modT", l)])

        def adaln_finish(l, subs=(0, 1, 2)):
            for s in subs:
                no = l * VL + (V_N1, V_NM, V_N2)[s]
                S.op("dve", lambda e: e.scalar_tensor_tensor(
                    out=Amod[:, l, s], in0=modT[:, l, (3 * s + 1) * 8:(3 * s + 2) * 8, :], scalar=1.0,
                    in1=vecs[:, no:no + 8].unsqueeze(2).to_broadcast([128, 8, 3]), op0=ALU.add, op1=ALU.mult),
                    reads=[("modT", l), ("vecs",)], writes=[("Amod", l)])
                gs = 1.0 if s == 1 else 0.5
                S.op("dve", lambda e: e.tensor_scalar(out=Gmod[:, l, s], in0=modT[:, l, (3 * s + 2) * 8:(3 * s + 3) * 8, :],
                                                      scalar1=gs, scalar2=None, op0=ALU.mult),
                     reads=[("modT", l)], writes=[("Gmod", l)])

        bg = []
        aux = ["dve"]

        def run_bg(n=1):
            for _ in range(n):
                if bg:
                    bg.pop(0)()

        def hap(t, kc):
            st, t0, w = t
            return (hT if st == "lat" else hcT)[:, kc, t0:t0 + w]

        def hkey(t, kc):
            return ("h", t[0], t[1], kc)

        def rstd_from_ps(ps_ap, pskey, w, scale, npart=128):
            sr, srk = tmp("rr")
            S.op("act", lambda e: e.activation(out=sr[:, 0:w], in_=ps_ap, func=AF.Ln, bias=epsc[:, 0:1], scale=scale),
                 reads=[pskey, ("epsc",)], writes=[srk])
            S.op("act", lambda e: e.activation(out=sr[:, 0:w], in_=sr[:, 0:w], func=AF.Exp, scale=-0.5), reads=[srk], writes=[srk])
            return sr, srk

        def modulate(t, l, s, bcol, xoff, xb=0):
            st, t0, w = t
            xn = xnb[xb]
            col = 2 if st == "ctx" else bcol
            ps, pk = ps_rot()
            for kc in range(KC):
                S.op("act", lambda e: e.activation(out=xn[:, kc, xoff:xoff + w], in_=hap(t, kc), func=AF.Square),
                     reads=[hkey(t, kc)], writes=[("xn", xb, kc, xoff)])
            for kc in range(KC):
                S.mm(ps[:, 0:w], [(ones, xn[:, kc, xoff:xoff + w])], reads=[("xn", xb, kc, xoff), ("ones",)], writes=[pk],
                     start=(kc == 0), stop=(kc == KC - 1))
            rr, rk = rstd_from_ps(ps[:, 0:w], pk, w, 1.0 / D)
            for kc in range(KC):
                tt, tk = tmp("t32")
                S.op("dve", lambda e: e.scalar_tensor_tensor(out=tt[:, 0:w], in0=hap(t, kc), scalar=Amod[:, l, s, kc, col:col + 1],
                                                             in1=rr[:, 0:w], op0=ALU.mult, op1=ALU.mult),
                     reads=[hkey(t, kc), rk, ("Amod", l)], writes=[tk])
                S.op("act", lambda e: e.activation(out=xn[:, kc, xoff:xoff + w], in_=tt[:, 0:w], func=AF.Identity,
                                                   bias=modT[:, l, 3 * s * 8 + kc, col:col + 1], scale=1.0),
                     reads=[tk, ("modT", l)], writes=[("xn", xb, kc, xoff)])

        def xn_keys(xoff, xb=0):
            return [("xn", xb, kc, xoff) for kc in range(KC)]

        def ffn(l, f, bcol, tiles):
            s = 0 if f == 0 else 2
            sbs = [tiles[i:i + 2] for i in range(0, len(tiles), 2)]

            def offs_of(sb):
                offs = []; o = 0
                for t in sb:
                    offs.append(o); o += t[2]
                return offs

            def mod_sb(i):
                for t, xo in zip(sbs[i], offs_of(sbs[i])):
                    modulate(t, l, s, bcol, xo, xb=i % 2)

            mod_sb(0)
            for i, sb in enumerate(sbs):
                offs = offs_of(sb)
                xb = i % 2
                xn = xnb[xb]
                for g in range(11):
                    if early:
                        early.pop(0)()
                    elif f == 1 and late:
                        late.pop(0)()
                    wa, wk = wa_next()
                    S.dma("sp", wa, w13s[l][f][g], reads=[("w13s", l, f, g)], writes=[wk])
                    for j in range(2):
                        n = 2 * g + j
                        for t, xo in zip(sb, offs):
                            w = t[2]
                            pa, pak = ps_rot(); pb, pbk = ps_rot()
                            S.mm(pa[:, 0:w], [(wa[:, kc, j * 128:(j + 1) * 128], xn[:, kc, xo:xo + w]) for kc in range(KC)],
                                 reads=[wk] + xn_keys(xo, xb), writes=[pak])
                            S.mm(pb[:, 0:w], [(wa[:, kc, 256 + j * 128:256 + (j + 1) * 128], xn[:, kc, xo:xo + w]) for kc in range(KC)],
                                 reads=[wk] + xn_keys(xo, xb), writes=[pbk])
                            sa, sak = tmp("t32")
                            S.op("act", lambda e: e.activation(out=sa[:, 0:w], in_=pa[:, 0:w], func=AF.Silu), reads=[pak], writes=[sak])
                            S.op("dve", lambda e: e.tensor_tensor(out=u_ff[:, n, xo:xo + w], in0=sa[:, 0:w], in1=pb[:, 0:w], op=ALU.mult),
                                 reads=[sak, pbk], writes=[("u", n, xo)])
                for g in range(4):
                    if g == 2 and i + 1 < len(sbs):
                        mod_sb(i + 1)
                    wb = wB[g % 2]; wbk = ("wB", g % 2)
                    S.dma("sp", wb, w2s[l][f][g], reads=[("w2s", l, f, g)], writes=[wbk])
                    for j in range(2):
                        m = 2 * g + j
                        for t, xo in zip(sb, offs):
                            w = t[2]; col = 2 if t[0] == "ctx" else bcol
                            py, pyk = ps_rot()
                            S.mm(py[:, 0:w], [(wb[:, n, j * 128:(j + 1) * 128], u_ff[:, n, xo:xo + w]) for n in range(NFF)],
                                 reads=[wbk] + [("u", n, xo) for n in range(NFF)], writes=[pyk])
                            S.op("dve", lambda e: e.scalar_tensor_tensor(out=hap(t, m), in0=py[:, 0:w], scalar=Gmod[:, l, s, m, col:col + 1],
                                                                         in1=hap(t, m), op0=ALU.mult, op1=ALU.add),
                                 reads=[pyk, ("Gmod", l), hkey(t, m)], writes=[hkey(t, m)])

        def gelu_to(out_ap, ps_ap, pskey, outkey, npart, w):
            S.op("act", lambda e: e.activation(out=out_ap, in_=ps_ap, func=AF.Gelu_apprx_tanh), reads=[pskey], writes=[outkey])

        def qk_norm_rope(ps, pk, w, gain_col, out_ap, outkey, rope, tok0):
            sq, sk = tmp("sq")
            S.op("act", lambda e: e.activation(out=sq[:, 0:w], in_=ps[:, 0:w], func=AF.Square), reads=[pk], writes=[sk])
            p2, p2k = ps_rot()
            S.mm(p2[:, 0:w], [(bdones, sq[:, 0:w])], reads=[sk, ("bdones",)], writes=[p2k])
            rr, rk = rstd_from_ps(p2[:, 0:w], p2k, w, 1.0 / 64)
            if not rope:
                S.op("dve", lambda e: e.scalar_tensor_tensor(out=out_ap, in0=ps[:, 0:w], scalar=vecs[:, gain_col:gain_col + 1],
                                                             in1=rr[:, 0:w], op0=ALU.mult, op1=ALU.mult),
                     reads=[pk, rk, ("vecs",)], writes=[outkey])
                return
            i = 0
            kg = kg32[i]; kk = ("kg", i)
            S.op("dve", lambda e: e.scalar_tensor_tensor(out=kg[:, 0:w], in0=ps[:, 0:w], scalar=vecs[:, gain_col:gain_col + 1],
                                                         in1=rr[:, 0:w], op0=ALU.mult, op1=ALU.mult),
                 reads=[pk, rk, ("vecs",)], writes=[kk])
            w1, w1k = tmp("t32")
            S.op(aux[0], lambda e: e.tensor_tensor(out=w1[:, 0:w], in0=rr[:, 0:w], in1=sinb[:, tok0:tok0 + w], op=ALU.mult),
                 reads=[rk, ("sin",)], writes=[w1k])
            t1, t1k = tmp("t32")
            gsw = 208 + 2 * (gain_col // VL) + (1 if gain_col % VL == V_KN else 0)
            for qd in range(4):
                o = qd * 32; so = (qd ^ 1) * 32
                S.op("dve", lambda e: e.scalar_tensor_tensor(out=t1[o:o + 32, 0:w], in0=ps[so:so + 32, 0:w], scalar=vecs[o:o + 32, gsw:gsw + 1],
                                                             in1=w1[o:o + 32, 0:w], op0=ALU.mult, op1=ALU.mult),
                     reads=[pk, w1k, ("vecs",)], writes=[t1k])
            S.op(aux[0], lambda e: e.tensor_tensor(out=kg[:, 0:w], in0=kg[:, 0:w], in1=cosb[:, tok0:tok0 + w], op=ALU.mult),
                 reads=[kk, ("cos",)], writes=[kk])
            S.op(aux[0], lambda e: e.tensor_tensor(out=out_ap, in0=kg[:, 0:w], in1=t1[:, 0:w], op=ALU.add),
                 reads=[kk, t1k], writes=[outkey])

        def koff(t):
            return t[1] if t[0] == "lat" else L + t[1]

        def pass1_all(l, bcol, tiles):
            kps_of = {}

            xb_of = {}

            def p1_mod(t):
                xb_of[t] = 0 if len(xb_of) % 2 == 0 else 2
                modulate(t, l, 1, bcol, 0, xb=xb_of[t])

            def p1_proj(t):
                st, t0, w = t
                ko = koff(t)
                xb = xb_of[t]
                xn = xnb[xb]
                wa, wk = wa_next()
                S.dma("sp", wa[:, :, 0:256], wins[l][0][:, :, 0:256], reads=[("wins", l, 0)], writes=[wk])
                kps = []
                for j in range(2):
                    ps, pk = PS[6 + j], ("ps", 6 + j)
                    S.mm(ps[:, 0:w], [(wa[:, kc, j * 128:(j + 1) * 128], xn[:, kc, 0:w]) for kc in range(KC)],
                         reads=[wk] + xn_keys(0, xb), writes=[pk])
                    kps.append((ps, pk))
                kps_of[t] = kps
                wa, wk = wa_next()
                S.dma("sp", wa[:, :, 0:384], wins[l][1][:, :, 0:384], reads=[("wins", l, 1)], writes=[wk])
                for si in range(w // 128):
                    gs = (ko + si * 128) // 128
                    ps, pk = ps_rot()
                    S.mm(ps[:, 0:384], [(xn[:, kc, si * 128:(si + 1) * 128], wa[:, kc, 0:384]) for kc in range(KC)],
                         reads=[wk] + xn_keys(0, xb), writes=[pk])
                    S.op("act", lambda e: e.activation(out=Vaug[:, gs, :, 0:64], in_=ps[:, 0:128].rearrange("p (k n) -> p k n", k=2),
                                                       func=AF.Copy), reads=[pk], writes=[("V", gs)])
                    S.op("act", lambda e: e.activation(out=xpool[:, gs, :], in_=ps[:, 128:384], func=AF.Copy), reads=[pk], writes=[("xpool", gs)])

            def p1_chain(t):
                st, t0, w = t
                ko = koff(t)
                kps = kps_of[t]
                for j in range(2):
                    qk_norm_rope(kps[j][0], kps[j][1], w, l * VL + V_KN, kT[:, j, ko:ko + w], ("kT", j, ko), st != "ctx", t0)

            p1_mod(tiles[0]); p1_proj(tiles[0])
            for k in range(1, len(tiles)):
                p1_mod(tiles[k])
                p1_chain(tiles[k - 1])
                p1_proj(tiles[k])
            p1_chain(tiles[-1])

        def sgu_items(l, t):
            st, t0, w = t
            hold = {}

            def item_u():
                wa, wk = wa_next()
                hold["wa"] = (wa, wk)
                S.dma("sp", wa, wins[l][2], reads=[("wins", l, 2)], writes=[wk])
                for j in range(2):
                    ps, pk = ps_rot()
                    S.mm(ps[:, 0:w], [(wa[:, kc, j * 128:(j + 1) * 128], xn[:, kc, 0:w]) for kc in range(KC)],
                         reads=[wk] + xn_keys(0), writes=[pk])
                    gelu_to(uT[:, j, 0:w], ps[:, 0:w], pk, ("uT", j), 128, w)

            def item_v(si):
                wa, wk = hold["wa"]
                ps, pk = ps_rot()
                S.mm(ps[:, 0:256], [(xn[:, kc, si * 128:(si + 1) * 128], wa[:, kc, 256:512]) for kc in range(KC)],
                     reads=[wk] + xn_keys(0), writes=[pk])
                i = cnt["vn"] % 2; cnt["vn"] += 1
                g_ = gv[i]; gk = ("gv", i); v_ = vn[i]; vk = ("vn", i)
                gelu_to(g_, ps[:, 0:256], pk, gk, 128, 256)
                sq, sk = tmp("sq")
                sm = small[:, (cnt["sm"] % 8) * 2:(cnt["sm"] % 8) * 2 + 1]; smk = ("sm", cnt["sm"] % 8); cnt["sm"] += 1
                S.op("act", lambda e: e.activation(out=sq[:, 0:256], in_=g_, func=AF.Square, accum_out=sm),
                     reads=[gk], writes=[sk, smk])
                S.op("act", lambda e: e.activation(out=sm, in_=sm, func=AF.Sqrt, bias=epsc[:, 0:1], scale=1.0 / 256),
                     reads=[smk, ("epsc",)], writes=[smk])
                S.op("dve", lambda e: e.reciprocal(out=sm, in_=sm), reads=[smk], writes=[smk])
                S.op("dve", lambda e: e.scalar_tensor_tensor(out=v_, in0=g_, scalar=sm, in1=sgun[:, l, :], op0=ALU.mult, op1=ALU.mult),
                     reads=[gk, smk, ("sgun",)], writes=[vk])
                p2, p2k = ps_rot()
                for gi in range(4):
                    S.mm(p2[(gi % 2) * 64:(gi % 2) * 64 + 64, (gi // 2) * 128:(gi // 2) * 128 + 128],
                         [(v_[:, gi * 64:(gi + 1) * 64], sguwT[:, l, gi, :])], reads=[vk, ("sguwT",)], writes=[p2k])
                for c in range(2):
                    tt, tk = tmp("t32")
                    S.op("dve", lambda e: e.tensor_tensor(out=tt[:, 0:128], in0=p2[:, c * 128:(c + 1) * 128], in1=sgub[:, l, c, :], op=ALU.add),
                         reads=[p2k, ("sgub",)], writes=[tk])
                    us = uT[:, c, si * 128:(si + 1) * 128]
                    S.op("dve", lambda e: e.tensor_tensor(out=us, in0=tt[:, 0:128], in1=us, op=ALU.mult),
                         reads=[tk, ("uT", c)], writes=[("uT", c)])

            return [item_u] + [(lambda si=si: item_v(si)) for si in range(w // 128)]

        def pass2(l, bcol, t):
            st, t0, w = t
            ko = koff(t)
            isctx = st == "ctx"
            col = 2 if isctx else bcol
            modulate(t, l, 1, bcol, 0)
            wa, wk = wa_next()
            S.dma("sp", wa, wins[l][3], reads=[("wins", l, 3)], writes=[wk])
            for j in range(4):
                ps, pk = ps_rot()
                S.mm(ps[:, 0:w], [(wa[:, kc, j * 128:(j + 1) * 128], xn[:, kc, 0:w]) for kc in range(KC)],
                     reads=[wk] + xn_keys(0), writes=[pk])
                qk_norm_rope(ps, pk, w, l * VL + V_QN, qT[:, j, 0:w], ("qT", j), not isctx, t0)
            side = sgu_items(l, t)
            nblk = 2 if isctx else 16
            sbase = 16 if isctx else 0

            def pool_item(c):
                pp, ppk = ps_rot()
                for tb in range(w // 128):
                    i = t0 // 128 + tb
                    for half in range(2):
                        g = 2 * c + half
                        srcs = []
                        if i > 0:
                            srcs.append((i - 1, 3))
                        srcs.append((i, 0 if i == 0 else (2 if i == nblk - 1 else 1)))
                        if i < nblk - 1:
                            srcs.append((i + 1, 4))
                        S.mm(pp[half * 64:half * 64 + 64, tb * 128:(tb + 1) * 128],
                             [(xpool[:, sbase + si, c * 128 + half * 64:c * 128 + half * 64 + 64], bands[:, g * 5 + kind, :]) for si, kind in srcs],
                             reads=[("xpool", sbase + si) for si, _ in srcs] + [("bands",)], writes=[ppk])
                S.op("act", lambda e: e.activation(out=pooled[:, 0:w], in_=pp[:, 0:w], func=AF.Copy), reads=[ppk], writes=[("pooled",)])
                po_, pok = ps_rot()
                S.mm(po_[:, 0:w], [(poolbd[:, l, c, :], pooled[:, 0:w])], reads=[("pooled",), ("poolbd",)], writes=[pok])
                pc = l * VL + V_PSC + c
                S.op("act", lambda e: e.activation(out=poolout[:, c, 0:w], in_=po_[:, 0:w], func=AF.Copy, scale=vecs[:, pc:pc + 1]),
                     reads=[pok, ("vecs",)], writes=[("poolout", c)])

            kslices = list(range(16, 18)) if isctx else list(range(NKS))
            steps = [(h, gs) for h in range(8) for gs in kslices]
            pend = []

            def kkeys(kv, gs):
                o = gs * 128
                for tt in _tiles():
                    k0 = koff(tt)
                    if k0 <= o < k0 + tt[2]:
                        return ("kT", kv, k0)
                raise AssertionError

            def emit_S(h, gs):
                ps, pk = ps_rot()
                pr = slice((h % 2) * 64, (h % 2) * 64 + 64)
                S.mm(ps[:, 0:w], [(kT[pr, h // 4, gs * 128:(gs + 1) * 128], qT[pr, h // 2, 0:w])],
                     reads=[kkeys(h // 4, gs), ("qT", h // 2)], writes=[pk])
                i = cnt["pt"] % 4; cnt["pt"] += 1
                pt = pTb[i]; ptk = ("pT", i)
                S.op("act", lambda e: e.activation(out=pt[:, 0:w], in_=ps[:, 0:w], func=AF.Exp, scale=0.125), reads=[pk], writes=[ptk])
                return pt, ptk

            def emit_PV(h, gs, pt, ptk):
                pv = PS[6 + (h % 2)]; pvk = ("ps", 6 + (h % 2))
                S.mm(pv[:, 0:w], [(Vaug[:, gs, h // 4, :], pt[:, 0:w])], reads=[("V", gs), ("Vones",), ptk], writes=[pvk],
                     start=(gs == kslices[0]), stop=(gs == kslices[-1]))
                if gs == kslices[-1]:
                    rd, rdk = tmp("rr")
                    S.op("dve", lambda e: e.reciprocal(out=rd[64:128, 0:w], in_=pv[64:128, 0:w]), reads=[pvk], writes=[rdk])
                    po = (h % 2) * 64
                    S.op("dve", lambda e: e.tensor_tensor(out=attnT[po:po + 64, h // 2, 0:w], in0=pv[0:64, 0:w], in1=rd[64:128, 0:w], op=ALU.mult),
                         reads=[pvk, rdk], writes=[("attnT", h // 2, h % 2)])

            nblk = 2 if isctx else 16
            sbase = 16 if isctx else 0
            for c in range(2):
                side.append(lambda c=c: pool_item(c))
            LA = 2
            for i, (h, gs) in enumerate(steps):
                pend.append((h, gs) + emit_S(h, gs))
                if len(pend) > LA:
                    emit_PV(*pend.pop(0))
                if gs == kslices[-1] and side and h >= 1:
                    side.pop(0)()
            while pend:
                emit_PV(*pend.pop(0))
            while side:
                side.pop(0)()
            utk = [("uT", 0), ("uT", 1)]
            for m in range(8):
                if t[1] >= 1280 or isctx:
                    run_bg()
                wa, wk = wa_next()
                S.dma("sp", wa, wins[l][4 + m], reads=[("wins", l, 4 + m)], writes=[wk])
                gts = []
                for gi in range(3):
                    ps, pk = ps_rot()
                    S.mm(ps[:, 0:w], [(wa[:, kc, gi * 128:(gi + 1) * 128], xn[:, kc, 0:w]) for kc in range(KC)],
                         reads=[wk] + xn_keys(0), writes=[pk])
                    i = cnt["gate"] % 6; cnt["gate"] += 1
                    S.op("act", lambda e: e.activation(out=gates[i][:, 0:w], in_=ps[:, 0:w], func=AF.Sigmoid), reads=[pk], writes=[("gate", i)])
                    gts.append((gates[i], ("gate", i)))
                brs = [([(wa[:, kc, 384:512], poolout[:, kc, 0:w]) for kc in range(2)], [("poolout", 0), ("poolout", 1)]),
                       ([(wa[:, 2 + kc, 384:512], attnT[:, kc, 0:w]) for kc in range(4)], [("attnT", kc, hh) for kc in range(4) for hh in range(2)]),
                       ([(wa[:, 6 + kc, 384:512], uT[:, kc, 0:w]) for kc in range(2)], utk)]
                prods = []
                for gi in range(3):
                    ps, pk = ps_rot()
                    S.mm(ps[:, 0:w], brs[gi][0], reads=[wk] + brs[gi][1], writes=[pk])
                    g_, gk = gts[gi]
                    pr_, prk = tmp("t32")
                    S.op("dve", lambda e: e.tensor_tensor(out=pr_[:, 0:w], in0=g_[:, 0:w], in1=ps[:, 0:w], op=ALU.mult),
                         reads=[gk, pk], writes=[prk])
                    prods.append((pr_, prk))
                    if gi == 1:
                        S.op(aux[0], lambda e: e.tensor_tensor(out=prods[0][0][:, 0:w], in0=prods[0][0][:, 0:w], in1=prods[1][0][:, 0:w], op=ALU.add),
                             reads=[prods[0][1], prods[1][1]], writes=[prods[0][1]])
                S.op(aux[0], lambda e: e.tensor_tensor(out=merged[:, m, 0:w], in0=prods[0][0][:, 0:w], in1=prods[2][0][:, 0:w], op=ALU.add),
                     reads=[prods[0][1], prods[2][1]], writes=[("merged", m)])
            for og in range(2):
                wa, wk = wa_next()
                S.dma("sp", wa, wouts[l][og], reads=[("wouts", l, og)], writes=[wk])
                for j in range(4):
                    m = og * 4 + j
                    ps, pk = ps_rot()
                    S.mm(ps[:, 0:w], [(wa[:, kc, j * 128:(j + 1) * 128], merged[:, kc, 0:w]) for kc in range(KC)],
                         reads=[wk] + [("merged", kc) for kc in range(KC)], writes=[pk])
                    S.op("dve", lambda e: e.scalar_tensor_tensor(out=hap(t, m), in0=ps[:, 0:w], scalar=Gmod[:, l, 1, m, col:col + 1],
                                                                 in1=hap(t, m), op0=ALU.mult, op1=ALU.add),
                         reads=[pk, ("Gmod", l), hkey(t, m)], writes=[hkey(t, m)])

        epsc = A.alloc(4 * 4, F32)
        S.op("dve", lambda e: e.memset(epsc, EPS), writes=[("epsc",)])
        S.op("act", lambda e: e.activation(out=scT, in_=cTs, func=AF.Silu), reads=[("cTs",)], writes=[("scT",)])
        l0 = layers[0]
        late = []
        early = []
        for g in range(6):
            adaln_piece(l0, g)
        adaln_finish(l0, (0,))
        precast_ffn(l0, 0)
        for g in range(6, 18):
            early.append(lambda g=g: adaln_piece(l0, g))
        early.append(lambda: adaln_finish(l0, (1, 2)))
        early.append(lambda: precast_mix(l0))
        early.append(lambda: precast_ffn(l0, 1))
        for l in layers[1:]:
            early.append(lambda l=l: precast_ffn(l, 0))
            early.append(lambda l=l: precast_mix(l))
            early.append(lambda l=l: precast_ffn(l, 1))
            for g in range(18):
                late.append(lambda l=l, g=g: adaln_piece(l, g))
            late.append(lambda l=l: adaln_finish(l))

        for b in range(nb):
            for kc in range(KC):
                ks = [hkey(t, kc) for t in _tiles(False)]
                S.dma("sp", hT[:, kc, :], xT[b][:, kc, :], writes=ks)
            S.dma("sp", hcT, ctxT[b], writes=[hkey(_tiles()[-1], kc) for kc in range(KC)])
            for l in layers:
                last = (l == 1)
                aux[0] = "dve" if (b == 0 and l == layers[0]) else "pool"
                if l != layers[0]:
                    run_bg(len(bg))
                    while late:
                        late.pop(0)()
                if "ffn1" in stages:
                    ffn(l, 0, b, _tiles(True))
                while early:
                    early.pop(0)()
                if "mix" in stages:
                    S.barrier()
                    S.op("dve", lambda e: e.memset(Vaug[:, :, :, 64:128], 1.0), writes=[("Vones",)])
                    pass1_all(l, b, _tiles(True))
                    S.barrier()
                    for t in (_tiles(not last) if p2_tiles is None else [_tiles()[i] for i in p2_tiles]):
                        pass2(l, b, t)
                    S.barrier()
                if "ffn2" in stages:
                    ffn(l, 1, b, _tiles(not last))
            if final:
                for t in _tiles(False):
                    st, t0, w = t
                    ps, pk = ps_rot()
                    for kc in range(KC):
                        sq, sk = tmp("sq")
                        S.op("act", lambda e: e.activation(out=sq[:, 0:w], in_=hap(t, kc), func=AF.Square), reads=[hkey(t, kc)], writes=[sk])
                        S.mm(ps[:, 0:w], [(ones, sq[:, 0:w])], reads=[sk, ("ones",)], writes=[pk], start=(kc == 0), stop=(kc == KC - 1))
                    rr, rk = rstd_from_ps(ps[:, 0:w], pk, w, 1.0 / D)
                    for kc in range(KC):
                        S.op("dve", lambda e: e.scalar_tensor_tensor(out=hap(t, kc), in0=hap(t, kc), scalar=vecs[:, V_FN + kc:V_FN + kc + 1],
                                                                     in1=rr[:, 0:w], op0=ALU.mult, op1=ALU.mult),
                             reads=[hkey(t, kc), rk, ("vecs",)], writes=[hkey(t, kc)])
            for t in _tiles(False):
                st, t0, w = t
                S.dma("sp", outT[b][:, :, t0:t0 + w], hT[:, :, t0:t0 + w], reads=[hkey(t, kc) for kc in range(KC)])
            S.dma("sp", hcout[b], hcT, reads=[hkey(_tiles()[-1], kc) for kc in range(KC)])
        S.drain("sp")
    return nc


def _fm(v):
    return np.ascontiguousarray(v.reshape(-1, 128).T)


def _qk_perm():
    old = np.zeros(64, np.int64)
    for r in range(2):
        for a in range(2):
            for i in range(16):
                old[r * 32 + a * 16 + i] = a * 32 + r * 16 + i
    return old


def _rope_tables():
    half = 32
    inv = (10000.0 ** (-np.arange(0, half, 2, dtype=np.float32) / half)).astype(np.float32)
    t = np.arange(L)
    row = (t // 64).astype(np.float32); col = (t % 64).astype(np.float32)
    cos = np.zeros((128, L), np.float32); sin = np.zeros((128, L), np.float32)
    for p in range(128):
        pp = p % 64
        r = pp // 32; a = (pp % 32) // 16; i = pp % 16
        ang = ((row if a == 0 else col) * inv[i]).astype(np.float32)
        cos[p] = np.cos(ang)
        sin[p] = -np.sin(ang) if r == 0 else np.sin(ang)
    return cos, sin, np.zeros((128, 128), np.float32)


def _bands():
    out = np.zeros((128, 20, 128), np.float32)
    n = 384
    for g, w in enumerate((2, 4, 8, 16)):
        hw = w // 2
        M = np.zeros((n, n), np.float64)
        for t in range(n):
            lo = max(t - hw, 0); hi = min(t + hw, n)
            M[lo:hi, t] = 1.0 / (hi - lo)
            M[t, t] -= 1.0
        out[:, g * 5 + 0] = M[0:128, 0:128]
        out[:, g * 5 + 1] = M[128:256, 128:256]
        out[:, g * 5 + 2] = M[256:384, 256:384]
        out[:, g * 5 + 3] = M[0:128, 128:256]
        out[:, g * 5 + 4] = M[128:256, 0:128]
    return out


def _host_prep(inp, core, nb=NB):
    f = lambda a: np.ascontiguousarray(np.asarray(a, dtype=np.float32))
    bs = [core * nb + i for i in range(nb)]
    m = {}
    m["xT"] = np.stack([f(inp["x"][b].T.reshape(KC, 128, L).transpose(1, 0, 2)) for b in bs])
    m["ctxT"] = np.stack([f(inp["ctx"][b].T.reshape(KC, 128, C).transpose(1, 0, 2)) for b in bs])
    cols = [inp["c"][b] for b in bs] + [inp["c_ctx"]]
    while len(cols) < 3:
        cols.insert(-1, cols[0])
    m["cT"] = f(np.stack([_fm(np.asarray(v)) for v in cols], axis=2))
    return m


def _shared_prep(inp):
    f = lambda a: np.ascontiguousarray(np.asarray(a, dtype=np.float32))
    vecs = np.zeros((128, NV), np.float32)
    for l in range(2):
        o = l * VL
        vecs[:, o + V_N1:o + V_N1 + 8] = _fm(inp["norm_ffn1"][l])
        vecs[:, o + V_NM:o + V_NM + 8] = _fm(inp["norm_mix"][l])
        vecs[:, o + V_N2:o + V_N2 + 8] = _fm(inp["norm_ffn2"][l])
        vecs[:, o + V_BADA:o + V_BADA + 72] = _fm(inp["b_ada"][l])
        vecs[:, o + V_PSC:o + V_PSC + 2] = _fm(inp["pool_scale"][l])
        vecs[:, o + V_QN] = np.tile(inp["q_norm"][l][_qk_perm()], 2)
        vecs[:, o + V_KN] = np.tile(inp["k_norm"][l][_qk_perm()], 2)
        sw = np.arange(128) ^ 32
        vecs[:, 208 + 2 * l] = vecs[sw, o + V_QN]
        vecs[:, 209 + 2 * l] = vecs[sw, o + V_KN]
    vecs[:, V_FN:V_FN + 8] = _fm(inp["final_norm"])
    m = {"vecs": vecs}
    for k in ("w_ada", "ffn1_w13", "ffn2_w13", "ffn1_w2", "ffn2_w2", "w_in", "w_br_pool", "w_br_attn", "w_br_sgu", "w_out"):
        m[k] = f(inp[k])
    pbd = np.zeros((2, 128, 2, 128), np.float32)
    for l in range(2):
        for g in range(4):
            c, hh = g // 2, g % 2
            pbd[l, hh * 64:(hh + 1) * 64, c, hh * 64:(hh + 1) * 64] = inp["pool_w"][l, g]
    m["poolbd"] = pbd
    m["sguwT"] = f(np.transpose(inp["sgu_w"], (0, 3, 1, 2)))
    m["sgun"] = f(np.broadcast_to(inp["sgu_norm"][:, None, :], (2, 128, 256)))
    sb = np.zeros((2, 128, 2, 128), np.float32)
    for l in range(2):
        for g in range(4):
            c, hh = g // 2, g % 2
            sb[l, hh * 64:(hh + 1) * 64, c, :] = inp["sgu_b"][l, g][None, :]
    m["sgub"] = sb
    cos, sin, perm = _rope_tables()
    m["ropecos"] = cos; m["ropesin"] = sin; m["ropeperm"] = perm
    m["bands"] = _bands()
    return m


_NC_CACHE = {}


def kernel(**inputs):
    inp = {k: np.asarray(v) for k, v in inputs.items()}
    if "full" not in _NC_CACHE:
        _NC_CACHE["full"] = build()
    nc = _NC_CACHE["full"]
    shared = _shared_prep(inp)
    in_maps = []
    for core in range(NCORES):
        m = dict(shared)
        m.update(_host_prep(inp, core))
        in_maps.append(m)
    res = run_bass_kernel_spmd(nc, in_maps, core_ids=list(range(NCORES)))
    out = np.empty((NCORES * NB, L, D), np.float32)
    for core in range(NCORES):
        o = res.results[core]["outT"]
        for i in range(NB):
            out[core * NB + i] = o[i].transpose(1, 0, 2).reshape(D, L).T
    return out
```
